# Optimizing a Trainium2 kernel written in Bass

```python
import math
import jax, jax.numpy as jnp
from jax import lax
import numpy as np


D_MODEL = 1024
BATCH = 8
SEQ = 4096
DEPTH = 2

GRID_W = 64
CTX_LEN = 256
N_MIXERS = 4
GROUP_WIDTH = D_MODEL // N_MIXERS
MIX_WIDTH = N_MIXERS * GROUP_WIDTH
Q_BLOCK = 128
CHUNK = 128
ROPE_THETA = 10000.0
NORM_EPS = 1e-6
FFN_RES = 0.5
D_FF = ((8 * D_MODEL // 3 + 127) // 128) * 128
N_MOD = 9

A_HEADS = 4
A_V_DIM = GROUP_WIDTH // A_HEADS
A_QK_DIM = A_V_DIM // 2
B_HEADS = 4
B_KV_HEADS = 2
B_HEAD_DIM = GROUP_WIDTH // B_HEADS
C_GROUPS = 4
C_GROUP_DIM = GROUP_WIDTH // C_GROUPS
D_HEADS = 4
D_V_DIM = GROUP_WIDTH // D_HEADS
D_NOPE_DIM = D_V_DIM
D_ROPE_DIM = D_V_DIM // 2
D_Q_RANK = D_MODEL // 4
D_KV_RANK = D_MODEL // 8

IN_SIZES = (A_HEADS * 2 * A_QK_DIM, A_HEADS * 2 * A_QK_DIM, A_HEADS * A_V_DIM,
            B_HEADS * B_HEAD_DIM, B_KV_HEADS * B_HEAD_DIM, B_KV_HEADS * B_HEAD_DIM,
            GROUP_WIDTH, GROUP_WIDTH,
            D_Q_RANK, D_KV_RANK, D_ROPE_DIM)
IN_WIDTH = (2 * A_HEADS * 2 * A_QK_DIM + A_HEADS * A_V_DIM + B_HEADS * B_HEAD_DIM
            + 2 * B_KV_HEADS * B_HEAD_DIM + 2 * GROUP_WIDTH + D_Q_RANK + D_KV_RANK + D_ROPE_DIM)

kernel_name = 'hybrid_parallel_group_dit_trunk'


def rms_norm(x, g):
    xf = x.astype(jnp.float32)
    y = xf * lax.rsqrt(jnp.mean(xf * xf, axis=-1, keepdims=True) + NORM_EPS)
    return (y * g.astype(jnp.float32)).astype(x.dtype)


def layer_norm(x, g, b):
    xf = x.astype(jnp.float32)
    mu = jnp.mean(xf, axis=-1, keepdims=True)
    xc = xf - mu
    y = xc * lax.rsqrt(jnp.mean(xc * xc, axis=-1, keepdims=True) + NORM_EPS)
    return (y * g.astype(jnp.float32) + b.astype(jnp.float32)).astype(x.dtype)


def split_cols(p, sizes):
    idx = [int(i) for i in np.cumsum(sizes)[:-1]]
    return jnp.split(p, idx, axis=-1)


def swiglu(h, w1, w2):
    g, u = jnp.split(h @ w1, 2, axis=-1)
    return (jax.nn.silu(g) * u) @ w2


def lambda_init(layer_idx):
    return 0.8 - 0.6 * math.exp(-0.3 * layer_idx)


def axial_rope_tables(n_rows, rot_dim):
    rows = jnp.repeat(jnp.arange(n_rows, dtype=jnp.float32), GRID_W)
    cols = jnp.tile(jnp.arange(GRID_W, dtype=jnp.float32), n_rows)
    axis_dim = rot_dim // 2
    inv_freq = ROPE_THETA ** (-jnp.arange(0, axis_dim, 2, dtype=jnp.float32) / axis_dim)
    ang_r = rows[:, None] * inv_freq[None, :]
    ang_c = cols[:, None] * inv_freq[None, :]
    return (jnp.cos(ang_r), jnp.sin(ang_r), jnp.cos(ang_c), jnp.sin(ang_c))


def apply_rope_axis(x, cos, sin):
    half = x.shape[-1] // 2
    shp = (1, x.shape[1]) + (1,) * (x.ndim - 3) + (half,)
    cos = cos.reshape(shp)
    sin = sin.reshape(shp)
    xf = x.astype(jnp.float32)
    x1, x2 = xf[..., :half], xf[..., half:]
    return jnp.concatenate([x1 * cos - x2 * sin, x2 * cos + x1 * sin], axis=-1).astype(x.dtype)


def apply_rope_2d(x, tabs):
    cr, sr, cc, sc = tabs
    a = x.shape[-1] // 2
    return jnp.concatenate([apply_rope_axis(x[..., :a], cr, sr),
                            apply_rope_axis(x[..., a:], cc, sc)], axis=-1)


def sweep_query_blocks(fn, q):
    b, s = q.shape[:2]
    nb = s // Q_BLOCK
    qb = jnp.moveaxis(q.reshape((b, nb, Q_BLOCK) + q.shape[2:]), 1, 0)
    out = jnp.moveaxis(lax.map(fn, qb), 0, 1)
    return out.reshape((b, s) + out.shape[3:])


def diff_core(q, k, v, lam, scale):
    s = jnp.einsum('bqhmd,bkhmd->bhmqk', q, k).astype(jnp.float32) * scale
    p = jax.nn.softmax(s, axis=-1)
    w = p[:, :, 0] - lam * p[:, :, 1]
    return jnp.einsum('bhqk,bkhd->bqhd', w.astype(v.dtype), v)


def gqa_core(q, k, v, scale):
    s = jnp.einsum('bqhgd,bkhd->bhgqk', q, k).astype(jnp.float32) * scale
    p = jax.nn.softmax(s, axis=-1).astype(v.dtype)
    return jnp.einsum('bhgqk,bkhd->bqhgd', p, v)


def diff_attention_group(pc, pl, lam_vecs, g_subln, lam_init, rope, need_ctx):
    def heads(q, k, v):
        b, s = q.shape[:2]
        return (q.reshape(b, s, A_HEADS, 2, A_QK_DIM), k.reshape(b, s, A_HEADS, 2, A_QK_DIM),
                v.reshape(b, s, A_HEADS, A_V_DIM))
    qc, kc, vc = heads(*pc)
    ql, kl, vl = heads(*pl)
    ql = apply_rope_2d(ql, rope)
    kl = apply_rope_2d(kl, rope)
    lv = lam_vecs.astype(jnp.float32)
    lam = jnp.exp(jnp.sum(lv[0] * lv[1])) - jnp.exp(jnp.sum(lv[2] * lv[3])) + lam_init
    scale = A_QK_DIM ** -0.5
    k_all = jnp.concatenate([kc, kl], axis=1)
    v_all = jnp.concatenate([vc, vl], axis=1)
    o_lat = sweep_query_blocks(lambda qb: diff_core(qb, k_all, v_all, lam, scale), ql)

    def finish(o):
        b, s = o.shape[:2]
        return (rms_norm(o, g_subln) * (1.0 - lam_init)).reshape(b, s, A_HEADS * A_V_DIM)
    o_ctx = finish(diff_core(qc, kc, vc, lam, scale)) if need_ctx else None
    return o_ctx, finish(o_lat)


def gqa_group(pc, pl, g_qn, g_kn, rope, need_ctx):
    def heads(q, k, v):
        b, s = q.shape[:2]
        q = rms_norm(q.reshape(b, s, B_HEADS, B_HEAD_DIM), g_qn)
        k = rms_norm(k.reshape(b, s, B_KV_HEADS, B_HEAD_DIM), g_kn)
        return q, k, v.reshape(b, s, B_KV_HEADS, B_HEAD_DIM)
    qc, kc, vc = heads(*pc)
    ql, kl, vl = heads(*pl)
    ql = apply_rope_2d(ql, rope)
    kl = apply_rope_2d(kl, rope)
    n_rep = B_HEADS // B_KV_HEADS

    def group(q):
        return q.reshape(q.shape[:2] + (B_KV_HEADS, n_rep, B_HEAD_DIM))

    def flat(o):
        return o.reshape(o.shape[:2] + (B_HEADS * B_HEAD_DIM,))
    scale = B_HEAD_DIM ** -0.5
    k_all = jnp.concatenate([kc, kl], axis=1)
    v_all = jnp.concatenate([vc, vl], axis=1)
    o_lat = sweep_query_blocks(lambda qb: gqa_core(qb, k_all, v_all, scale), group(ql))
    o_ctx = flat(gqa_core(group(qc), kc, vc, scale)) if need_ctx else None
    return o_ctx, flat(o_lat)


def chunk_gmlp(u, v, w_sp, b_sp, ln_g, ln_b):
    u = jax.nn.gelu(u)
    v = layer_norm(jax.nn.gelu(v), ln_g, ln_b)
    b, s, _ = v.shape
    vr = v.reshape(b, s // CHUNK, CHUNK, C_GROUPS, C_GROUP_DIM)
    mixed = jnp.einsum('gpq,bnqgc->bnpgc', w_sp, vr) + b_sp.T[None, None, :, :, None]
    return u * mixed.reshape(b, s, C_GROUPS * C_GROUP_DIM)


def mla_group(pc, pl, g_qa, w_uq, g_kva, w_ukv, rope, need_ctx):
    def expand(cq, ckv, kr):
        b, s = cq.shape[:2]
        q = (rms_norm(cq, g_qa) @ w_uq).reshape(b, s, D_HEADS, D_NOPE_DIM + D_ROPE_DIM)
        kv = (rms_norm(ckv, g_kva) @ w_ukv).reshape(b, s, D_HEADS, D_NOPE_DIM + D_V_DIM)
        return (q[..., :D_NOPE_DIM], q[..., D_NOPE_DIM:], kv[..., :D_NOPE_DIM],
                kv[..., D_NOPE_DIM:], kr[:, :, None, :])

    def assemble(qn, qr, kn, kr):
        q = jnp.concatenate([qn, qr], axis=-1)[:, :, :, None, :]
        k = jnp.concatenate([kn, jnp.broadcast_to(kr, kn.shape[:3] + (D_ROPE_DIM,))], axis=-1)
        return q, k
    qn_c, qr_c, kn_c, v_c, kr_c = expand(*pc)
    qn_l, qr_l, kn_l, v_l, kr_l = expand(*pl)
    qr_l = apply_rope_2d(qr_l, rope)
    kr_l = apply_rope_2d(kr_l, rope)
    q_c, k_c = assemble(qn_c, qr_c, kn_c, kr_c)
    q_l, k_l = assemble(qn_l, qr_l, kn_l, kr_l)
    scale = (D_NOPE_DIM + D_ROPE_DIM) ** -0.5

    def flat(o):
        return o.reshape(o.shape[:2] + (D_HEADS * D_V_DIM,))
    k_all = jnp.concatenate([k_c, k_l], axis=1)
    v_all = jnp.concatenate([v_c, v_l], axis=1)
    o_lat = sweep_query_blocks(lambda qb: gqa_core(qb, k_all, v_all, scale), q_l)
    o_ctx = flat(gqa_core(q_c, k_c, v_c, scale)) if need_ctx else None
    return o_ctx, flat(o_lat)


def trunk_layer(x_ctx, x_lat, mod_ctx, mod_lat, params, ropes, lam_init, need_ctx):
    (g_pre, g_post, w_ffn1_in, w_ffn1_out, w_ffn2_in, w_ffn2_out, w_in, w_out,
     lam_vecs, g_subln, g_qnorm, g_knorm, w_spatial, b_spatial, ln_g, ln_b,
     g_q_a, w_uq, g_kv_a, w_ukv) = params
    rope_a, rope_b, rope_d = ropes

    def mod(m, i):
        return m[:, i, None, :]

    def pre(xs, m, j):
        return rms_norm(xs, g_pre[j]) * (1.0 + mod(m, 3 * j + 1)) + mod(m, 3 * j)

    def post(xs, y, m, j, w):
        return xs + w * mod(m, 3 * j + 2) * rms_norm(y, g_post[j])

    x_ctx = post(x_ctx, swiglu(pre(x_ctx, mod_ctx, 0), w_ffn1_in, w_ffn1_out), mod_ctx, 0, FFN_RES)
    x_lat = post(x_lat, swiglu(pre(x_lat, mod_lat, 0), w_ffn1_in, w_ffn1_out), mod_lat, 0, FFN_RES)

    pc = split_cols(pre(x_ctx, mod_ctx, 1) @ w_in, IN_SIZES)
    pl = split_cols(pre(x_lat, mod_lat, 1) @ w_in, IN_SIZES)
    a_c, a_l = diff_attention_group(pc[0:3], pl[0:3], lam_vecs, g_subln, lam_init, rope_a, need_ctx)
    b_c, b_l = gqa_group(pc[3:6], pl[3:6], g_qnorm, g_knorm, rope_b, need_ctx)
    c_l = chunk_gmlp(pl[6], pl[7], w_spatial, b_spatial, ln_g, ln_b)
    d_c, d_l = mla_group(pc[8:11], pl[8:11], g_q_a, w_uq, g_kv_a, w_ukv, rope_d, need_ctx)
    y_lat = jnp.concatenate([a_l, b_l, c_l, d_l], axis=-1) @ w_out
    x_lat = post(x_lat, y_lat, mod_lat, 1, 1.0)

    x_lat = post(x_lat, swiglu(pre(x_lat, mod_lat, 2), w_ffn2_in, w_ffn2_out), mod_lat, 2, FFN_RES)

    if need_ctx:
        c_c = chunk_gmlp(pc[6], pc[7], w_spatial, b_spatial, ln_g, ln_b)
        y_ctx = jnp.concatenate([a_c, b_c, c_c, d_c], axis=-1) @ w_out
        x_ctx = post(x_ctx, y_ctx, mod_ctx, 1, 1.0)
        x_ctx = post(x_ctx, swiglu(pre(x_ctx, mod_ctx, 2), w_ffn2_in, w_ffn2_out), mod_ctx, 2, FFN_RES)
    return x_ctx, x_lat


def setup_inputs(seed: int = 0) -> dict:
    key = jax.random.key(seed)
    ks = jax.random.split(key, 26)
    f32 = jnp.float32
    L = DEPTH

    def nrm(k, shape, s):
        return jax.random.normal(k, shape, f32) * s

    def gain(k, shape):
        return 1.0 + 0.05 * jax.random.normal(k, shape, f32)
    return {
        'x': nrm(ks[0], (BATCH, SEQ, D_MODEL), 1.0),
        'c': nrm(ks[1], (BATCH, D_MODEL), 1.0),
        'ctx': nrm(ks[2], (BATCH, CTX_LEN, D_MODEL), 1.0),
        'c_ctx': nrm(ks[3], (D_MODEL,), 1.0),
        'w_ada': nrm(ks[4], (L, D_MODEL, N_MOD * D_MODEL), D_MODEL ** -0.5),
        'b_ada': nrm(ks[5], (L, N_MOD * D_MODEL), 0.02),
        'g_pre': gain(ks[6], (L, 3, D_MODEL)),
        'g_post': gain(ks[7], (L, 3, D_MODEL)),
        'w_ffn1_in': nrm(ks[8], (L, D_MODEL, 2 * D_FF), D_MODEL ** -0.5),
        'w_ffn1_out': nrm(ks[9], (L, D_FF, D_MODEL), D_FF ** -0.5),
        'w_ffn2_in': nrm(ks[10], (L, D_MODEL, 2 * D_FF), D_MODEL ** -0.5),
        'w_ffn2_out': nrm(ks[11], (L, D_FF, D_MODEL), D_FF ** -0.5),
        'w_in': nrm(ks[12], (L, D_MODEL, IN_WIDTH), D_MODEL ** -0.5),
        'w_out': nrm(ks[13], (L, MIX_WIDTH, D_MODEL), MIX_WIDTH ** -0.5),
        'lam_vecs': nrm(ks[14], (L, 4, A_QK_DIM), 0.1),
        'g_subln': gain(ks[15], (L, A_V_DIM)),
        'g_qnorm': gain(ks[16], (L, B_HEAD_DIM)),
        'g_knorm': gain(ks[17], (L, B_HEAD_DIM)),
        'w_spatial': nrm(ks[18], (L, C_GROUPS, CHUNK, CHUNK), CHUNK ** -0.5),
        'b_spatial': gain(ks[19], (L, C_GROUPS, CHUNK)),
        'ln_g': gain(ks[20], (L, GROUP_WIDTH)),
        'ln_b': nrm(ks[21], (L, GROUP_WIDTH), 0.02),
        'g_q_a': gain(ks[22], (L, D_Q_RANK)),
        'w_uq': nrm(ks[23], (L, D_Q_RANK, D_HEADS * (D_NOPE_DIM + D_ROPE_DIM)), D_Q_RANK ** -0.5),
        'g_kv_a': gain(ks[24], (L, D_KV_RANK)),
        'w_ukv': nrm(ks[25], (L, D_KV_RANK, D_HEADS * (D_NOPE_DIM + D_V_DIM)), D_KV_RANK ** -0.5),
    }


def reference(x, c, ctx, c_ctx, w_ada, b_ada, g_pre, g_post, w_ffn1_in, w_ffn1_out,
              w_ffn2_in, w_ffn2_out, w_in, w_out, lam_vecs, g_subln, g_qnorm, g_knorm,
              w_spatial, b_spatial, ln_g, ln_b, g_q_a, w_uq, g_kv_a, w_ukv):
    n_rows = x.shape[1] // GRID_W
    ropes = (axial_rope_tables(n_rows, A_QK_DIM),
             axial_rope_tables(n_rows, B_HEAD_DIM),
             axial_rope_tables(n_rows, D_ROPE_DIM))
    x_ctx, x_lat = ctx, x
    for l in range(DEPTH):
        mod_lat = (jax.nn.silu(c) @ w_ada[l] + b_ada[l]).reshape(c.shape[0], N_MOD, D_MODEL)
        mod_ctx = (jax.nn.silu(c_ctx)[None, :] @ w_ada[l] + b_ada[l]).reshape(1, N_MOD, D_MODEL)
        params = (g_pre[l], g_post[l], w_ffn1_in[l], w_ffn1_out[l], w_ffn2_in[l], w_ffn2_out[l],
                  w_in[l], w_out[l], lam_vecs[l], g_subln[l], g_qnorm[l], g_knorm[l],
                  w_spatial[l], b_spatial[l], ln_g[l], ln_b[l],
                  g_q_a[l], w_uq[l], g_kv_a[l], w_ukv[l])
        x_ctx, x_lat = trunk_layer(x_ctx, x_lat, mod_ctx, mod_lat, params, ropes,
                                   lambda_init(l), l < DEPTH - 1)
    return x_lat
```

```python
import math
from contextlib import ExitStack
import numpy as np
import ml_dtypes
import concourse.bass as bass
import concourse.mybir as mybir
from concourse.bass_utils import run_bass_kernel_spmd

AF = mybir.ActivationFunctionType
ALU = mybir.AluOpType
AX = mybir.AxisListType
F32 = mybir.dt.float32
BF16 = mybir.dt.bfloat16

N_DMA_SEMS = 32

D = 1024
SEQ = 4096
CTX = 256
T = SEQ + CTX
NT = T // 128
DFF = 2816
NFF = DFF // 128
DEPTH = 2
EPS = 1e-6
INW = 2208


class Buf:
    __slots__ = ("name", "writers", "readers")

    def __init__(self, name):
        self.name = name
        self.writers = []
        self.readers = []


class Instr:
    __slots__ = ("eng", "fn", "is_dma", "deps", "sig", "idx", "needs_inc", "phase")


class Prog:
    ENGS = ("pe", "act", "dve", "pool", "sp")

    def __init__(self, nc):
        self.nc = nc
        self.instrs = []
        self.bufs = {}
        self.bstart = 0
        self.phase = 'init'
        self.profile = False
        self._rec = None

    def buf(self, name):
        b = self.bufs.get(name)
        if b is None:
            b = Buf(name)
            self.bufs[name] = b
        return b

    def _new(self, eng, fn, is_dma, deps):
        ins = Instr()
        ins.eng = eng
        ins.fn = fn
        ins.is_dma = is_dma
        ins.idx = len(self.instrs)
        ins.needs_inc = False
        ins.sig = None
        ins.phase = self.phase
        ins.deps = sorted(deps)
        for d in ins.deps:
            self.instrs[d].needs_inc = True
        self.instrs.append(ins)
        return ins

    def begin_record(self):
        self._rec = []

    def end_record(self):
        r = self._rec
        self._rec = None
        return r

    def replay(self, ops):
        for o in ops:
            self.op(*o)

    def op(self, eng, fn, reads=(), writes=(), partial=False, is_dma=False):
        if self._rec is not None:
            self._rec.append((eng, fn, tuple(reads), tuple(writes), partial, is_dma))
            return None
        reads = [self.buf(x) for x in reads if x is not None]
        writes = [self.buf(x) for x in writes if x is not None]
        instrs = self.instrs
        deps = set()
        for r in reads:
            deps.update(r.writers)
        for wb in writes:
            deps.update(wb.writers)
            deps.update(wb.readers)
        if eng == "pe" and not is_dma:
            deps = {d for d in deps if instrs[d].eng != "pe" or instrs[d].is_dma}
        ins = self._new(eng, fn, is_dma, deps)
        for r in reads:
            r.readers.append(ins.idx)
        for wb in writes:
            if partial:
                wb.writers.append(ins.idx)
            else:
                wb.writers = [ins.idx]
                wb.readers = []
        return ins

    def dma(self, q, fn, reads=(), writes=(), partial=False):
        return self.op(q, fn, reads, writes, partial, is_dma=True)

    def barrier(self):
        last = {}
        dmas = []
        for ins in self.instrs[self.bstart:]:
            if ins.fn is None:
                continue
            if ins.is_dma:
                dmas.append(ins.idx)
            else:
                last[ins.eng] = ins.idx
        for e in self.ENGS:
            deps = list(last.values()) + dmas
            self._new(e, None, False, deps)
        self.bstart = len(self.instrs)
        for b in self.bufs.values():
            b.writers = []
            b.readers = []

    def emit(self):
        nc = self.nc
        with ExitStack() as es:
            esem = {e: es.enter_context(nc.semaphore("s_" + e)) for e in self.ENGS}
            dsems = [es.enter_context(nc.semaphore("d%d" % i)) for i in range(N_DMA_SEMS)]
            dval = [0] * N_DMA_SEMS
            ecnt = {e: 0 for e in self.ENGS}
            k = 0
            pre = {}
            for ins in self.instrs:
                if ins.is_dma:
                    si = k % N_DMA_SEMS
                    k += 1
                    pre[ins.idx] = (dsems[si], dval[si])
                    dval[si] += 16
                    ins.sig = (dsems[si], dval[si])
                elif ins.needs_inc:
                    ecnt[ins.eng] += 1
                    ins.sig = (esem[ins.eng], ecnt[ins.eng])
            streams = {e: [i for i in self.instrs if i.eng == e] for e in self.ENGS}
            instrs = self.instrs

            def run(e, name):
                waited = {}

                def wait(sem, val):
                    if val <= 0:
                        return
                    key = id(sem)
                    if waited.get(key, 0) >= val:
                        return
                    e.wait_ge(sem, val)
                    waited[key] = val

                cur = [None, None]
                for ins in streams[name]:
                    if self.profile and ins.fn is not None and ins.phase != cur[0]:
                        if cur[0] is not None:
                            nc.leave_named_scope(cur[0], cur[1], False)
                        cur[0] = ins.phase
                        cur[1], _ = nc.enter_named_scope(ins.phase, False)
                    need = {}
                    for d in ins.deps:
                        s, v = instrs[d].sig
                        k2 = id(s)
                        if k2 not in need or need[k2][1] < v:
                            need[k2] = (s, v)
                    for s, v in need.values():
                        wait(s, v)
                    if ins.fn is None:
                        continue
                    if ins.is_dma:
                        s, v = pre[ins.idx]
                        wait(s, v)
                    r = ins.fn(e)
                    if ins.is_dma:
                        r.then_inc(ins.sig[0], 16)
                    elif ins.needs_inc:
                        r.then_inc(ins.sig[0], 1)
                if cur[0] is not None:
                    nc.leave_named_scope(cur[0], cur[1], False)

            with nc.Block() as block:
                @block.sync
                def _(e):
                    run(e, "sp")

                @block.tensor
                def _(e):
                    run(e, "pe")

                @block.scalar
                def _(e):
                    run(e, "act")

                @block.vector
                def _(e):
                    run(e, "dve")

                @block.gpsimd
                def _(e):
                    run(e, "pool")


ARENA_WORDS = 53100


class KB:
    def __init__(self, debug=()):
        self.debug = set(debug)
        nc = bass.Bass("TRN2", target_bir_lowering=False)
        self.nc = nc
        self.P = Prog(nc)
        self.P.profile = 'profile' in self.debug
        self.es = ExitStack()
        self.arena = self.es.enter_context(nc.sbuf_tensor("arena", [128, ARENA_WORDS], F32))
        self.psum = self.es.enter_context(nc.psum_tensor("psum", [128, 4096], F32))
        self.off = 0
        self.uid = 0

    def alloc(self, nfree, dt=F32):
        nb = nfree * (2 if dt == BF16 else 4)
        nw = (nb + 3) // 4
        nw = (nw + 7) // 8 * 8
        assert self.off + nw <= ARENA_WORDS, ("SBUF arena overflow", self.off, nw)
        a = self.arena[:, self.off:self.off + nw]
        self.off += nw
        if dt == BF16:
            a = a.bitcast(BF16)[:, 0:nfree]
        else:
            a = a[:, 0:nfree]
        return a

    def mark(self):
        return self.off

    def release(self, m):
        self.off = m

    def bank(self, i, dt=F32, n=1):
        a = self.psum[:, 512 * i:512 * (i + n)]
        if dt == BF16:
            a = a.bitcast(BF16)
        return a

    def dram(self, name, shape, dt, kind="Internal"):
        if name in self.debug:
            kind = "ExternalOutput"
        return self.nc.dram_tensor(name, list(shape), dt, kind=kind).ap()

    def name(self, p):
        self.uid += 1
        return "%s_%d" % (p, self.uid)

    def mm(self, out, lhsT, rhs, start, stop, reads, writes, **kw):
        self.P.op("pe", lambda e: e.matmul(out, lhsT=lhsT, rhs=rhs, start=start, stop=stop, **kw),
                  reads=reads, writes=writes, partial=True)

    def tr(self, out, in_, reads, writes):
        ident = self.ident
        self.P.op("pe", lambda e: e.transpose(out=out, in_=in_, identity=ident),
                  reads=list(reads) + ["ident"], writes=writes, partial=True)

    def act(self, out, in_, func, reads, writes, partial=False, eng="act", **kw):
        self.P.op(eng, lambda e: e.activation(out=out, in_=in_, func=func, **kw),
                  reads=reads, writes=writes, partial=partial)

    def tt(self, eng, out, in0, in1, op, reads, writes, partial=False):
        self.P.op(eng, lambda e: e.tensor_tensor(out=out, in0=in0, in1=in1, op=op),
                  reads=reads, writes=writes, partial=partial)

    def ts(self, eng, out, in0, s1, op0, reads, writes, s2=None, op1=None, partial=False):
        if op1 is None:
            self.P.op(eng, lambda e: e.tensor_scalar(out=out, in0=in0, scalar1=s1, scalar2=None, op0=op0),
                      reads=reads, writes=writes, partial=partial)
        else:
            self.P.op(eng, lambda e: e.tensor_scalar(out=out, in0=in0, scalar1=s1, scalar2=s2, op0=op0, op1=op1),
                      reads=reads, writes=writes, partial=partial)

    def stt(self, out, in0, scalar, in1, op0, op1, reads, writes, partial=False):
        self.P.op("dve", lambda e: e.scalar_tensor_tensor(out=out, in0=in0, scalar=scalar, in1=in1, op0=op0, op1=op1),
                  reads=reads, writes=writes, partial=partial)

    def cp(self, eng, out, in_, reads, writes, partial=False):
        if eng == "act":
            self.P.op("act", lambda e: e.copy(out=out, in_=in_), reads=reads, writes=writes, partial=partial)
        else:
            self.P.op(eng, lambda e: e.tensor_copy(out=out, in_=in_), reads=reads, writes=writes, partial=partial)

    def ld(self, q, out, in_, reads, writes, partial=False, **kw):
        self.P.dma(q, lambda e: e.dma_start(out=out, in_=in_, **kw), reads=reads, writes=writes, partial=partial)

    def rstd(self, out, ss, n, reads_name, scale):
        self.ts("dve", out, ss, scale, ALU.mult, reads=[reads_name], writes=[reads_name], s2=EPS, op1=ALU.add)
        nh = self.neghalf[:, 0:n]
        self.tt("pool", out, out, nh, ALU.pow, reads=[reads_name, "consts"], writes=[reads_name])

    def load_cast(self, stg, dst, src, wname, n3=None):
        i = self.stg_i
        self.stg_i += 1
        sl = i % len(stg)
        n = dst.shape[-1] if n3 is None else dst.shape[-1] * (dst.shape[-2] if len(dst.shape) > 2 else 1)
        sv = stg[sl][:, 0:n]
        sn = "stg%d_%d" % (id(stg) % 1000, sl)
        if n3 is not None:
            sv3 = sv.rearrange("p (a n) -> p a n", n=n3)
            self.ld("sp" if i % 2 == 0 else "act", sv3, src, [], [sn])
            d3 = dst if len(dst.shape) > 2 else dst.rearrange("p (a n) -> p a n", n=n3)
            self.cp(("dve", "pool", "dve")[i % 3], d3, sv3, [sn], [wname], partial=True)
        else:
            self.ld("sp" if i % 2 == 0 else "act", sv, src, [], [sn])
            self.cp(("dve", "pool", "dve")[i % 3], dst, sv, [sn], [wname], partial=True)

    def setup_consts(self):
        P = self.P
        self.ident = self.alloc(128, BF16)
        self.neghalf = self.alloc(16, F32)
        self.ones_f = self.alloc(64, F32)
        self.ones_b = self.alloc(128, BF16)
        self.epsb = self.alloc(1, F32)
        m = self.mark()
        idf = self.alloc(128, F32)
        ident, nh, of, ob, epsb = self.ident, self.neghalf, self.ones_f, self.ones_b, self.epsb
        P.op("pool", lambda e: e.memset(idf, 0.0), writes=["idf"])
        P.op("pool", lambda e: e.affine_select(out=idf, in_=idf, pattern=[[-1, 128]], compare_op=ALU.not_equal,
                                               fill=1.0, base=0, channel_multiplier=1), reads=["idf"], writes=["idf"])
        P.op("dve", lambda e: e.tensor_copy(out=ident, in_=idf), reads=["idf"], writes=["ident"])
        P.op("pool", lambda e: e.memset(nh, -0.5), writes=["consts"])
        P.op("pool", lambda e: e.memset(of, 1.0), writes=["consts"], partial=True)
        P.op("pool", lambda e: e.memset(ob, 1.0), writes=["consts"], partial=True)
        P.op("pool", lambda e: e.memset(epsb, EPS), writes=["consts"], partial=True)
        P.barrier()
        self.release(m)

    def phase_mod(self, l):
        self.P.phase = 'mod%d' % l
        P = self.P
        I = self.inp
        m = self.mark()
        cv = self.alloc(16, F32)
        sc = self.alloc(16, F32)
        cv3 = cv.rearrange("p (c s) -> p c s", s=2)
        sc3 = sc.rearrange("p (c s) -> p c s", s=2)
        modsb = self.alloc(9 * D, F32)
        bada = self.alloc(9 * D, F32)
        gpre = self.alloc(3 * D, F32)
        gpost = self.alloc(3 * D, F32)
        dv = self.alloc(9 * D, F32)
        NS = 3
        wsl = [self.alloc(8 * 512, F32) for _ in range(NS)]
        self.ld("sp", cv3, I["cvec"], [], ["cv"])
        self.ld("sp", bada[0:2, :], I["b_ada"][l:l + 1, :].partition_broadcast(2), [], ["bada"])
        self.ld("sp", gpre[0:2, :], I["g_pre"][l:l + 1].rearrange("o j d -> o (j d)").partition_broadcast(2), [], ["gpre"])
        self.ld("sp", gpost[0:2, :], I["g_post"][l:l + 1].rearrange("o j d -> o (j d)").partition_broadcast(2), [], ["gpost"])
        self.act(sc, cv, AF.Tanh, ["cv"], ["sc"], scale=0.5)
        self.ts("dve", sc, sc, 1.0, ALU.add, ["sc"], ["sc"], s2=0.5, op1=ALU.mult)
        self.tt("dve", sc, sc, cv, ALU.mult, ["sc", "cv"], ["sc"])
        wada = I["w_ada"][l].rearrange("(c p) n -> p c n", p=128)
        for n in range(18):
            s = n % NS
            w3 = wsl[s].rearrange("p (c n) -> p c n", n=512)
            self.ld("sp" if n % 2 == 0 else "act", w3, wada[:, :, n * 512:(n + 1) * 512], [], ["wsl%d" % s])
            pb = "psm%d" % (n % 2)
            po = self.bank(n % 2)[0:2, :]
            for c in range(8):
                self.mm(po, sc3[:, c, :], w3[:, c, :], c == 0, c == 7, ["sc", "wsl%d" % s], [pb])
            self.tt("dve", modsb[0:2, n * 512:(n + 1) * 512], po, bada[0:2, n * 512:(n + 1) * 512], ALU.add,
                    [pb, "bada"], ["modsb"], partial=True)
        for j in range(3):
            wj = 1.0 if j == 1 else 0.5
            sh = modsb[0:2, (3 * j) * D:(3 * j + 1) * D]
            scl = modsb[0:2, (3 * j + 1) * D:(3 * j + 2) * D]
            gt = modsb[0:2, (3 * j + 2) * D:(3 * j + 3) * D]
            self.stt(dv[0:2, (3 * j) * D:(3 * j + 1) * D], scl, 1.0, gpre[0:2, j * D:(j + 1) * D], ALU.add, ALU.mult,
                     ["modsb", "gpre"], ["dv"], partial=True)
            self.cp("dve", dv[0:2, (3 * j + 1) * D:(3 * j + 2) * D], sh, ["modsb"], ["dv"], partial=True)
            self.stt(dv[0:2, (3 * j + 2) * D:(3 * j + 3) * D], gt, wj, gpost[0:2, j * D:(j + 1) * D], ALU.mult, ALU.mult,
                     ["modsb", "gpost"], ["dv"], partial=True)
        self.ld("sp", self.DV[l].rearrange("s j k d -> s (j k d)"), dv[0:2, :], ["dv"], ["DV%d" % l])
        P.barrier()
        self.release(m)

    def phase_ffn(self, l, j, src, dst, blocks):
        self.P.phase = 'ffn%d_%d' % (l, j)
        P = self.P
        I = self.inp
        m = self.mark()
        w_in = I["w_ffn1_in" if j == 0 else "w_ffn2_in"][l]
        w_out = I["w_ffn1_out" if j == 0 else "w_ffn2_out"][l]
        w1 = self.alloc(8 * 2 * DFF, BF16)
        w2 = self.alloc(NFF * D, BF16)
        w13 = w1.rearrange("p (c n) -> p c n", n=2 * DFF)
        w23 = w2.rearrange("p (c n) -> p c n", n=D)
        QW = 704
        stg = [self.alloc(QW) for _ in range(4)]
        self.stg_i = 0
        w_in_v = w_in.rearrange("(c p) n -> p c n", p=128)
        for q in range(4):
            for half in range(2):
                c0 = half * DFF + q * QW
                for c in range(8):
                    self.load_cast(stg, w13[:, c, c0:c0 + QW], w_in_v[:, c, c0:c0 + QW], "w1_%d_%d" % (half, q))
        w_out_v = w_out.rearrange("(c p) n -> p c n", p=128)
        for c0 in range(NFF):
            for h in range(2):
                self.load_cast(stg, w23[:, c0, h * 512:(h + 1) * 512], w_out_v[:, c0, h * 512:(h + 1) * 512], "w2")
        G = self.alloc(D)
        S = self.alloc(D)
        Gp = self.alloc(D)
        xin = [self.alloc(2 * D) for _ in range(2)]
        hb = self.alloc(2 * D, BF16)
        hT = self.alloc(8 * 256, BF16)
        hT3 = hT.rearrange("p (c t) -> p c t", t=256)
        actT = self.alloc(NFF * 256, BF16)
        actT3 = actT.rearrange("p (c t) -> p c t", t=256)
        tmp = [self.alloc(D) for _ in range(2)]
        junk = self.alloc(D, BF16)
        sg = [self.alloc(256) for _ in range(3)]
        st = [self.alloc(8) for _ in range(2)]
        cur_stream = [None, None]

        def load_mod(s):
            if cur_stream[0] == s:
                return
            cur_stream[0] = s
            dvl = self.DV[l]
            self.ld("sp", G, dvl[s, j, 0:1, :].partition_broadcast(128), ["DV%d" % l], ["G"])
            self.ld("sp", S, dvl[s, j, 1:2, :].partition_broadcast(128), ["DV%d" % l], ["S"])

        def load_gp(s):
            if cur_stream[1] == s:
                return
            cur_stream[1] = s
            dvl = self.DV[l]
            self.ld("sp", Gp, dvl[s, j, 2:3, :].partition_broadcast(128), ["DV%d" % l], ["Gp"])

        def load(i):
            b, s = blocks[i]
            sl = i % 2
            x3 = xin[sl].rearrange("p (t d) -> p t d", d=D)
            self.ld("sp", x3, src(b).rearrange("(t p) d -> p t d", p=128), [self.srcname(b)], ["xin%d" % sl])

        def prenorm(i):
            b, s = blocks[i]
            sl = i % 2
            load_mod(s)
            xn = "xin%d" % sl
            stn = "st%d" % sl
            for t in range(2):
                self.act(junk, xin[sl][:, t * D:(t + 1) * D], AF.Square, [xn], [stn, "junk"], partial=True,
                         accum_out=st[sl][:, t:t + 1])
            self.ts("dve", st[sl][:, 2:4], st[sl][:, 0:2], 1.0 / D, ALU.mult, [stn], [stn], s2=EPS, op1=ALU.add)
            self.tt("pool", st[sl][:, 2:4], st[sl][:, 2:4], self.neghalf[:, 0:2], ALU.pow, [stn, "consts"], [stn])
            for t in range(2):
                tn = "tmp%d" % t
                self.stt(tmp[t], xin[sl][:, t * D:(t + 1) * D], st[sl][:, 2 + t:3 + t], G, ALU.mult, ALU.mult,
                         [xn, stn, "G"], [tn])
                self.tt("dve", hb[:, t * D:(t + 1) * D], tmp[t], S, ALU.add, [tn, "S"], ["hb%d" % t])

        def transposes(i):
            for t in range(2):
                pT = self.bank(0, BF16)
                pT3 = pT.rearrange("p (c t) -> p c t", t=128)
                for c in range(8):
                    self.tr(pT3[:, c, :], hb[:, t * D + c * 128:t * D + (c + 1) * 128], ["hb%d" % t], ["psT"])
                self.cp("act", hT3[:, :, t * 128:(t + 1) * 128], pT3, ["psT"], ["hT"], partial=True)

        def mm1(i, side=()):
            side = list(side)
            per = max(1, (len(side) + 13) // 14)
            for f in range(NFF):
                if f >= 3 and side:
                    P.replay(side[:per])
                    side = side[per:]
                bk = 1 + (f % 3)
                pn = "psM%d" % bk
                pm = self.bank(bk)
                wq = sorted({(f * 128) // 704, (f * 128 + 127) // 704})
                for half in range(2):
                    col = half * DFF + f * 128
                    wn = ["w1_%d_%d" % (half, q) for q in wq]
                    for c in range(8):
                        self.mm(pm[:, half * 256:(half + 1) * 256], w13[:, c, col:col + 128], hT3[:, c, :],
                                c == 0, c == 7, wn + ["hT"], [pn])
                sgi = f % 3
                self.act(sg[sgi], pm[:, 0:256], AF.Silu, [pn], ["sg%d" % sgi])
                self.tt("dve", actT3[:, f, :], pm[:, 256:512], sg[sgi], ALU.mult, [pn, "sg%d" % sgi], ["actT"], partial=True)
            P.replay(side)

        def mm2_post(i):
            b, s = blocks[i]
            sl = i % 2
            xn = "xin%d" % sl
            stn = "st%d" % sl
            load_gp(s)
            for t in range(2):
                py = self.bank(4 + 2 * t, n=2)
                pn = "psY%d" % t
                for half in range(2):
                    for f in range(NFF):
                        self.mm(py[:, half * 512:(half + 1) * 512], actT3[:, f, t * 128:(t + 1) * 128],
                                w23[:, f, half * 512:(half + 1) * 512], f == 0, f == NFF - 1, ["actT", "w2"], [pn])
                self.act(junk, py, AF.Square, [pn], [stn, "junk"], partial=True, accum_out=st[sl][:, 4 + t:5 + t])
            self.ts("dve", st[sl][:, 6:8], st[sl][:, 4:6], 1.0 / D, ALU.mult, [stn], [stn], s2=EPS, op1=ALU.add)
            self.tt("pool", st[sl][:, 6:8], st[sl][:, 6:8], self.neghalf[:, 0:2], ALU.pow, [stn, "consts"], [stn])
            for t in range(2):
                py = self.bank(4 + 2 * t, n=2)
                pn = "psY%d" % t
                tn = "tmp%d" % t
                self.stt(tmp[t], py, st[sl][:, 6 + t:7 + t], Gp, ALU.mult, ALU.mult, [pn, stn, "Gp"], [tn])
                self.tt("dve", xin[sl][:, t * D:(t + 1) * D], tmp[t], xin[sl][:, t * D:(t + 1) * D], ALU.add,
                        [tn, xn], [xn], partial=True)
            x3 = xin[sl].rearrange("p (t d) -> p t d", d=D)
            self.ld("sp", dst(b).rearrange("(t p) d -> p t d", p=128), x3, [xn], [self.dstname(b)])

        n = len(blocks)
        load(0)
        prenorm(0)
        transposes(0)
        for i in range(n):
            if i + 1 < n:
                load(i + 1)
            side = ()
            if i + 1 < n:
                P.begin_record()
                prenorm(i + 1)
                side = P.end_record()
            mm1(i, side)
            if i + 1 < n:
                transposes(i + 1)
            mm2_post(i)
        P.barrier()
        self.release(m)

    def srcname(self, b):
        return "XS%d" % b

    def dstname(self, b):
        return "XS%d" % b


    def bc(self, ap, G):
        shp = list(ap.shape)
        return ap[:, None].to_broadcast([shp[0], G] + shp[1:])

    def rope(self, sfx, eng_a, xv, outv, Ct, St, G, Dh, t1, t2, rn, wn, partial=True):
        w = Dh // 4
        t13 = t1[:, 0:G * Dh].rearrange("p (g d) -> p g d", d=Dh)
        t25 = t2[:, 0:G * Dh].rearrange("p (g a h w) -> p g a h w", a=2, h=2, w=w)
        t23 = t2[:, 0:G * Dh].rearrange("p (g d) -> p g d", d=Dh)
        xv5 = xv.rearrange("p g (a h w) -> p g a h w", a=2, h=2, w=w)
        S4 = St.rearrange("p (a h w) -> p a h w", a=2, h=2, w=w)
        self.tt(eng_a, t13, xv, self.bc(Ct, G), ALU.mult, rn + ["rope"], ["rt1_%d" % sfx])
        self.tt("dve", t25[:, :, :, 0, :], xv5[:, :, :, 1, :], self.bc(S4[:, :, 0, :], G), ALU.mult, rn + ["rope"], ["rt2_%d" % sfx], partial=True)
        self.tt("dve", t25[:, :, :, 1, :], xv5[:, :, :, 0, :], self.bc(S4[:, :, 1, :], G), ALU.mult, rn + ["rope"], ["rt2_%d" % sfx], partial=True)
        self.tt("dve", outv, t13, t23, ALU.add, ["rt1_%d" % sfx, "rt2_%d" % sfx], wn, partial=partial)

    def phase_proj(self, l, need_q_ctx):
        self.P.phase = 'proj%d' % l
        P = self.P
        I = self.inp
        m = self.mark()
        NW = INW
        w_in = self.alloc(8 * NW, BF16)
        w_in3 = w_in.rearrange("p (c n) -> p c n", n=NW)
        w_uq = self.alloc(2 * 384, BF16)
        w_uq3 = w_uq.rearrange("p (c n) -> p c n", n=384)
        w_ukv = self.alloc(512, BF16)
        wsp = self.alloc(4 * 128, BF16)
        wsp3 = wsp.rearrange("p (g n) -> p g n", n=128)
        G = self.alloc(D)
        S = self.alloc(D)
        gqk = self.alloc(384)
        lng = self.alloc(256)
        lnb = self.alloc(256)
        gqa = self.alloc(256)
        gkva = self.alloc(128)
        bsp = self.alloc(4)
        xt = [self.alloc(D) for _ in range(4)]
        rts = [self.alloc(192) for _ in range(6)]
        hb = self.alloc(D, BF16)
        hT = self.alloc(8 * 128, BF16)
        hT3 = hT.rearrange("p (c t) -> p c t", t=128)
        pj = [self.alloc(NW) for _ in range(4)]
        t1_s = [self.alloc(512) for _ in range(2)]
        t2_s = [self.alloc(512) for _ in range(2)]
        ta = self.alloc(512)
        tb = self.alloc(512)
        tB_s = [self.alloc(384) for _ in range(2)]
        tC_s = [self.alloc(512) for _ in range(2)]
        tD_s = [self.alloc(256) for _ in range(2)]
        guv_s = [self.alloc(512) for _ in range(2)]
        junk = self.alloc(D, BF16)
        qkA_s = [self.alloc(512, BF16) for _ in range(2)]
        qkB_s = [self.alloc(384, BF16) for _ in range(2)]
        vnb_s = [self.alloc(256, BF16) for _ in range(2)]
        cl_s = [self.alloc(256, BF16) for _ in range(2)]
        cn_s = [self.alloc(384, BF16) for _ in range(2)]
        cT_s = [self.alloc(3 * 128, BF16) for _ in range(2)]
        qDb_s = [self.alloc(4 * 96, BF16) for _ in range(2)]
        kDb_s = [self.alloc(4 * 96, BF16) for _ in range(2)]
        krr_s = [self.alloc(32) for _ in range(2)]
        qDf_s = [self.alloc(128) for _ in range(2)]
        st = [self.alloc(32) for _ in range(4)]
        mstg = self.mark()
        stg = [self.alloc(2208) for _ in range(6)]
        self.stg_i = 0
        wv = I["w_in_p"][l].rearrange("(c p) n -> p c n", p=128)
        for c in range(8):
            self.load_cast(stg, w_in3[:, c, :], wv[:, c, :], "w_in")
        for c in range(2):
            self.load_cast(stg, w_uq3[:, c, :], I["w_uq"][l][c * 128:(c + 1) * 128, :], "w_uq")
        self.load_cast(stg, w_ukv, I["w_ukv"][l], "w_ukv")
        self.load_cast(stg, wsp, I["wspT"][l], "wsp", n3=128)
        self.ld("sp", gqk, I["gqk"][l:l + 1, :].partition_broadcast(128), [], ["par"], partial=True)
        self.ld("sp", lng, I["ln_g"][l:l + 1, :].partition_broadcast(128), [], ["par"], partial=True)
        self.ld("sp", lnb, I["ln_b"][l:l + 1, :].partition_broadcast(128), [], ["par"], partial=True)
        self.ld("sp", gqa, I["g_q_a"][l:l + 1, :].partition_broadcast(128), [], ["par"], partial=True)
        self.ld("sp", gkva, I["g_kv_a"][l:l + 1, :].partition_broadcast(128), [], ["par"], partial=True)
        self.ld("sp", bsp, I["bspT"][l], [], ["par"], partial=True)
        P.barrier()
        self.release(mstg)
        stgT = [self.alloc(17 * 512, BF16) for _ in range(2)]
        vst = [self.alloc(4 * 1280, BF16) for _ in range(2)]
        for sl in range(2):
            v4 = vst[sl].rearrange("p (t h c) -> p t h c", h=10, c=128)
            P.op("pool", lambda e, v4=v4: e.memset(v4[:, :, :, 64:128], 1.0), writes=["vst%d" % sl], partial=True)

        blocks = [(0, 2, 1)] + [(2 + 4 * i, 4, 0) for i in range(8)]
        tiles = []
        for bi, (g0, nt, s) in enumerate(blocks):
            for tt in range(nt):
                tiles.append((bi, g0, nt, s, tt))
        cur = [None]

        def loadx(ti):
            bi, g0, nt, s, tt = tiles[ti]
            gt = g0 + tt
            k4 = ti % 4
            self.ld("sp", xt[k4], self.XS[gt * 128:(gt + 1) * 128, :], ["XSall"], ["pxt%d" % k4])
            if s == 0:
                lt = gt - 2
                self.ld("sp", rts[ti % 6], I["rope"][lt * 128:(lt + 1) * 128, :], [], ["rt%d" % (ti % 6)])

        def stageA(ti):
            bi, g0, nt, s, tt = tiles[ti]
            ps = ti % 4
            if cur[0] != s:
                cur[0] = s
                dvl = self.DV[l]
                self.ld("sp", G, dvl[s, 1, 0:1, :].partition_broadcast(128), ["DV%d" % l], ["G"])
                self.ld("sp", S, dvl[s, 1, 1:2, :].partition_broadcast(128), ["DV%d" % l], ["S"])
            xn = "pxt%d" % ps
            stn = "pst%d" % ps
            x = xt[ps]
            sv = st[ps]
            self.act(junk, x, AF.Square, [xn], [stn, "junk"], partial=True, accum_out=sv[:, 0:1])
            self.ts("dve", sv[:, 1:2], sv[:, 0:1], 1.0 / D, ALU.mult, [stn], [stn], s2=EPS, op1=ALU.add)
            self.tt("pool", sv[:, 1:2], sv[:, 1:2], self.neghalf[:, 0:1], ALU.pow, [stn, "consts"], [stn])
            self.stt(ta, x[:, 0:512], sv[:, 1:2], G[:, 0:512], ALU.mult, ALU.mult, [xn, stn, "G"], ["ta"])
            self.tt("pool", hb[:, 0:512], ta, S[:, 0:512], ALU.add, ["ta", "S"], ["hb"], partial=True)
            self.stt(tb, x[:, 512:1024], sv[:, 1:2], G[:, 512:1024], ALU.mult, ALU.mult, [xn, stn, "G"], ["tb"])
            self.tt("pool", hb[:, 512:1024], tb, S[:, 512:1024], ALU.add, ["tb", "S"], ["hb"], partial=True)
            pT = self.bank(0, BF16)
            pT3 = pT.rearrange("p (c t) -> p c t", t=128)
            for c in range(8):
                self.tr(pT3[:, c, :], hb[:, c * 128:(c + 1) * 128], ["hb"], ["psT"])
            self.cp("act", hT3, pT3, ["psT"], ["hT"])
            chunks = [(0, 512), (512, 512), (1024, 512), (1536, 512), (2048, 160)]
            for n, (c0, cw) in enumerate(chunks):
                bk = 1 + (n % 2)
                pn = "psP%d" % bk
                po = self.bank(bk)[:, 0:cw]
                for c in range(8):
                    self.mm(po, hT3[:, c, :], w_in3[:, c, c0:c0 + cw], c == 0, c == 7, ["hT", "w_in"], [pn])
                self.cp("act" if n % 2 == 0 else "dve", pj[ps][:, c0:c0 + cw], po, [pn], ["pj%d" % ps], partial=True)

        def stageB(ti):
            bi, g0, nt, s, tt = tiles[ti]
            sl = bi % 2
            ps = ti % 4
            p = ti % 2
            X = 3 + 2 * p
            Y = 4 + 2 * p
            psX = "ps%d" % X
            psY = "ps%d" % Y
            t1, t2, tB, tC, tD, guv, krr, qDf = t1_s[p], t2_s[p], tB_s[p], tC_s[p], tD_s[p], guv_s[p], krr_s[p], qDf_s[p]
            qkA, qkB, vnb, cl, cn, cT, qDb, kDb = qkA_s[p], qkB_s[p], vnb_s[p], cl_s[p], cn_s[p], cT_s[p], qDb_s[p], kDb_s[p]
            cT3 = cT.rearrange("p (c t) -> p c t", t=128)
            lat = (s == 0)
            latA = lat and 'norA' not in self.debug
            latB = lat and 'norB' not in self.debug
            latDq = lat and 'norDq' not in self.debug
            latDk = lat and 'norDk' not in self.debug
            pjn = "pj%d" % ps
            stn = "pst%d" % ps
            x = pj[ps]
            sv = st[ps]
            rt = rts[ti % 6]
            C32, S32, C64, S64 = rt[:, 0:32], rt[:, 32:64], rt[:, 64:128], rt[:, 128:192]
            rn = ["rt%d" % (ti % 6)]
            vn = "vst%d" % sl
            v4 = vst[sl].rearrange("p (t h c) -> p t h c", h=10, c=128)
            xa = x[:, 0:512].rearrange("p (g d) -> p g d", d=32)
            if latA:
                self.rope(p, "pool", xa, qkA.rearrange("p (g d) -> p g d", d=32), C32, S32, 16, 32, t1, t2, [pjn] + rn, ["qkA%d" % p], partial=False)
            else:
                self.cp("pool", qkA, x[:, 0:512], [pjn], ["qkA%d" % p])
            xb = x[:, 512:896]
            xb3 = xb.rearrange("p (g d) -> p g d", d=64)
            self.act(tB, xb, AF.Square, [pjn], ["tB%d" % p])
            P.op("dve", lambda e: e.tensor_reduce(out=sv[:, 8:14], in_=tB.rearrange("p (g d) -> p g d", d=64),
                                                  op=ALU.add, axis=AX.X), reads=["tB%d" % p], writes=[stn])
            self.ts("dve", sv[:, 8:14], sv[:, 8:14], 1.0 / 64, ALU.mult, [stn], [stn], s2=EPS, op1=ALU.add)
            self.tt("pool", sv[:, 8:14], sv[:, 8:14], self.neghalf[:, 0:6], ALU.pow, [stn, "consts"], [stn])
            tB3 = tB.rearrange("p (g d) -> p g d", d=64)
            self.tt("dve", tB3, xb3, sv[:, 8:14, None].to_broadcast([128, 6, 64]), ALU.mult, [pjn, stn], ["tB%d" % p])
            self.tt("pool", tB, tB, gqk, ALU.mult, ["tB%d" % p, "par"], ["tB%d" % p])
            if latB:
                self.rope(p, "pool", tB3, qkB.rearrange("p (g d) -> p g d", d=64), C64, S64, 6, 64, t1, t2, ["tB%d" % p] + rn, ["qkB%d" % p], partial=False)
            else:
                self.cp("pool", qkB, tB, ["tB%d" % p], ["qkB%d" % p])
            self.cp("act", v4[:, tt, 0:4, 0:64], x[:, 1024:1280].rearrange("p (h c) -> p h c", c=64), [pjn], [vn], partial=True)
            self.cp("act", v4[:, tt, 4:6, 0:64], x[:, 896:1024].rearrange("p (h c) -> p h c", c=64), [pjn], [vn], partial=True)
            xc = x[:, 1280:1792]
            self.act(tC, xc, AF.Square, [pjn], ["tC%d" % p])
            self.ts("dve", tC, tC, 0.044715, ALU.mult, ["tC%d" % p], ["tC%d" % p], s2=1.0, op1=ALU.add)
            self.tt("pool", tC, tC, xc, ALU.mult, ["tC%d" % p, pjn], ["tC%d" % p])
            self.act(tC, tC, AF.Tanh, ["tC%d" % p], ["tC%d" % p], scale=0.7978845608028654)
            self.ts("dve", tC, tC, 1.0, ALU.add, ["tC%d" % p], ["tC%d" % p], s2=0.5, op1=ALU.mult)
            self.tt("dve", guv, tC, xc, ALU.mult, ["tC%d" % p, pjn], ["guv%d" % p])
            gv = guv[:, 256:512]
            self.act(junk[:, 0:256], gv, AF.Identity, ["guv%d" % p], [stn, "junk"], partial=True, accum_out=sv[:, 16:17])
            self.act(junk[:, 256:512], gv, AF.Square, ["guv%d" % p], [stn, "junk"], partial=True, accum_out=sv[:, 17:18])
            self.ts("dve", sv[:, 18:19], sv[:, 16:17], 1.0 / 256, ALU.mult, [stn], [stn])
            self.tt("dve", sv[:, 19:20], sv[:, 18:19], sv[:, 18:19], ALU.mult, [stn], [stn])
            self.stt(sv[:, 20:21], sv[:, 17:18], 1.0 / 256, sv[:, 19:20], ALU.mult, ALU.subtract, [stn], [stn])
            self.ts("dve", sv[:, 20:21], sv[:, 20:21], EPS, ALU.add, [stn], [stn])
            self.tt("pool", sv[:, 20:21], sv[:, 20:21], self.neghalf[:, 0:1], ALU.pow, [stn, "consts"], [stn])
            self.ts("dve", tD, gv, sv[:, 18:19], ALU.subtract, ["guv%d" % p, stn], ["tD%d" % p], s2=sv[:, 20:21], op1=ALU.mult)
            self.tt("pool", tD, tD, lng, ALU.mult, ["tD%d" % p, "par"], ["tD%d" % p])
            self.tt("pool", vnb, tD, lnb, ALU.add, ["tD%d" % p, "par"], ["vnb%d" % p])
            pg = self.bank(X)[:, 256:512]
            for g in range(4):
                self.mm(pg[:, g * 64:(g + 1) * 64], wsp3[:, g, :], vnb[:, g * 64:(g + 1) * 64], True, True, ["wsp", "vnb%d" % p], [psX])
            for g in range(4):
                self.stt(cl[:, g * 64:(g + 1) * 64], pg[:, g * 64:(g + 1) * 64], bsp[:, g:g + 1], guv[:, g * 64:(g + 1) * 64],
                         ALU.add, ALU.mult, [psX, "par", "guv%d" % p], ["cl%d" % p], partial=True)
            self.act(junk[:, 512:768], x[:, 1792:2048], AF.Square, [pjn], [stn, "junk"], partial=True, accum_out=sv[:, 24:25])
            self.act(junk[:, 768:896], x[:, 2048:2176], AF.Square, [pjn], [stn, "junk"], partial=True, accum_out=sv[:, 25:26])
            self.ts("dve", sv[:, 26:27], sv[:, 24:25], 1.0 / 256, ALU.mult, [stn], [stn], s2=EPS, op1=ALU.add)
            self.ts("dve", sv[:, 27:28], sv[:, 25:26], 1.0 / 128, ALU.mult, [stn], [stn], s2=EPS, op1=ALU.add)
            self.tt("pool", sv[:, 26:28], sv[:, 26:28], self.neghalf[:, 0:2], ALU.pow, [stn, "consts"], [stn])
            self.stt(cn[:, 0:256], x[:, 1792:2048], sv[:, 26:27], gqa, ALU.mult, ALU.mult, [pjn, stn, "par"], ["cn%d" % p], partial=True)
            self.stt(cn[:, 256:384], x[:, 2048:2176], sv[:, 27:28], gkva, ALU.mult, ALU.mult, [pjn, stn, "par"], ["cn%d" % p], partial=True)
            pc = self.bank(X, BF16)[:, 0:384]
            pc3 = pc.rearrange("p (c t) -> p c t", t=128)
            for c in range(3):
                self.tr(pc3[:, c, :], cn[:, c * 128:(c + 1) * 128], ["cn%d" % p], [psX])
            self.cp("act", cT3, pc3, [psX], ["cT%d" % p])
            pq = self.bank(Y)[:, 0:384]
            for c in range(2):
                self.mm(pq, cT3[:, c, :], w_uq3[:, c, :], c == 0, c == 1, ["cT%d" % p, "w_uq"], [psY])
            pq3 = pq.rearrange("p (h d) -> p h d", d=96)
            qD3 = qDb.rearrange("p (h d) -> p h d", d=96)
            kD3 = kDb.rearrange("p (h d) -> p h d", d=96)
            qDf3 = qDf.rearrange("p (h d) -> p h d", d=32)
            self.cp("act", qD3[:, :, 0:64], pq3[:, :, 0:64], [psY], ["qDb%d" % p], partial=True)
            if latDq:
                self.cp("act", qDf3, pq3[:, :, 64:96], [psY], ["qDf%d" % p])
            else:
                self.cp("act", qD3[:, :, 64:96], pq3[:, :, 64:96], [psY], ["qDb%d" % p], partial=True)
            pkv = self.bank(Y)
            self.mm(pkv, cT3[:, 2, :], w_ukv, True, True, ["cT%d" % p, "w_ukv"], [psY])
            pkv3 = pkv.rearrange("p (h d) -> p h d", d=128)
            self.cp("act", kD3[:, :, 0:64], pkv3[:, :, 0:64], [psY], ["kDb%d" % p], partial=True)
            self.cp("act", v4[:, tt, 6:10, 0:64], pkv3[:, :, 64:128], [psY], [vn], partial=True)
            if latDq:
                self.rope(p, "pool", qDf3, qD3[:, :, 64:96], C32, S32, 4, 32, t1, t2, ["qDf%d" % p] + rn, ["qDb%d" % p])
            if latDk:
                self.rope(p, "pool", x[:, 2176:2208].rearrange("p (g d) -> p g d", d=32), krr.rearrange("p (g d) -> p g d", d=32),
                          C32, S32, 1, 32, t1, t2, [pjn] + rn, ["krr%d" % p], partial=False)
                self.cp("pool", kD3[:, :, 64:96], self.bc(krr, 4), ["krr%d" % p], ["kDb%d" % p], partial=True)
            else:
                self.cp("pool", kD3[:, :, 64:96], self.bc(x[:, 2176:2208], 4), [pjn], ["kDb%d" % p], partial=True)
            sT = stgT[sl].rearrange("p (b t) -> p b t", t=512)
            sn = "stgT%d" % sl
            c0 = tt * 128
            p7 = self.bank(Y, BF16).rearrange("p (c t) -> p c t", t=128)
            p5 = self.bank(X, BF16)[:, 384:512]
            for k in range(4):
                self.tr(p7[:, k, :], qkA[:, k * 128:(k + 1) * 128], ["qkA%d" % p], [psY])
            for k in range(3):
                self.tr(p7[:, 4 + k, :], qkB[:, k * 128:(k + 1) * 128], ["qkB%d" % p], [psY])
            self.tr(p7[:, 7, :], cl[:, 0:128], ["cl%d" % p], [psY])
            self.tr(p5, cl[:, 128:256], ["cl%d" % p], [psX])
            self.cp("dve", sT[:, 0:8, c0:c0 + 128], p7, [psY], [sn], partial=True)
            self.cp("act", sT[:, 8, c0:c0 + 128], p5, [psX], [sn], partial=True)
            for h in range(4):
                self.tr(p7[0:96, h, :], qDb[:, h * 96:(h + 1) * 96], ["qDb%d" % p], [psY])
            for h in range(4):
                self.tr(p7[0:96, 4 + h, :], kDb[:, h * 96:(h + 1) * 96], ["kDb%d" % p], [psY])
            self.cp("dve", sT[0:96, 9:17, c0:c0 + 128], p7[0:96], [psY], [sn], partial=True)

        def store_block(bi):
            g0, nt, s = blocks[bi]
            sl = bi % 2
            sT = stgT[sl].rearrange("p (b t) -> p b t", t=512)
            sn = "stgT%d" % sl
            t0 = g0 * 128
            n = nt * 128
            L = "L%d" % l

            def st_(dst, b0, nb, rows, name, q):
                self.ld(q, dst[:, 0:rows, t0:t0 + n].rearrange("b p t -> p b t"), sT[0:rows, b0:b0 + nb, 0:n], [sn], [name])
            st_(self.QAT, 0, 2, 128, "QAT", "sp")
            st_(self.KAT, 2, 2, 128, "KAT", "sp")
            st_(self.QBT, 4, 2, 128, "QBT", "sp")
            st_(self.KBT, 6, 1, 128, "KBT", "sp")
            st_(self.OT[4:6], 7, 2, 128, "OTC", "sp")
            st_(self.QDT, 9, 4, 96, "QDT", "sp")
            st_(self.KDT, 13, 4, 96, "KDT", "sp")
            v3 = vst[sl][:, 0:nt * 1280].rearrange("p (t c) -> p t c", c=1280)
            self.ld("sp", self.VV[t0:t0 + n, :].rearrange("(t p) c -> p t c", p=128), v3, ["vst%d" % sl], ["VV"])

        nti = len(tiles)
        for dflag in self.debug:
            if dflag.startswith("ptiles="):
                nti = int(dflag[7:])

        def weave(lists):
            lists = [list(x) for x in lists]
            out = []
            while any(lists):
                for x in lists:
                    if x:
                        out.append(x.pop(0))
            return out

        def sprinkle(main, extra):
            if not extra:
                return list(main)
            out = []
            n, k = len(main), len(extra)
            j = 0
            for i, o in enumerate(main):
                out.append(o)
                while j < k and (j + 1) * n <= (i + 1) * k:
                    out.append(extra[j])
                    j += 1
            out.extend(extra[j:])
            return out

        for ti in range(min(4, nti)):
            loadx(ti)
        stageA(0)
        if nti > 1:
            stageA(1)
        for t0 in range(0, nti, 2):
            recs = []
            for ti in (t0, t0 + 1):
                if ti < nti:
                    P.begin_record()
                    stageB(ti)
                    recs.append(P.end_record())
            P.begin_record()
            for ti in (t0 + 2, t0 + 3):
                if ti < nti:
                    stageA(ti)
                if ti + 2 < nti:
                    loadx(ti + 2)
            recA = P.end_record()
            P.replay(sprinkle(weave(recs), recA))
            bi, g0, nt, s, tt = tiles[min(t0 + 1, nti - 1)]
            if tt == nt - 1 and "pnostore" not in self.debug:
                store_block(bi)
        P.barrier()
        self.release(m)

    def att_cfg(self, mixer):
        if mixer == "A":
            return dict(QT=self.QAT, KT=self.KAT, nqb=2, nkb=2, nh=4, vh0=0, ob0=0, rows=128, scale=32 ** -0.5)
        if mixer == "B":
            return dict(QT=self.QBT, KT=self.KBT, nqb=2, nkb=1, nh=2, vh0=4, ob0=2, rows=128, scale=64 ** -0.5)
        return dict(QT=self.QDT, KT=self.KDT, nqb=4, nkb=4, nh=4, vh0=6, ob0=6, rows=96, scale=96 ** -0.5)

    def att_prepare(self, mixer):
        c = self.att_cfg(mixer)
        nkb, nh, rows, vh0, KT = c["nkb"], c["nh"], c["rows"], c["vh0"], c["KT"]
        kt = self.alloc(nkb * T, BF16)
        kt3 = kt.rearrange("p (b t) -> p b t", t=T)
        vv = self.alloc(NT * nh * 128, BF16)
        vv4 = vv.rearrange("p (t h c) -> p t h c", h=nh, c=128)
        for b in range(nkb):
            self.ld("sp" if b % 2 == 0 else "act", kt3[0:rows, b, :], KT[b, 0:rows, :], ["KT"], ["kt" + mixer], partial=True)
        for q4 in range(0, NT, 9):
            n = min(9, NT - q4)
            self.ld("act" if (q4 // 9) % 2 == 0 else "sp", vv4[:, q4:q4 + n, :, :],
                    self.VV[q4 * 128:(q4 + n) * 128, vh0 * 128:(vh0 + nh) * 128].rearrange("(t p) (h c) -> p t h c", p=128, c=128),
                    ["VV"], ["vv" + mixer], partial=True)
        return kt3, vv4

    def att_shared(self):
        sh = {}
        sh["qt"] = [self.alloc(4096, BF16) for _ in range(2)]
        sh["pt"] = [self.alloc(512, BF16) for _ in range(4)]
        sh["ostg"] = [self.alloc(2 * 512, BF16) for _ in range(2)]
        sh["rl"] = [self.alloc(512) for _ in range(2)]
        for nm in ("d1", "d2", "dff", "sq", "lnv"):
            sh[nm] = self.alloc(512)
        sh["lamt"] = self.alloc(128)
        sh["lams"] = self.alloc(8)
        sh["gsub"] = self.alloc(2)
        return sh

    def phase_att_all(self, l, do_ctx):
        P = self.P
        m = self.mark()
        self.P.phase = 'attA%d' % l
        preps = {}
        for mixer in ("A", "B", "D"):
            preps[mixer] = self.att_prepare(mixer)
        sh = self.att_shared()
        for mixer in ("A", "B", "D"):
            self.phase_att(l, mixer, do_ctx, preps[mixer], sh)
        P.barrier()
        self.release(m)

    def phase_att(self, l, mixer, do_ctx, prep, sh):
        self.P.phase = 'att%s%d' % (mixer, l)
        P = self.P
        I = self.inp
        lam_init = 0.8 - 0.6 * math.exp(-0.3 * l)
        c = self.att_cfg(mixer)
        QT, KT, nqb, nkb, nh, vh0, ob0, rows, scale = (c["QT"], c["KT"], c["nqb"], c["nkb"], c["nh"], c["vh0"], c["ob0"],
                                                       c["rows"], c["scale"])
        if mixer == "A":
            maps = [(mm_ // 4, mm_ // 4, 32 * (mm_ % 4), 32, mm_ // 2) for mm_ in range(8)]
        elif mixer == "B":
            maps = [(j, 0, 64 * i, 64, i) for j in range(2) for i in range(2)]
            true_head = [0, 2, 1, 3]
        else:
            maps = [(h, h, 0, 96, h) for h in range(4)]
        kt3, vv4 = prep
        ktn = "kt%s" % mixer
        vvn = "vv%s" % mixer
        nmask = {"A": 4, "B": 2}.get(mixer, 1)
        qt = [q[:, 0:nqb * nmask * 512] for q in sh["qt"]]
        if nmask > 1:
            for sl in range(2):
                P.op("pool", lambda e, sl=sl: e.memset(qt[sl], 0.0), writes=["qt%d" % sl])
        NPT = 4
        pt, ostg, rl = sh["pt"], sh["ostg"], sh["rl"]
        d1, d2, dff, sq, lnv, lamt, lams, gsub = (sh["d1"], sh["d2"], sh["dff"], sh["sq"], sh["lnv"], sh["lamt"], sh["lams"],
                                                  sh["gsub"])
        if mixer == "A":
            self.ld("sp", lamt, I["lam_vecs"][l:l + 1].rearrange("o a d -> o (a d)").partition_broadcast(128), [], ["lamt"])
            lt3 = lamt.rearrange("p (a d) -> p a d", d=32)
            self.tt("dve", d1[:, 0:32], lt3[:, 0, :], lt3[:, 1, :], ALU.mult, ["lamt"], ["d1"])
            self.tt("dve", d1[:, 32:64], lt3[:, 2, :], lt3[:, 3, :], ALU.mult, ["lamt"], ["d1"], partial=True)
            P.op("dve", lambda e: e.tensor_reduce(out=lams[:, 0:2], in_=d1[:, 0:64].rearrange("p (a d) -> p a d", d=32),
                                                  op=ALU.add, axis=AX.X), reads=["d1"], writes=["lams"])
            self.act(lams[:, 2:4], lams[:, 0:2], AF.Exp, ["lams"], ["lams"])
            self.tt("dve", lams[:, 4:5], lams[:, 3:4], lams[:, 2:3], ALU.subtract, ["lams"], ["lams"])
            self.ts("dve", lams[:, 5:6], lams[:, 4:5], -lam_init, ALU.add, ["lams"], ["lams"])
            self.ld("sp", gsub[0:64, 0:1], I["g_subln"][l].rearrange("(d o) -> d o", o=1), [], ["gsub"])
            self.ts("dve", gsub[0:64, 1:2], gsub[0:64, 0:1], 1.0 - lam_init, ALU.mult, ["gsub"], ["gsub"])
        neg_lam = lams[0:64, 5:6]
        gcol = gsub[0:64, 1:2]

        chunks = []
        if do_ctx:
            chunks.append((0, 256, 0, 2))
        for i in range(8):
            chunks.append((256 + 512 * i, 512, 0, NT))

        def load_q(ci):
            t0, nq, k0, k1 = chunks[ci]
            sl = ci % 2
            q3 = qt[sl].rearrange("p (b t) -> p b t", t=512)
            if nmask == 1:
                self.ld("sp", q3[0:rows, :, 0:nq], QT[:, 0:rows, t0:t0 + nq].rearrange("b p t -> p b t"), ["QT"], ["qt%d" % sl])
            else:
                rw = 128 // nmask
                k = 0
                for b in range(nqb):
                    for mi in range(nmask):
                        self.ld("sp", q3[mi * rw:(mi + 1) * rw, b * nmask + mi, 0:nq],
                                QT[b, mi * rw:(mi + 1) * rw, t0:t0 + nq], ["QT"], ["qt%d" % sl], partial=True)
                        k += 1

        LA = 2
        SB = 4
        EPD = 24 if mixer == "A" else 8
        pending = []
        state = {"step": 0, "accset": 0, "ep": 0}

        def do_chunk(ci):
            t0, nq, k0, k1 = chunks[ci]
            sl = ci % 2
            q3 = qt[sl].rearrange("p (b t) -> p b t", t=512)
            qn = "qt%d" % sl
            o3 = ostg[sl].rearrange("p (b t) -> p b t", t=512)
            on = "ostg%d" % sl
            steps = []
            for mi, mp in enumerate(maps):
                for k in range(k0, k1):
                    steps.append((mi, k))
            ns = len(steps)
            nk = k1 - k0
            def accbank(mi):
                if mixer == "A":
                    return 4 + 2 * ((mi // 2) % 2) + (mi % 2)
                return 4 + (mi % 4)

            def qk(si):
                mi, k = steps[si]
                qb, kb, r0, K, vh = maps[mi]
                g = state["step"] + si
                bk = g % SB
                if nmask == 1:
                    self.mm(self.bank(bk)[:, 0:nq], kt3[r0:r0 + K, kb, k * 128:(k + 1) * 128], q3[r0:r0 + K, qb, 0:nq],
                            True, True, [ktn, qn], ["psS%d" % bk])
                else:
                    qslot = qb * nmask + r0 // K
                    self.mm(self.bank(bk)[:, 0:nq], kt3[:, kb, k * 128:(k + 1) * 128], q3[:, qslot, 0:nq],
                            True, True, [ktn, qn], ["psS%d" % bk])

            def ex_pv(si):
                mi, k = steps[si]
                qb, kb, r0, K, vh = maps[mi]
                g = state["step"] + si
                bk = g % SB
                pi = g % NPT
                self.act(pt[pi][:, 0:nq], self.bank(bk)[:, 0:nq], AF.Exp, ["psS%d" % bk], ["pt%d" % pi], scale=scale)
                ab = accbank(mi)
                if k == k0:
                    last = -1
                    for pi_, pe_ in enumerate(pending):
                        if pe_[2] == ab:
                            last = pi_
                    for _ in range(last + 1):
                        P.replay(pending.pop(0)[1])
                self.mm(self.bank(ab)[:, 0:nq], vv4[:, k, vh, :], pt[pi][:, 0:nq], k == k0, k == k1 - 1,
                        [vvn, "pt%d" % pi], ["psA%d" % ab])
                gstep = state["step"] + si
                if k == k1 - 1:
                    P.begin_record()
                    epilogue(mi)
                    rec = P.end_record()
                    tail = []
                    if mixer == "A" and mi % 2 == 1:
                        tail = rec[-3:]
                        rec = rec[:-3]
                    P.begin_record()
                    if si == ns - 1:
                        self.ld("sp", self.OT[ob0:ob0 + 2, :, t0:t0 + nq].rearrange("b p t -> p b t"), o3[:, :, 0:nq], [on], ["OTw"])
                    tail = tail + P.end_record()
                    pending.append((gstep + EPD, rec, ab))
                    if tail:
                        pending.append((gstep + EPD + 20, tail, ab))
                while pending and pending[0][0] <= gstep:
                    P.replay(pending.pop(0)[1])

            def epilogue(mi):
                ab = accbank(mi)
                an = "psA%d" % ab
                acc = self.bank(ab)
                ri = state["ep"] % 2
                state["ep"] += 1
                rln = "rl%d" % ri
                P.op("dve", lambda e: e.reciprocal(out=rl[ri][0:64, 0:nq], in_=acc[64:128, 0:nq]), reads=[an], writes=[rln])
                if mixer == "A":
                    h = mi // 2
                    dst = d1 if mi % 2 == 0 else d2
                    dn = "d1" if mi % 2 == 0 else "d2"
                    self.tt("dve", dst[0:64, 0:nq], acc[0:64, 0:nq], rl[ri][0:64, 0:nq], ALU.mult, [an, rln], [dn])
                    if mi % 2 == 1:
                        self.stt(dff[0:64, 0:nq], d2[0:64, 0:nq], neg_lam, d1[0:64, 0:nq], ALU.mult, ALU.add,
                                 ["d1", "d2", "lams"], ["dff"])
                        self.tt("pool", sq[0:64, 0:nq], dff[0:64, 0:nq], dff[0:64, 0:nq], ALU.mult, ["dff"], ["sq"])
                        bk = 0
                        g = state["step"]
                        self.mm(self.bank(3)[0:64, 0:nq], self.ones_f[0:64, 0:64], sq[0:64, 0:nq], True, True,
                                ["sq", "consts"], ["psS3"])
                        self.act(lnv[0:64, 0:nq], self.bank(3)[0:64, 0:nq], AF.Ln, ["psS3", "consts"], ["lnv"],
                                 scale=1.0 / 64, bias=self.epsb[0:64, 0:1])
                        self.act(lnv[0:64, 0:nq], lnv[0:64, 0:nq], AF.Exp, ["lnv"], ["lnv"], scale=-0.5)
                        orow = 64 * (h % 2)
                        self.stt(o3[orow:orow + 64, h // 2, 0:nq], dff[0:64, 0:nq], gcol, lnv[0:64, 0:nq], ALU.mult, ALU.mult,
                                 ["dff", "gsub", "lnv"], [on], partial=True)
                else:
                    h = true_head[mi] if mixer == "B" else mi
                    orow = 64 * (h % 2)
                    self.tt("dve", o3[orow:orow + 64, h // 2, 0:nq], acc[0:64, 0:nq], rl[ri][0:64, 0:nq], ALU.mult,
                            [an, rln], [on], partial=True)

            for si in range(min(LA, ns)):
                qk(si)
            for si in range(ns):
                if si + LA < ns:
                    qk(si + LA)
                ex_pv(si)
            state["step"] += ns

        if mixer == "A":
            SB = 3
        load_q(0)
        for ci in range(len(chunks)):
            if ci + 1 < len(chunks):
                load_q(ci + 1)
            do_chunk(ci)
        while pending:
            P.replay(pending.pop(0)[1])

    def phase_out(self, l, tiles0, ntiles, stream_of):
        self.P.phase = 'out%d' % l
        P = self.P
        I = self.inp
        m = self.mark()
        wo = self.alloc(8 * D, BF16)
        wo3 = wo.rearrange("p (c n) -> p c n", n=D)
        Gp = self.alloc(D)
        xin = [self.alloc(2 * D) for _ in range(2)]
        ot = [self.alloc(8 * 256, BF16) for _ in range(2)]
        tmp = [self.alloc(D) for _ in range(2)]
        junk = self.alloc(D, BF16)
        st = [self.alloc(8) for _ in range(2)]
        stg = [self.alloc(1024) for _ in range(6)]
        self.stg_i = 0
        wv = I["w_out"][l].rearrange("(c p) n -> p c n", p=128)
        for c in range(8):
            self.load_cast(stg, wo3[:, c, :], wv[:, c, :], "wo")
        blocks = [(tiles0 + 2 * i) for i in range(ntiles // 2)]
        cur = [None]

        def load(i):
            g0 = blocks[i]
            sl = i % 2
            x3 = xin[sl].rearrange("p (t d) -> p t d", d=D)
            self.ld("sp", x3, self.XS[g0 * 128:(g0 + 2) * 128, :].rearrange("(t p) d -> p t d", p=128), ["XSr%d" % g0], ["oxin%d" % sl])
            o3 = ot[sl].rearrange("p (b t) -> p b t", t=256)
            self.ld("sp", o3, self.OT[:, :, g0 * 128:(g0 + 2) * 128].rearrange("b p t -> p b t"), ["OTall"], ["ot%d" % sl])

        def compute(i):
            g0 = blocks[i]
            sl = i % 2
            s = stream_of(g0)
            if cur[0] != s:
                cur[0] = s
                self.ld("sp", Gp, self.DV[l][s, 1, 2:3, :].partition_broadcast(128), ["DV%d" % l], ["Gp"])
            o3 = ot[sl].rearrange("p (b t) -> p b t", t=256)
            xn = "oxin%d" % sl
            stn = "ost%d" % sl
            for t in range(2):
                py = self.bank(2 * ((2 * i + t) % 4), n=2)
                pn = "psY%d" % ((2 * i + t) % 4)
                for half in range(2):
                    for b in range(8):
                        self.mm(py[:, half * 512:(half + 1) * 512], o3[:, b, t * 128:(t + 1) * 128],
                                wo3[:, b, half * 512:(half + 1) * 512], b == 0, b == 7, ["ot%d" % sl, "wo"], [pn])
                self.act(junk, py, AF.Square, [pn], [stn, "junk"], partial=True, accum_out=st[sl][:, t:t + 1])
            self.ts("dve", st[sl][:, 2:4], st[sl][:, 0:2], 1.0 / D, ALU.mult, [stn], [stn], s2=EPS, op1=ALU.add)
            self.tt("pool", st[sl][:, 2:4], st[sl][:, 2:4], self.neghalf[:, 0:2], ALU.pow, [stn, "consts"], [stn])
            for t in range(2):
                py = self.bank(2 * ((2 * i + t) % 4), n=2)
                pn = "psY%d" % ((2 * i + t) % 4)
                tn = "otmp%d" % t
                self.stt(tmp[t], py, st[sl][:, 2 + t:3 + t], Gp, ALU.mult, ALU.mult, [pn, stn, "Gp"], [tn])
                self.tt("pool", xin[sl][:, t * D:(t + 1) * D], tmp[t], xin[sl][:, t * D:(t + 1) * D], ALU.add,
                        [tn, xn], [xn], partial=True)
            x3 = xin[sl].rearrange("p (t d) -> p t d", d=D)
            self.ld("sp", self.XS[g0 * 128:(g0 + 2) * 128, :].rearrange("(t p) d -> p t d", p=128), x3, [xn], ["XSw%d" % g0])

        n = len(blocks)
        load(0)
        for i in range(n):
            if i + 1 < n:
                load(i + 1)
            compute(i)
        P.barrier()
        self.release(m)


def build(debug=(), stop_after=None):
    kb = KB(debug)
    nc = kb.nc
    inp = {}

    def din(name, shape, dt=F32):
        inp[name] = nc.dram_tensor(name, list(shape), dt, kind="ExternalInput").ap()

    din("x", [SEQ, D])
    din("ctx", [CTX, D])
    din("cvec", [128, 8, 2])
    din("w_ada", [DEPTH, D, 9 * D])
    din("b_ada", [DEPTH, 9 * D])
    din("g_pre", [DEPTH, 3, D])
    din("g_post", [DEPTH, 3, D])
    din("w_ffn1_in", [DEPTH, D, 2 * DFF])
    din("w_ffn1_out", [DEPTH, DFF, D])
    din("w_ffn2_in", [DEPTH, D, 2 * DFF])
    din("w_ffn2_out", [DEPTH, DFF, D])
    din("w_in_p", [DEPTH, D, INW])
    din("w_out", [DEPTH, D, D])
    din("w_uq", [DEPTH, 256, 384])
    din("w_ukv", [DEPTH, 128, 512])
    din("wspT", [DEPTH, 128, 4, 128])
    din("bspT", [DEPTH, 128, 4])
    din("gqk", [DEPTH, 384])
    din("ln_g", [DEPTH, 256])
    din("ln_b", [DEPTH, 256])
    din("g_q_a", [DEPTH, 256])
    din("g_kv_a", [DEPTH, 128])
    din("lam_vecs", [DEPTH, 4, 32])
    din("g_subln", [DEPTH, 64])
    din("rope", [SEQ, 192])
    kb.inp = inp
    out = nc.dram_tensor("out", [SEQ, D], F32, kind="ExternalOutput").ap()
    kb.XS = kb.dram("XS", [T, D], F32)
    kb.DV = [kb.dram("DV%d" % l, [2, 3, 3, D], F32) for l in range(DEPTH)]
    kb.QAT = kb.dram("QAT", [2, 128, T], BF16)
    kb.KAT = kb.dram("KAT", [2, 128, T], BF16)
    kb.QBT = kb.dram("QBT", [2, 128, T], BF16)
    kb.KBT = kb.dram("KBT", [1, 128, T], BF16)
    kb.QDT = kb.dram("QDT", [4, 128, T], BF16)
    kb.KDT = kb.dram("KDT", [4, 128, T], BF16)
    kb.OT = kb.dram("OT", [8, 128, T], BF16)
    kb.VV = kb.dram("VV", [T, 1280], BF16)

    def done():
        kb.P.emit()
        return kb

    kb.setup_consts()
    for l in range(DEPTH):
        kb.phase_mod(l)
    if stop_after == "mod":
        return done()

    def src0(b):
        return inp["ctx"] if b == 0 else inp["x"][(b - 1) * 256:b * 256, :]

    def xs(b):
        return kb.XS[b * 256:(b + 1) * 256, :]

    def outb(b):
        return out[(b - 1) * 256:b * 256, :]

    allblocks = [(0, 1)] + [(b, 0) for b in range(1, 17)]
    latblocks = [(b, 0) for b in range(1, 17)]
    for l in range(DEPTH):
        need_ctx = (l < DEPTH - 1)
        if "noffn" in kb.debug and l == 0:
            kb.ld("sp", kb.XS[0:CTX, :].rearrange("(p a) d -> p (a d)", p=128), inp["ctx"].rearrange("(p a) d -> p (a d)", p=128), [], ["XScopy"], partial=True)
            for i in range(8):
                kb.ld("sp", kb.XS[CTX + 512 * i:CTX + 512 * (i + 1), :].rearrange("(p a) d -> p (a d)", p=128),
                      inp["x"][512 * i:512 * (i + 1), :].rearrange("(p a) d -> p (a d)", p=128), [], ["XScopy"], partial=True)
            kb.P.barrier()
        else:
            kb.phase_ffn(l, 0, src0 if l == 0 else xs, xs, allblocks)
        if stop_after == "ffn%d" % l:
            return done()
        kb.phase_proj(l, need_ctx)
        if stop_after == "proj%d" % l:
            return done()
        kb.phase_att_all(l, need_ctx)
        if stop_after == "attD%d" % l:
            return done()
        if need_ctx:
            kb.phase_out(l, 0, NT, lambda g: 1 if g < 2 else 0)
        else:
            kb.phase_out(l, 2, NT - 2, lambda g: 0)
        if stop_after == "out%d" % l:
            return done()
        if need_ctx:
            kb.phase_ffn(l, 2, xs, xs, allblocks)
        else:
            kb.phase_ffn(l, 2, xs, outb, latblocks)
        if stop_after == "ffnb%d" % l:
            return done()
    return done()


def _rope_table():
    def tabs(rot_dim):
        rows = np.repeat(np.arange(SEQ // 64, dtype=np.float32), 64)
        cols = np.tile(np.arange(64, dtype=np.float32), SEQ // 64)
        axis_dim = rot_dim // 2
        inv_freq = (np.float32(10000.0) ** (-np.arange(0, axis_dim, 2, dtype=np.float32) / np.float32(axis_dim))).astype(np.float32)
        ar = rows[:, None] * inv_freq[None, :]
        ac = cols[:, None] * inv_freq[None, :]
        cr, sr, cc, sc = np.cos(ar), np.sin(ar), np.cos(ac), np.sin(ac)
        C = np.concatenate([cr, cr, cc, cc], axis=1)
        S = np.concatenate([-sr, sr, -sc, sc], axis=1)
        return C.astype(np.float32), S.astype(np.float32)
    C32, S32 = tabs(32)
    C64, S64 = tabs(64)
    return np.ascontiguousarray(np.concatenate([C32, S32, C64, S64], axis=1).astype(np.float32))


def _w_in_perm():
    o = np.arange(INW)
    qb = 768 + np.concatenate([np.arange(0, 64), np.arange(128, 192), np.arange(64, 128), np.arange(192, 256)])
    return np.concatenate([o[0:256], o[256:512], qb, o[1024:1152], o[1152:1280], o[512:768], o[1280:2208]])


def host_inputs(inputs):
    f = lambda a: np.ascontiguousarray(np.asarray(a, dtype=np.float32))
    shared = {}
    for k in ["w_ada", "b_ada", "g_pre", "g_post", "w_ffn1_in", "w_ffn1_out", "w_ffn2_in", "w_ffn2_out",
              "w_out", "w_uq", "w_ukv", "ln_g", "ln_b", "g_q_a", "g_kv_a", "lam_vecs", "g_subln"]:
        shared[k] = f(inputs[k])
    shared["w_in_p"] = f(np.asarray(inputs["w_in"])[:, :, _w_in_perm()])
    shared["wspT"] = f(np.asarray(inputs["w_spatial"]).transpose(0, 3, 1, 2))
    shared["bspT"] = f(np.asarray(inputs["b_spatial"]).transpose(0, 2, 1))
    gq = np.asarray(inputs["g_qnorm"])
    gk = np.asarray(inputs["g_knorm"])
    shared["gqk"] = f(np.concatenate([gq, gq, gq, gq, gk, gk], axis=1))
    shared["rope"] = _rope_table()
    x = np.asarray(inputs["x"])
    ctx = np.asarray(inputs["ctx"])
    c = np.asarray(inputs["c"])
    cc = np.asarray(inputs["c_ctx"])
    maps = []
    for b in range(x.shape[0]):
        mm = dict(shared)
        mm["x"] = f(x[b])
        mm["ctx"] = f(ctx[b])
        mm["cvec"] = f(np.stack([c[b].reshape(8, 128).T, cc.reshape(8, 128).T], axis=-1))
        maps.append(mm)
    return maps


_CACHE = {}


def kernel(**inputs):
    if "kb" not in _CACHE:
        _CACHE["kb"] = build()
    kb = _CACHE["kb"]
    maps = host_inputs(inputs)
    res = run_bass_kernel_spmd(kb.nc, maps, core_ids=list(range(len(maps))))
    return np.stack([np.asarray(r["out"], dtype=np.float32) for r in res.results], axis=0)
```

```python
import math
from contextlib import ExitStack
import numpy as np
import ml_dtypes
import concourse.bass as bass
import concourse.mybir as mybir
from concourse.bass_utils import run_bass_kernel_spmd

AF = mybir.ActivationFunctionType
ALU = mybir.AluOpType
AX = mybir.AxisListType
F32 = mybir.dt.float32
BF16 = mybir.dt.bfloat16

N_DMA_SEMS = 32

D = 1024
SEQ = 4096
CTX = 256
T = SEQ + CTX
NT = T // 128
DFF = 2816
NFF = DFF // 128
DEPTH = 2
EPS = 1e-6
INW = 2208


class Buf:
    __slots__ = ("name", "writers", "readers")

    def __init__(self, name):
        self.name = name
        self.writers = []
        self.readers = []


class Instr:
    __slots__ = ("eng", "fn", "is_dma", "deps", "sig", "idx", "needs_inc", "phase")


class Prog:
    ENGS = ("pe", "act", "dve", "pool", "sp")

    def __init__(self, nc):
        self.nc = nc
        self.instrs = []
        self.bufs = {}
        self.bstart = 0
        self.phase = 'init'
        self.profile = False
        self._rec = None

    def buf(self, name):
        b = self.bufs.get(name)
        if b is None:
            b = Buf(name)
            self.bufs[name] = b
        return b

    def _new(self, eng, fn, is_dma, deps):
        ins = Instr()
        ins.eng = eng
        ins.fn = fn
        ins.is_dma = is_dma
        ins.idx = len(self.instrs)
        ins.needs_inc = False
        ins.sig = None
        ins.phase = self.phase
        ins.deps = sorted(deps)
        for d in ins.deps:
            self.instrs[d].needs_inc = True
        self.instrs.append(ins)
        return ins

    def begin_record(self):
        self._rec = []

    def end_record(self):
        r = self._rec
        self._rec = None
        return r

    def replay(self, ops):
        for o in ops:
            self.op(*o)

    def op(self, eng, fn, reads=(), writes=(), partial=False, is_dma=False):
        if self._rec is not None:
            self._rec.append((eng, fn, tuple(reads), tuple(writes), partial, is_dma))
            return None
        reads = [self.buf(x) for x in reads if x is not None]
        writes = [self.buf(x) for x in writes if x is not None]
        instrs = self.instrs
        deps = set()
        for r in reads:
            deps.update(r.writers)
        for wb in writes:
            deps.update(wb.writers)
            deps.update(wb.readers)
        if eng == "pe" and not is_dma:
            deps = {d for d in deps if instrs[d].eng != "pe" or instrs[d].is_dma}
        ins = self._new(eng, fn, is_dma, deps)
        for r in reads:
            r.readers.append(ins.idx)
        for wb in writes:
            if partial:
                wb.writers.append(ins.idx)
            else:
                wb.writers = [ins.idx]
                wb.readers = []
        return ins

    def dma(self, q, fn, reads=(), writes=(), partial=False):
        return self.op(q, fn, reads, writes, partial, is_dma=True)

    def barrier(self):
        last = {}
        dmas = []
        for ins in self.instrs[self.bstart:]:
            if ins.fn is None:
                continue
            if ins.is_dma:
                dmas.append(ins.idx)
            else:
                last[ins.eng] = ins.idx
        for e in self.ENGS:
            deps = list(last.values()) + dmas
            self._new(e, None, False, deps)
        self.bstart = len(self.instrs)
        for b in self.bufs.values():
            b.writers = []
            b.readers = []

    def emit(self):
        nc = self.nc
        with ExitStack() as es:
            esem = {e: es.enter_context(nc.semaphore("s_" + e)) for e in self.ENGS}
            dsems = [es.enter_context(nc.semaphore("d%d" % i)) for i in range(N_DMA_SEMS)]
            dval = [0] * N_DMA_SEMS
            ecnt = {e: 0 for e in self.ENGS}
            k = 0
            pre = {}
            for ins in self.instrs:
                if ins.is_dma:
                    si = k % N_DMA_SEMS
                    k += 1
                    pre[ins.idx] = (dsems[si], dval[si])
                    dval[si] += 16
                    ins.sig = (dsems[si], dval[si])
                elif ins.needs_inc:
                    ecnt[ins.eng] += 1
                    ins.sig = (esem[ins.eng], ecnt[ins.eng])
            streams = {e: [i for i in self.instrs if i.eng == e] for e in self.ENGS}
            instrs = self.instrs

            def run(e, name):
                waited = {}

                def wait(sem, val):
                    if val <= 0:
                        return
                    key = id(sem)
                    if waited.get(key, 0) >= val:
                        return
                    e.wait_ge(sem, val)
                    waited[key] = val

                cur = [None, None]
                for ins in streams[name]:
                    if self.profile and ins.fn is not None and ins.phase != cur[0]:
                        if cur[0] is not None:
                            nc.leave_named_scope(cur[0], cur[1], False)
                        cur[0] = ins.phase
                        cur[1], _ = nc.enter_named_scope(ins.phase, False)
                    need = {}
                    for d in ins.deps:
                        s, v = instrs[d].sig
                        k2 = id(s)
                        if k2 not in need or need[k2][1] < v:
                            need[k2] = (s, v)
                    for s, v in need.values():
                        wait(s, v)
                    if ins.fn is None:
                        continue
                    if ins.is_dma:
                        s, v = pre[ins.idx]
                        wait(s, v)
                    r = ins.fn(e)
                    if ins.is_dma:
                        r.then_inc(ins.sig[0], 16)
                    elif ins.needs_inc:
                        r.then_inc(ins.sig[0], 1)
                if cur[0] is not None:
                    nc.leave_named_scope(cur[0], cur[1], False)

            with nc.Block() as block:
                @block.sync
                def _(e):
                    run(e, "sp")

                @block.tensor
                def _(e):
                    run(e, "pe")

                @block.scalar
                def _(e):
                    run(e, "act")

                @block.vector
                def _(e):
                    run(e, "dve")

                @block.gpsimd
                def _(e):
                    run(e, "pool")


ARENA_WORDS = 53100


class KB:
    def __init__(self, debug=()):
        self.debug = set(debug)
        nc = bass.Bass("TRN2", target_bir_lowering=False)
        self.nc = nc
        self.P = Prog(nc)
        self.P.profile = 'profile' in self.debug
        self.es = ExitStack()
        self.arena = self.es.enter_context(nc.sbuf_tensor("arena", [128, ARENA_WORDS], F32))
        self.psum = self.es.enter_context(nc.psum_tensor("psum", [128, 4096], F32))
        self.off = 0
        self.uid = 0

    def alloc(self, nfree, dt=F32):
        nb = nfree * (2 if dt == BF16 else 4)
        nw = (nb + 3) // 4
        nw = (nw + 7) // 8 * 8
        assert self.off + nw <= ARENA_WORDS, ("SBUF arena overflow", self.off, nw)
        a = self.arena[:, self.off:self.off + nw]
        self.off += nw
        if dt == BF16:
            a = a.bitcast(BF16)[:, 0:nfree]
        else:
            a = a[:, 0:nfree]
        return a

    def mark(self):
        return self.off

    def release(self, m):
        self.off = m

    def bank(self, i, dt=F32, n=1):
        a = self.psum[:, 512 * i:512 * (i + n)]
        if dt == BF16:
            a = a.bitcast(BF16)
        return a

    def dram(self, name, shape, dt, kind="Internal"):
        if name in self.debug:
            kind = "ExternalOutput"
        return self.nc.dram_tensor(name, list(shape), dt, kind=kind).ap()

    def name(self, p):
        self.uid += 1
        return "%s_%d" % (p, self.uid)

    def mm(self, out, lhsT, rhs, start, stop, reads, writes, **kw):
        self.P.op("pe", lambda e: e.matmul(out, lhsT=lhsT, rhs=rhs, start=start, stop=stop, **kw),
                  reads=reads, writes=writes, partial=True)

    def tr(self, out, in_, reads, writes):
        ident = self.ident
        self.P.op("pe", lambda e: e.transpose(out=out, in_=in_, identity=ident),
                  reads=list(reads) + ["ident"], writes=writes, partial=True)

    def act(self, out, in_, func, reads, writes, partial=False, eng="act", **kw):
        self.P.op(eng, lambda e: e.activation(out=out, in_=in_, func=func, **kw),
                  reads=reads, writes=writes, partial=partial)

    def tt(self, eng, out, in0, in1, op, reads, writes, partial=False):
        self.P.op(eng, lambda e: e.tensor_tensor(out=out, in0=in0, in1=in1, op=op),
                  reads=reads, writes=writes, partial=partial)

    def ts(self, eng, out, in0, s1, op0, reads, writes, s2=None, op1=None, partial=False):
        if op1 is None:
            self.P.op(eng, lambda e: e.tensor_scalar(out=out, in0=in0, scalar1=s1, scalar2=None, op0=op0),
                      reads=reads, writes=writes, partial=partial)
        else:
            self.P.op(eng, lambda e: e.tensor_scalar(out=out, in0=in0, scalar1=s1, scalar2=s2, op0=op0, op1=op1),
                      reads=reads, writes=writes, partial=partial)

    def stt(self, out, in0, scalar, in1, op0, op1, reads, writes, partial=False):
        self.P.op("dve", lambda e: e.scalar_tensor_tensor(out=out, in0=in0, scalar=scalar, in1=in1, op0=op0, op1=op1),
                  reads=reads, writes=writes, partial=partial)

    def cp(self, eng, out, in_, reads, writes, partial=False):
        if eng == "act":
            self.P.op("act", lambda e: e.copy(out=out, in_=in_), reads=reads, writes=writes, partial=partial)
        else:
            self.P.op(eng, lambda e: e.tensor_copy(out=out, in_=in_), reads=reads, writes=writes, partial=partial)

    def ld(self, q, out, in_, reads, writes, partial=False, **kw):
        self.P.dma(q, lambda e: e.dma_start(out=out, in_=in_, **kw), reads=reads, writes=writes, partial=partial)

    def rstd(self, out, ss, n, reads_name, scale):
        self.ts("dve", out, ss, scale, ALU.mult, reads=[reads_name], writes=[reads_name], s2=EPS, op1=ALU.add)
        nh = self.neghalf[:, 0:n]
        self.tt("pool", out, out, nh, ALU.pow, reads=[reads_name, "consts"], writes=[reads_name])

    def load_cast(self, stg, dst, src, wname, n3=None, q=None):
        i = self.stg_i
        self.stg_i += 1
        sl = i % len(stg)
        n = dst.shape[-1] if n3 is None else dst.shape[-1] * (dst.shape[-2] if len(dst.shape) > 2 else 1)
        sv = stg[sl][:, 0:n]
        sn = "stg%d_%d" % (id(stg) % 1000, sl)
        if n3 is not None:
            sv3 = sv.rearrange("p (a n) -> p a n", n=n3)
            self.ld(q or ("sp" if i % 2 == 0 else "act"), sv3, src, [], [sn])
            d3 = dst if len(dst.shape) > 2 else dst.rearrange("p (a n) -> p a n", n=n3)
            self.cp(("dve", "pool", "dve")[i % 3], d3, sv3, [sn], [wname], partial=True)
        else:
            self.ld(q or ("sp" if i % 2 == 0 else "act"), sv, src, [], [sn])
            self.cp(("dve", "pool", "dve")[i % 3], dst, sv, [sn], [wname], partial=True)

    def setup_consts(self):
        P = self.P
        self.ident = self.alloc(128, BF16)
        self.neghalf = self.alloc(16, F32)
        self.ones_f = self.alloc(64, F32)
        self.ones_b = self.alloc(128, BF16)
        self.epsb = self.alloc(1, F32)
        m = self.mark()
        idf = self.alloc(128, F32)
        ident, nh, of, ob, epsb = self.ident, self.neghalf, self.ones_f, self.ones_b, self.epsb
        P.op("pool", lambda e: e.memset(idf, 0.0), writes=["idf"])
        P.op("pool", lambda e: e.affine_select(out=idf, in_=idf, pattern=[[-1, 128]], compare_op=ALU.not_equal,
                                               fill=1.0, base=0, channel_multiplier=1), reads=["idf"], writes=["idf"])
        P.op("dve", lambda e: e.tensor_copy(out=ident, in_=idf), reads=["idf"], writes=["ident"])
        P.op("pool", lambda e: e.memset(nh, -0.5), writes=["consts"])
        P.op("pool", lambda e: e.memset(of, 1.0), writes=["consts"], partial=True)
        P.op("pool", lambda e: e.memset(ob, 1.0), writes=["consts"], partial=True)
        P.op("pool", lambda e: e.memset(epsb, EPS), writes=["consts"], partial=True)
        P.barrier()
        self.release(m)

    def phase_mod(self, l):
        self.P.phase = 'mod%d' % l
        P = self.P
        I = self.inp
        m = self.mark()
        cv = self.alloc(16, F32)
        sc = self.alloc(16, F32)
        cv3 = cv.rearrange("p (c s) -> p c s", s=2)
        sc3 = sc.rearrange("p (c s) -> p c s", s=2)
        modsb = self.alloc(9 * D, F32)
        bada = self.alloc(9 * D, F32)
        gpre = self.alloc(3 * D, F32)
        gpost = self.alloc(3 * D, F32)
        dv = self.alloc(9 * D, F32)
        NS = 3
        wsl = [self.alloc(8 * 512, F32) for _ in range(NS)]
        self.ld("sp", cv3, I["cvec"], [], ["cv"])
        self.ld("sp", bada[0:2, :], I["b_ada"][l:l + 1, :].partition_broadcast(2), [], ["bada"])
        self.ld("sp", gpre[0:2, :], I["g_pre"][l:l + 1].rearrange("o j d -> o (j d)").partition_broadcast(2), [], ["gpre"])
        self.ld("sp", gpost[0:2, :], I["g_post"][l:l + 1].rearrange("o j d -> o (j d)").partition_broadcast(2), [], ["gpost"])
        self.act(sc, cv, AF.Tanh, ["cv"], ["sc"], scale=0.5)
        self.ts("dve", sc, sc, 1.0, ALU.add, ["sc"], ["sc"], s2=0.5, op1=ALU.mult)
        self.tt("dve", sc, sc, cv, ALU.mult, ["sc", "cv"], ["sc"])
        wada = I["w_ada"][l].rearrange("(c p) n -> p c n", p=128)
        for n in range(18):
            s = n % NS
            w3 = wsl[s].rearrange("p (c n) -> p c n", n=512)
            self.ld("sp" if n % 2 == 0 else "act", w3, wada[:, :, n * 512:(n + 1) * 512], [], ["wsl%d" % s])
            pb = "psm%d" % (n % 2)
            po = self.bank(n % 2)[0:2, :]
            for c in range(8):
                self.mm(po, sc3[:, c, :], w3[:, c, :], c == 0, c == 7, ["sc", "wsl%d" % s], [pb])
            self.tt("dve", modsb[0:2, n * 512:(n + 1) * 512], po, bada[0:2, n * 512:(n + 1) * 512], ALU.add,
                    [pb, "bada"], ["modsb"], partial=True)
        for j in range(3):
            wj = 1.0 if j == 1 else 0.5
            sh = modsb[0:2, (3 * j) * D:(3 * j + 1) * D]
            scl = modsb[0:2, (3 * j + 1) * D:(3 * j + 2) * D]
            gt = modsb[0:2, (3 * j + 2) * D:(3 * j + 3) * D]
            self.stt(dv[0:2, (3 * j) * D:(3 * j + 1) * D], scl, 1.0, gpre[0:2, j * D:(j + 1) * D], ALU.add, ALU.mult,
                     ["modsb", "gpre"], ["dv"], partial=True)
            self.cp("dve", dv[0:2, (3 * j + 1) * D:(3 * j + 2) * D], sh, ["modsb"], ["dv"], partial=True)
            self.stt(dv[0:2, (3 * j + 2) * D:(3 * j + 3) * D], gt, wj, gpost[0:2, j * D:(j + 1) * D], ALU.mult, ALU.mult,
                     ["modsb", "gpost"], ["dv"], partial=True)
        self.ld("sp", self.DV[l].rearrange("s j k d -> s (j k d)"), dv[0:2, :], ["dv"], ["DV%d" % l])
        P.barrier()
        self.release(m)

    def phase_ffn(self, l, j, src, dst, blocks):
        self.P.phase = 'ffn%d_%d' % (l, j)
        P = self.P
        I = self.inp
        m = self.mark()
        w_in = I["w_ffn1_in" if j == 0 else "w_ffn2_in"][l]
        w_out = I["w_ffn1_out" if j == 0 else "w_ffn2_out"][l]
        w1 = self.alloc(8 * 2 * DFF, BF16)
        w2 = self.alloc(NFF * D, BF16)
        w13 = w1.rearrange("p (c n) -> p c n", n=2 * DFF)
        w23 = w2.rearrange("p (c n) -> p c n", n=D)
        QW = 704
        stg = [self.alloc(QW) for _ in range(4)]
        self.stg_i = 0
        w_in_v = w_in.rearrange("(c p) n -> p c n", p=128)
        wq_ops = {}
        for q in range(4):
            P.begin_record()
            for half in range(2):
                c0 = half * DFF + q * QW
                for c in range(8):
                    self.load_cast(stg, w13[:, c, c0:c0 + QW], w_in_v[:, c, c0:c0 + QW], "w1_%d_%d" % (half, q), q="sp")
            wq_ops[q] = P.end_record()
        w_out_v = w_out.rearrange("(c p) n -> p c n", p=128)
        P.begin_record()
        for c0 in range(NFF):
            for h in range(2):
                self.load_cast(stg, w23[:, c0, h * 512:(h + 1) * 512], w_out_v[:, c0, h * 512:(h + 1) * 512], "w2", q="sp")
        w2_ops = P.end_record()
        G = self.alloc(D)
        S = self.alloc(D)
        Gp = self.alloc(D)
        xin = [self.alloc(2 * D) for _ in range(2)]
        hb = self.alloc(2 * D, BF16)
        hT = self.alloc(8 * 256, BF16)
        hT3 = hT.rearrange("p (c t) -> p c t", t=256)
        actT = self.alloc(NFF * 256, BF16)
        actT3 = actT.rearrange("p (c t) -> p c t", t=256)
        tmp = [self.alloc(D) for _ in range(2)]
        junk = self.alloc(D, BF16)
        sg = [self.alloc(256) for _ in range(3)]
        st = [self.alloc(8) for _ in range(2)]
        cur_stream = [None, None]

        def load_mod(s):
            if cur_stream[0] == s:
                return
            cur_stream[0] = s
            dvl = self.DV[l]
            self.ld("sp", G, dvl[s, j, 0:1, :].partition_broadcast(128), ["DV%d" % l], ["G"])
            self.ld("sp", S, dvl[s, j, 1:2, :].partition_broadcast(128), ["DV%d" % l], ["S"])

        def load_gp(s):
            if cur_stream[1] == s:
                return
            cur_stream[1] = s
            dvl = self.DV[l]
            self.ld("sp", Gp, dvl[s, j, 2:3, :].partition_broadcast(128), ["DV%d" % l], ["Gp"])

        def load(i):
            b, s = blocks[i]
            sl = i % 2
            x3 = xin[sl].rearrange("p (t d) -> p t d", d=D)
            self.ld("sp", x3, src(b).rearrange("(t p) d -> p t d", p=128), [self.srcname(b)], ["xin%d" % sl])

        def prenorm(i):
            b, s = blocks[i]
            sl = i % 2
            load_mod(s)
            xn = "xin%d" % sl
            stn = "st%d" % sl
            for t in range(2):
                self.act(junk, xin[sl][:, t * D:(t + 1) * D], AF.Square, [xn], [stn, "junk"], partial=True,
                         accum_out=st[sl][:, t:t + 1])
            self.ts("dve", st[sl][:, 2:4], st[sl][:, 0:2], 1.0 / D, ALU.mult, [stn], [stn], s2=EPS, op1=ALU.add)
            self.tt("pool", st[sl][:, 2:4], st[sl][:, 2:4], self.neghalf[:, 0:2], ALU.pow, [stn, "consts"], [stn])
            for t in range(2):
                tn = "tmp%d" % t
                self.stt(tmp[t], xin[sl][:, t * D:(t + 1) * D], st[sl][:, 2 + t:3 + t], G, ALU.mult, ALU.mult,
                         [xn, stn, "G"], [tn])
                self.tt("dve", hb[:, t * D:(t + 1) * D], tmp[t], S, ALU.add, [tn, "S"], ["hb%d" % t])

        def transposes(i):
            for t in range(2):
                pT = self.bank(0, BF16)
                pT3 = pT.rearrange("p (c t) -> p c t", t=128)
                for c in range(8):
                    self.tr(pT3[:, c, :], hb[:, t * D + c * 128:t * D + (c + 1) * 128], ["hb%d" % t], ["psT"])
                self.cp("act", hT3[:, :, t * 128:(t + 1) * 128], pT3, ["psT"], ["hT"], partial=True)

        def mm1(i, side=(), wside=None):
            side = list(side)
            per = max(1, (len(side) + 13) // 14)
            for f in range(NFF):
                if f >= 3 and side:
                    P.replay(side[:per])
                    side = side[per:]
                if wside is not None:
                    wqo, w2o = wside
                    need = sorted({(f * 128) // 704, (f * 128 + 127) // 704})
                    for q in need:
                        if wqo.get(q):
                            P.replay(wqo[q])
                            wqo[q] = []
                    nxt = need[-1] + 1
                    if wqo.get(nxt):
                        P.replay(wqo[nxt][:8])
                        wqo[nxt] = wqo[nxt][8:]
                    if w2o:
                        P.replay(w2o[:4])
                        del w2o[:4]
                bk = 1 + (f % 3)
                pn = "psM%d" % bk
                pm = self.bank(bk)
                wq = sorted({(f * 128) // 704, (f * 128 + 127) // 704})
                for half in range(2):
                    col = half * DFF + f * 128
                    wn = ["w1_%d_%d" % (half, q) for q in wq]
                    for c in range(8):
                        self.mm(pm[:, half * 256:(half + 1) * 256], w13[:, c, col:col + 128], hT3[:, c, :],
                                c == 0, c == 7, wn + ["hT"], [pn])
                sgi = f % 3
                self.act(sg[sgi], pm[:, 0:256], AF.Silu, [pn], ["sg%d" % sgi])
                self.tt("dve", actT3[:, f, :], pm[:, 256:512], sg[sgi], ALU.mult, [pn, "sg%d" % sgi], ["actT"], partial=True)
            P.replay(side)
            if wside is not None:
                for q in sorted(wside[0]):
                    P.replay(wside[0][q])
                    wside[0][q] = []
                P.replay(wside[1])
                del wside[1][:]

        def mm2_post(i):
            b, s = blocks[i]
            sl = i % 2
            xn = "xin%d" % sl
            stn = "st%d" % sl
            load_gp(s)
            for t in range(2):
                py = self.bank(4 + 2 * t, n=2)
                pn = "psY%d" % t
                for half in range(2):
                    for f in range(NFF):
                        self.mm(py[:, half * 512:(half + 1) * 512], actT3[:, f, t * 128:(t + 1) * 128],
                                w23[:, f, half * 512:(half + 1) * 512], f == 0, f == NFF - 1, ["actT", "w2"], [pn])
                self.act(junk, py, AF.Square, [pn], [stn, "junk"], partial=True, accum_out=st[sl][:, 4 + t:5 + t])
            self.ts("dve", st[sl][:, 6:8], st[sl][:, 4:6], 1.0 / D, ALU.mult, [stn], [stn], s2=EPS, op1=ALU.add)
            self.tt("pool", st[sl][:, 6:8], st[sl][:, 6:8], self.neghalf[:, 0:2], ALU.pow, [stn, "consts"], [stn])
            for t in range(2):
                py = self.bank(4 + 2 * t, n=2)
                pn = "psY%d" % t
                tn = "tmp%d" % t
                self.stt(tmp[t], py, st[sl][:, 6 + t:7 + t], Gp, ALU.mult, ALU.mult, [pn, stn, "Gp"], [tn])
                self.tt("dve", xin[sl][:, t * D:(t + 1) * D], tmp[t], xin[sl][:, t * D:(t + 1) * D], ALU.add,
                        [tn, xn], [xn], partial=True)
            x3 = xin[sl].rearrange("p (t d) -> p t d", d=D)
            self.ld("sp", dst(b).rearrange("(t p) d -> p t d", p=128), x3, [xn], [self.dstname(b)])

        n = len(blocks)
        load(0)
        prenorm(0)
        P.replay(wq_ops[0])
        wq_ops[0] = []
        transposes(0)
        for i in range(n):
            if i + 1 < n:
                load(i + 1)
            side = ()
            if i + 1 < n:
                P.begin_record()
                prenorm(i + 1)
                side = P.end_record()
            mm1(i, side, (wq_ops, w2_ops) if i == 0 else None)
            if i + 1 < n:
                transposes(i + 1)
            mm2_post(i)
        P.barrier()
        self.release(m)

    def srcname(self, b):
        return "XS%d" % b

    def dstname(self, b):
        return "XS%d" % b


    def bc(self, ap, G):
        shp = list(ap.shape)
        return ap[:, None].to_broadcast([shp[0], G] + shp[1:])

    def rope(self, sfx, eng_a, xv, outv, Ct, St, G, Dh, t1, t2, rn, wn, partial=True):
        w = Dh // 4
        t13 = t1[:, 0:G * Dh].rearrange("p (g d) -> p g d", d=Dh)
        t25 = t2[:, 0:G * Dh].rearrange("p (g a h w) -> p g a h w", a=2, h=2, w=w)
        t23 = t2[:, 0:G * Dh].rearrange("p (g d) -> p g d", d=Dh)
        xv5 = xv.rearrange("p g (a h w) -> p g a h w", a=2, h=2, w=w)
        S4 = St.rearrange("p (a h w) -> p a h w", a=2, h=2, w=w)
        self.tt(eng_a, t13, xv, self.bc(Ct, G), ALU.mult, rn + ["rope"], ["rt1_%d" % sfx])
        self.tt("dve", t25[:, :, :, 0, :], xv5[:, :, :, 1, :], self.bc(S4[:, :, 0, :], G), ALU.mult, rn + ["rope"], ["rt2_%d" % sfx], partial=True)
        self.tt("dve", t25[:, :, :, 1, :], xv5[:, :, :, 0, :], self.bc(S4[:, :, 1, :], G), ALU.mult, rn + ["rope"], ["rt2_%d" % sfx], partial=True)
        self.tt("dve", outv, t13, t23, ALU.add, ["rt1_%d" % sfx, "rt2_%d" % sfx], wn, partial=partial)

    def phase_proj(self, l, need_q_ctx):
        self.P.phase = 'proj%d' % l
        P = self.P
        I = self.inp
        m = self.mark()
        NW = INW
        w_in = self.alloc(8 * NW, BF16)
        w_in3 = w_in.rearrange("p (c n) -> p c n", n=NW)
        w_uq = self.alloc(2 * 384, BF16)
        w_uq3 = w_uq.rearrange("p (c n) -> p c n", n=384)
        w_ukv = self.alloc(512, BF16)
        wsp = self.alloc(4 * 128, BF16)
        wsp3 = wsp.rearrange("p (g n) -> p g n", n=128)
        G = self.alloc(D)
        S = self.alloc(D)
        gqk = self.alloc(384)
        lng = self.alloc(256)
        lnb = self.alloc(256)
        gqa = self.alloc(256)
        gkva = self.alloc(128)
        bsp = self.alloc(4)
        xt = [self.alloc(D) for _ in range(4)]
        rts = [self.alloc(192) for _ in range(6)]
        hb = self.alloc(D, BF16)
        hT = self.alloc(8 * 128, BF16)
        hT3 = hT.rearrange("p (c t) -> p c t", t=128)
        pj = [self.alloc(NW) for _ in range(4)]
        t1_s = [self.alloc(512) for _ in range(2)]
        t2_s = [self.alloc(512) for _ in range(2)]
        ta = self.alloc(512)
        tb = self.alloc(512)
        tB_s = [self.alloc(384) for _ in range(2)]
        tC_s = [self.alloc(512) for _ in range(2)]
        tD_s = [self.alloc(256) for _ in range(2)]
        guv_s = [self.alloc(512) for _ in range(2)]
        junk = self.alloc(D, BF16)
        qkA_s = [self.alloc(512, BF16) for _ in range(2)]
        qkB_s = [self.alloc(384, BF16) for _ in range(2)]
        vnb_s = [self.alloc(256, BF16) for _ in range(2)]
        cl_s = [self.alloc(256, BF16) for _ in range(2)]
        cn_s = [self.alloc(384, BF16) for _ in range(2)]
        cT_s = [self.alloc(3 * 128, BF16) for _ in range(2)]
        qDb_s = [self.alloc(4 * 96, BF16) for _ in range(2)]
        kDb_s = [self.alloc(4 * 96, BF16) for _ in range(2)]
        krr_s = [self.alloc(32) for _ in range(2)]
        qDf_s = [self.alloc(128) for _ in range(2)]
        st = [self.alloc(32) for _ in range(4)]
        mstg = self.mark()
        stg = [self.alloc(2208) for _ in range(6)]
        self.stg_i = 0
        wv = I["w_in_p"][l].rearrange("(c p) n -> p c n", p=128)
        for c in range(8):
            self.load_cast(stg, w_in3[:, c, :], wv[:, c, :], "w_in")
        for c in range(2):
            self.load_cast(stg, w_uq3[:, c, :], I["w_uq"][l][c * 128:(c + 1) * 128, :], "w_uq")
        self.load_cast(stg, w_ukv, I["w_ukv"][l], "w_ukv")
        self.load_cast(stg, wsp, I["wspT"][l], "wsp", n3=128)
        self.ld("sp", gqk, I["gqk"][l:l + 1, :].partition_broadcast(128), [], ["par"], partial=True)
        self.ld("sp", lng, I["ln_g"][l:l + 1, :].partition_broadcast(128), [], ["par"], partial=True)
        self.ld("sp", lnb, I["ln_b"][l:l + 1, :].partition_broadcast(128), [], ["par"], partial=True)
        self.ld("sp", gqa, I["g_q_a"][l:l + 1, :].partition_broadcast(128), [], ["par"], partial=True)
        self.ld("sp", gkva, I["g_kv_a"][l:l + 1, :].partition_broadcast(128), [], ["par"], partial=True)
        self.ld("sp", bsp, I["bspT"][l], [], ["par"], partial=True)
        P.barrier()
        self.release(mstg)
        stgT = [self.alloc(17 * 512, BF16) for _ in range(2)]
        vst = [self.alloc(4 * 1280, BF16) for _ in range(2)]
        for sl in range(2):
            v4 = vst[sl].rearrange("p (t h c) -> p t h c", h=10, c=128)
            P.op("pool", lambda e, v4=v4: e.memset(v4[:, :, :, 64:128], 1.0), writes=["vst%d" % sl], partial=True)

        blocks = [(0, 2, 1)] + [(2 + 4 * i, 4, 0) for i in range(8)]
        tiles = []
        for bi, (g0, nt, s) in enumerate(blocks):
            for tt in range(nt):
                tiles.append((bi, g0, nt, s, tt))
        cur = [None]

        def loadx(ti):
            bi, g0, nt, s, tt = tiles[ti]
            gt = g0 + tt
            k4 = ti % 4
            self.ld("sp", xt[k4], self.XS[gt * 128:(gt + 1) * 128, :], ["XSall"], ["pxt%d" % k4])
            if s == 0:
                lt = gt - 2
                self.ld("sp", rts[ti % 6], I["rope"][lt * 128:(lt + 1) * 128, :], [], ["rt%d" % (ti % 6)])

        def stageA(ti):
            bi, g0, nt, s, tt = tiles[ti]
            ps = ti % 4
            if cur[0] != s:
                cur[0] = s
                dvl = self.DV[l]
                self.ld("sp", G, dvl[s, 1, 0:1, :].partition_broadcast(128), ["DV%d" % l], ["G"])
                self.ld("sp", S, dvl[s, 1, 1:2, :].partition_broadcast(128), ["DV%d" % l], ["S"])
            xn = "pxt%d" % ps
            stn = "pst%d" % ps
            x = xt[ps]
            sv = st[ps]
            self.act(junk, x, AF.Square, [xn], [stn, "junk"], partial=True, accum_out=sv[:, 0:1])
            self.ts("dve", sv[:, 1:2], sv[:, 0:1], 1.0 / D, ALU.mult, [stn], [stn], s2=EPS, op1=ALU.add)
            self.tt("pool", sv[:, 1:2], sv[:, 1:2], self.neghalf[:, 0:1], ALU.pow, [stn, "consts"], [stn])
            self.stt(ta, x[:, 0:512], sv[:, 1:2], G[:, 0:512], ALU.mult, ALU.mult, [xn, stn, "G"], ["ta"])
            self.tt("pool", hb[:, 0:512], ta, S[:, 0:512], ALU.add, ["ta", "S"], ["hb"], partial=True)
            self.stt(tb, x[:, 512:1024], sv[:, 1:2], G[:, 512:1024], ALU.mult, ALU.mult, [xn, stn, "G"], ["tb"])
            self.tt("pool", hb[:, 512:1024], tb, S[:, 512:1024], ALU.add, ["tb", "S"], ["hb"], partial=True)
            pT = self.bank(0, BF16)
            pT3 = pT.rearrange("p (c t) -> p c t", t=128)
            for c in range(8):
                self.tr(pT3[:, c, :], hb[:, c * 128:(c + 1) * 128], ["hb"], ["psT"])
            self.cp("act", hT3, pT3, ["psT"], ["hT"])
            chunks = [(0, 512), (512, 512), (1024, 512), (1536, 512), (2048, 160)]
            for n, (c0, cw) in enumerate(chunks):
                bk = 1 + (n % 2)
                pn = "psP%d" % bk
                po = self.bank(bk)[:, 0:cw]
                for c in range(8):
                    self.mm(po, hT3[:, c, :], w_in3[:, c, c0:c0 + cw], c == 0, c == 7, ["hT", "w_in"], [pn])
                self.cp("act" if n % 2 == 0 else "dve", pj[ps][:, c0:c0 + cw], po, [pn], ["pj%d" % ps], partial=True)

        def stageB(ti):
            bi, g0, nt, s, tt = tiles[ti]
            sl = bi % 2
            ps = ti % 4
            p = ti % 2
            X = 3 + 2 * p
            Y = 4 + 2 * p
            psX = "ps%d" % X
            psY = "ps%d" % Y
            t1, t2, tB, tC, tD, guv, krr, qDf = t1_s[p], t2_s[p], tB_s[p], tC_s[p], tD_s[p], guv_s[p], krr_s[p], qDf_s[p]
            qkA, qkB, vnb, cl, cn, cT, qDb, kDb = qkA_s[p], qkB_s[p], vnb_s[p], cl_s[p], cn_s[p], cT_s[p], qDb_s[p], kDb_s[p]
            cT3 = cT.rearrange("p (c t) -> p c t", t=128)
            lat = (s == 0)
            latA = lat and 'norA' not in self.debug
            latB = lat and 'norB' not in self.debug
            latDq = lat and 'norDq' not in self.debug
            latDk = lat and 'norDk' not in self.debug
            pjn = "pj%d" % ps
            stn = "pst%d" % ps
            x = pj[ps]
            sv = st[ps]
            rt = rts[ti % 6]
            C32, S32, C64, S64 = rt[:, 0:32], rt[:, 32:64], rt[:, 64:128], rt[:, 128:192]
            rn = ["rt%d" % (ti % 6)]
            vn = "vst%d" % sl
            v4 = vst[sl].rearrange("p (t h c) -> p t h c", h=10, c=128)
            xa = x[:, 0:512].rearrange("p (g d) -> p g d", d=32)
            if latA:
                self.rope(p, "pool", xa, qkA.rearrange("p (g d) -> p g d", d=32), C32, S32, 16, 32, t1, t2, [pjn] + rn, ["qkA%d" % p], partial=False)
            else:
                self.cp("pool", qkA, x[:, 0:512], [pjn], ["qkA%d" % p])
            xb = x[:, 512:896]
            xb3 = xb.rearrange("p (g d) -> p g d", d=64)
            self.act(tB, xb, AF.Square, [pjn], ["tB%d" % p])
            P.op("dve", lambda e: e.tensor_reduce(out=sv[:, 8:14], in_=tB.rearrange("p (g d) -> p g d", d=64),
                                                  op=ALU.add, axis=AX.X), reads=["tB%d" % p], writes=[stn])
            self.ts("dve", sv[:, 8:14], sv[:, 8:14], 1.0 / 64, ALU.mult, [stn], [stn], s2=EPS, op1=ALU.add)
            self.tt("pool", sv[:, 8:14], sv[:, 8:14], self.neghalf[:, 0:6], ALU.pow, [stn, "consts"], [stn])
            tB3 = tB.rearrange("p (g d) -> p g d", d=64)
            self.tt("dve", tB3, xb3, sv[:, 8:14, None].to_broadcast([128, 6, 64]), ALU.mult, [pjn, stn], ["tB%d" % p])
            self.tt("pool", tB, tB, gqk, ALU.mult, ["tB%d" % p, "par"], ["tB%d" % p])
            if latB:
                self.rope(p, "pool", tB3, qkB.rearrange("p (g d) -> p g d", d=64), C64, S64, 6, 64, t1, t2, ["tB%d" % p] + rn, ["qkB%d" % p], partial=False)
            else:
                self.cp("pool", qkB, tB, ["tB%d" % p], ["qkB%d" % p])
            self.cp("act", v4[:, tt, 0:4, 0:64], x[:, 1024:1280].rearrange("p (h c) -> p h c", c=64), [pjn], [vn], partial=True)
            self.cp("act", v4[:, tt, 4:6, 0:64], x[:, 896:1024].rearrange("p (h c) -> p h c", c=64), [pjn], [vn], partial=True)
            xc = x[:, 1280:1792]
            self.act(tC, xc, AF.Square, [pjn], ["tC%d" % p])
            self.ts("dve", tC, tC, 0.044715, ALU.mult, ["tC%d" % p], ["tC%d" % p], s2=1.0, op1=ALU.add)
            self.tt("pool", tC, tC, xc, ALU.mult, ["tC%d" % p, pjn], ["tC%d" % p])
            self.act(tC, tC, AF.Tanh, ["tC%d" % p], ["tC%d" % p], scale=0.7978845608028654)
            self.ts("dve", tC, tC, 1.0, ALU.add, ["tC%d" % p], ["tC%d" % p], s2=0.5, op1=ALU.mult)
            self.tt("dve", guv, tC, xc, ALU.mult, ["tC%d" % p, pjn], ["guv%d" % p])
            gv = guv[:, 256:512]
            self.act(junk[:, 0:256], gv, AF.Identity, ["guv%d" % p], [stn, "junk"], partial=True, accum_out=sv[:, 16:17])
            self.act(junk[:, 256:512], gv, AF.Square, ["guv%d" % p], [stn, "junk"], partial=True, accum_out=sv[:, 17:18])
            self.ts("dve", sv[:, 18:19], sv[:, 16:17], 1.0 / 256, ALU.mult, [stn], [stn])
            self.tt("dve", sv[:, 19:20], sv[:, 18:19], sv[:, 18:19], ALU.mult, [stn], [stn])
            self.stt(sv[:, 20:21], sv[:, 17:18], 1.0 / 256, sv[:, 19:20], ALU.mult, ALU.subtract, [stn], [stn])
            self.ts("dve", sv[:, 20:21], sv[:, 20:21], EPS, ALU.add, [stn], [stn])
            self.tt("pool", sv[:, 20:21], sv[:, 20:21], self.neghalf[:, 0:1], ALU.pow, [stn, "consts"], [stn])
            self.ts("dve", tD, gv, sv[:, 18:19], ALU.subtract, ["guv%d" % p, stn], ["tD%d" % p], s2=sv[:, 20:21], op1=ALU.mult)
            self.tt("pool", tD, tD, lng, ALU.mult, ["tD%d" % p, "par"], ["tD%d" % p])
            self.tt("pool", vnb, tD, lnb, ALU.add, ["tD%d" % p, "par"], ["vnb%d" % p])
            pg = self.bank(X)[:, 256:512]
            for g in range(4):
                self.mm(pg[:, g * 64:(g + 1) * 64], wsp3[:, g, :], vnb[:, g * 64:(g + 1) * 64], True, True, ["wsp", "vnb%d" % p], [psX])
            for g in range(4):
                self.stt(cl[:, g * 64:(g + 1) * 64], pg[:, g * 64:(g + 1) * 64], bsp[:, g:g + 1], guv[:, g * 64:(g + 1) * 64],
                         ALU.add, ALU.mult, [psX, "par", "guv%d" % p], ["cl%d" % p], partial=True)
            self.act(junk[:, 512:768], x[:, 1792:2048], AF.Square, [pjn], [stn, "junk"], partial=True, accum_out=sv[:, 24:25])
            self.act(junk[:, 768:896], x[:, 2048:2176], AF.Square, [pjn], [stn, "junk"], partial=True, accum_out=sv[:, 25:26])
            self.ts("dve", sv[:, 26:27], sv[:, 24:25], 1.0 / 256, ALU.mult, [stn], [stn], s2=EPS, op1=ALU.add)
            self.ts("dve", sv[:, 27:28], sv[:, 25:26], 1.0 / 128, ALU.mult, [stn], [stn], s2=EPS, op1=ALU.add)
            self.tt("pool", sv[:, 26:28], sv[:, 26:28], self.neghalf[:, 0:2], ALU.pow, [stn, "consts"], [stn])
            self.stt(cn[:, 0:256], x[:, 1792:2048], sv[:, 26:27], gqa, ALU.mult, ALU.mult, [pjn, stn, "par"], ["cn%d" % p], partial=True)
            self.stt(cn[:, 256:384], x[:, 2048:2176], sv[:, 27:28], gkva, ALU.mult, ALU.mult, [pjn, stn, "par"], ["cn%d" % p], partial=True)
            pc = self.bank(X, BF16)[:, 0:384]
            pc3 = pc.rearrange("p (c t) -> p c t", t=128)
            for c in range(3):
                self.tr(pc3[:, c, :], cn[:, c * 128:(c + 1) * 128], ["cn%d" % p], [psX])
            self.cp("act", cT3, pc3, [psX], ["cT%d" % p])
            pq = self.bank(Y)[:, 0:384]
            for c in range(2):
                self.mm(pq, cT3[:, c, :], w_uq3[:, c, :], c == 0, c == 1, ["cT%d" % p, "w_uq"], [psY])
            pq3 = pq.rearrange("p (h d) -> p h d", d=96)
            qD3 = qDb.rearrange("p (h d) -> p h d", d=96)
            kD3 = kDb.rearrange("p (h d) -> p h d", d=96)
            qDf3 = qDf.rearrange("p (h d) -> p h d", d=32)
            self.cp("act", qD3[:, :, 0:64], pq3[:, :, 0:64], [psY], ["qDb%d" % p], partial=True)
            if latDq:
                self.cp("act", qDf3, pq3[:, :, 64:96], [psY], ["qDf%d" % p])
            else:
                self.cp("act", qD3[:, :, 64:96], pq3[:, :, 64:96], [psY], ["qDb%d" % p], partial=True)
            pkv = self.bank(Y)
            self.mm(pkv, cT3[:, 2, :], w_ukv, True, True, ["cT%d" % p, "w_ukv"], [psY])
            pkv3 = pkv.rearrange("p (h d) -> p h d", d=128)
            self.cp("act", kD3[:, :, 0:64], pkv3[:, :, 0:64], [psY], ["kDb%d" % p], partial=True)
            self.cp("act", v4[:, tt, 6:10, 0:64], pkv3[:, :, 64:128], [psY], [vn], partial=True)
            if latDq:
                self.rope(p, "pool", qDf3, qD3[:, :, 64:96], C32, S32, 4, 32, t1, t2, ["qDf%d" % p] + rn, ["qDb%d" % p])
            if latDk:
                self.rope(p, "pool", x[:, 2176:2208].rearrange("p (g d) -> p g d", d=32), krr.rearrange("p (g d) -> p g d", d=32),
                          C32, S32, 1, 32, t1, t2, [pjn] + rn, ["krr%d" % p], partial=False)
                self.cp("pool", kD3[:, :, 64:96], self.bc(krr, 4), ["krr%d" % p], ["kDb%d" % p], partial=True)
            else:
                self.cp("pool", kD3[:, :, 64:96], self.bc(x[:, 2176:2208], 4), [pjn], ["kDb%d" % p], partial=True)
            sT = stgT[sl].rearrange("p (b t) -> p b t", t=512)
            sn = "stgT%d" % sl
            c0 = tt * 128
            p7 = self.bank(Y, BF16).rearrange("p (c t) -> p c t", t=128)
            p5 = self.bank(X, BF16)[:, 384:512]
            for k in range(4):
                self.tr(p7[:, k, :], qkA[:, k * 128:(k + 1) * 128], ["qkA%d" % p], [psY])
            for k in range(3):
                self.tr(p7[:, 4 + k, :], qkB[:, k * 128:(k + 1) * 128], ["qkB%d" % p], [psY])
            self.tr(p7[:, 7, :], cl[:, 0:128], ["cl%d" % p], [psY])
            self.tr(p5, cl[:, 128:256], ["cl%d" % p], [psX])
            self.cp("dve", sT[:, 0:8, c0:c0 + 128], p7, [psY], [sn], partial=True)
            self.cp("act", sT[:, 8, c0:c0 + 128], p5, [psX], [sn], partial=True)
            for h in range(4):
                self.tr(p7[0:96, h, :], qDb[:, h * 96:(h + 1) * 96], ["qDb%d" % p], [psY])
            for h in range(4):
                self.tr(p7[0:96, 4 + h, :], kDb[:, h * 96:(h + 1) * 96], ["kDb%d" % p], [psY])
            self.cp("dve", sT[0:96, 9:17, c0:c0 + 128], p7[0:96], [psY], [sn], partial=True)

        def store_block(bi):
            g0, nt, s = blocks[bi]
            sl = bi % 2
            sT = stgT[sl].rearrange("p (b t) -> p b t", t=512)
            sn = "stgT%d" % sl
            t0 = g0 * 128
            n = nt * 128
            L = "L%d" % l

            def st_(dst, b0, nb, rows, name, q):
                self.ld(q, dst[:, 0:rows, t0:t0 + n].rearrange("b p t -> p b t"), sT[0:rows, b0:b0 + nb, 0:n], [sn], [name])
            st_(self.QAT, 0, 2, 128, "QAT", "sp")
            st_(self.KAT, 2, 2, 128, "KAT", "sp")
            st_(self.QBT, 4, 2, 128, "QBT", "sp")
            st_(self.KBT, 6, 1, 128, "KBT", "sp")
            st_(self.OT[4:6], 7, 2, 128, "OTC", "sp")
            st_(self.QDT, 9, 4, 96, "QDT", "sp")
            st_(self.KDT, 13, 4, 96, "KDT", "sp")
            v3 = vst[sl][:, 0:nt * 1280].rearrange("p (t c) -> p t c", c=1280)
            self.ld("sp", self.VV[t0:t0 + n, :].rearrange("(t p) c -> p t c", p=128), v3, ["vst%d" % sl], ["VV"])

        nti = len(tiles)
        for dflag in self.debug:
            if dflag.startswith("ptiles="):
                nti = int(dflag[7:])

        def weave(lists):
            lists = [list(x) for x in lists]
            out = []
            while any(lists):
                for x in lists:
                    if x:
                        out.append(x.pop(0))
            return out

        def sprinkle(main, extra):
            if not extra:
                return list(main)
            out = []
            n, k = len(main), len(extra)
            j = 0
            for i, o in enumerate(main):
                out.append(o)
                while j < k and (j + 1) * n <= (i + 1) * k:
                    out.append(extra[j])
                    j += 1
            out.extend(extra[j:])
            return out

        for ti in range(min(4, nti)):
            loadx(ti)
        stageA(0)
        if nti > 1:
            stageA(1)
        for t0 in range(0, nti, 2):
            recs = []
            for ti in (t0, t0 + 1):
                if ti < nti:
                    P.begin_record()
                    stageB(ti)
                    recs.append(P.end_record())
            P.begin_record()
            for ti in (t0 + 2, t0 + 3):
                if ti < nti:
                    stageA(ti)
                if ti + 2 < nti:
                    loadx(ti + 2)
            recA = P.end_record()
            P.replay(sprinkle(weave(recs), recA))
            bi, g0, nt, s, tt = tiles[min(t0 + 1, nti - 1)]
            if tt == nt - 1 and "pnostore" not in self.debug:
                store_block(bi)
        P.barrier()
        self.release(m)

    def att_cfg(self, mixer):
        if mixer == "A":
            return dict(QT=self.QAT, KT=self.KAT, nqb=2, nkb=2, nh=4, vh0=0, ob0=0, rows=128, scale=32 ** -0.5)
        if mixer == "B":
            return dict(QT=self.QBT, KT=self.KBT, nqb=2, nkb=1, nh=2, vh0=4, ob0=2, rows=128, scale=64 ** -0.5)
        return dict(QT=self.QDT, KT=self.KDT, nqb=4, nkb=4, nh=4, vh0=6, ob0=6, rows=96, scale=96 ** -0.5)

    def att_prepare(self, mixer):
        c = self.att_cfg(mixer)
        nkb, nh, rows, vh0, KT = c["nkb"], c["nh"], c["rows"], c["vh0"], c["KT"]
        kt = self.alloc(nkb * T, BF16)
        kt3 = kt.rearrange("p (b t) -> p b t", t=T)
        vv = self.alloc(NT * nh * 128, BF16)
        vv4 = vv.rearrange("p (t h c) -> p t h c", h=nh, c=128)
        for b in range(nkb):
            self.ld("sp" if b % 2 == 0 else "act", kt3[0:rows, b, :], KT[b, 0:rows, :], ["KT"], ["kt" + mixer], partial=True)
        for q4 in range(0, NT, 9):
            n = min(9, NT - q4)
            self.ld("act" if (q4 // 9) % 2 == 0 else "sp", vv4[:, q4:q4 + n, :, :],
                    self.VV[q4 * 128:(q4 + n) * 128, vh0 * 128:(vh0 + nh) * 128].rearrange("(t p) (h c) -> p t h c", p=128, c=128),
                    ["VV"], ["vv" + mixer], partial=True)
        return kt3, vv4

    def att_shared(self):
        sh = {}
        sh["qt"] = [self.alloc(4096, BF16) for _ in range(2)]
        sh["pt"] = [self.alloc(512, BF16) for _ in range(4)]
        sh["ostg"] = [self.alloc(2 * 512, BF16) for _ in range(2)]
        sh["rl"] = [self.alloc(512) for _ in range(2)]
        for nm in ("d1", "d2", "dff", "sq", "lnv"):
            sh[nm] = self.alloc(512)
        sh["lamt"] = self.alloc(128)
        sh["lams"] = self.alloc(8)
        sh["gsub"] = self.alloc(2)
        return sh

    def phase_att_all(self, l, do_ctx):
        P = self.P
        m = self.mark()
        self.P.phase = 'attA%d' % l
        preps = {}
        for mixer in ("A", "B", "D"):
            preps[mixer] = self.att_prepare(mixer)
        sh = self.att_shared()
        for mixer in ("A", "B", "D"):
            self.phase_att(l, mixer, do_ctx, preps[mixer], sh)
        P.barrier()
        self.release(m)

    def phase_att(self, l, mixer, do_ctx, prep, sh):
        self.P.phase = 'att%s%d' % (mixer, l)
        P = self.P
        I = self.inp
        lam_init = 0.8 - 0.6 * math.exp(-0.3 * l)
        c = self.att_cfg(mixer)
        QT, KT, nqb, nkb, nh, vh0, ob0, rows, scale = (c["QT"], c["KT"], c["nqb"], c["nkb"], c["nh"], c["vh0"], c["ob0"],
                                                       c["rows"], c["scale"])
        if mixer == "A":
            maps = [(mm_ // 4, mm_ // 4, 32 * (mm_ % 4), 32, mm_ // 2) for mm_ in range(8)]
        elif mixer == "B":
            maps = [(j, 0, 64 * i, 64, i) for j in range(2) for i in range(2)]
            true_head = [0, 2, 1, 3]
        else:
            maps = [(h, h, 0, 96, h) for h in range(4)]
        kt3, vv4 = prep
        ktn = "kt%s" % mixer
        vvn = "vv%s" % mixer
        nmask = {"A": 4, "B": 2}.get(mixer, 1)
        qt = [q[:, 0:nqb * nmask * 512] for q in sh["qt"]]
        if nmask > 1:
            for sl in range(2):
                P.op("pool", lambda e, sl=sl: e.memset(qt[sl], 0.0), writes=["qt%d" % sl])
        NPT = 4
        pt, ostg, rl = sh["pt"], sh["ostg"], sh["rl"]
        d1, d2, dff, sq, lnv, lamt, lams, gsub = (sh["d1"], sh["d2"], sh["dff"], sh["sq"], sh["lnv"], sh["lamt"], sh["lams"],
                                                  sh["gsub"])
        if mixer == "A":
            self.ld("sp", lamt, I["lam_vecs"][l:l + 1].rearrange("o a d -> o (a d)").partition_broadcast(128), [], ["lamt"])
            lt3 = lamt.rearrange("p (a d) -> p a d", d=32)
            self.tt("dve", d1[:, 0:32], lt3[:, 0, :], lt3[:, 1, :], ALU.mult, ["lamt"], ["d1"])
            self.tt("dve", d1[:, 32:64], lt3[:, 2, :], lt3[:, 3, :], ALU.mult, ["lamt"], ["d1"], partial=True)
            P.op("dve", lambda e: e.tensor_reduce(out=lams[:, 0:2], in_=d1[:, 0:64].rearrange("p (a d) -> p a d", d=32),
                                                  op=ALU.add, axis=AX.X), reads=["d1"], writes=["lams"])
            self.act(lams[:, 2:4], lams[:, 0:2], AF.Exp, ["lams"], ["lams"])
            self.tt("dve", lams[:, 4:5], lams[:, 3:4], lams[:, 2:3], ALU.subtract, ["lams"], ["lams"])
            self.ts("dve", lams[:, 5:6], lams[:, 4:5], -lam_init, ALU.add, ["lams"], ["lams"])
            self.ld("sp", gsub[0:64, 0:1], I["g_subln"][l].rearrange("(d o) -> d o", o=1), [], ["gsub"])
            self.ts("dve", gsub[0:64, 1:2], gsub[0:64, 0:1], 1.0 - lam_init, ALU.mult, ["gsub"], ["gsub"])
        neg_lam = lams[0:64, 5:6]
        gcol = gsub[0:64, 1:2]

        chunks = []
        if do_ctx:
            chunks.append((0, 256, 0, 2))
        for i in range(8):
            chunks.append((256 + 512 * i, 512, 0, NT))

        def load_q(ci):
            t0, nq, k0, k1 = chunks[ci]
            sl = ci % 2
            q3 = qt[sl].rearrange("p (b t) -> p b t", t=512)
            if nmask == 1:
                self.ld("sp", q3[0:rows, :, 0:nq], QT[:, 0:rows, t0:t0 + nq].rearrange("b p t -> p b t"), ["QT"], ["qt%d" % sl])
            else:
                rw = 128 // nmask
                k = 0
                for b in range(nqb):
                    for mi in range(nmask):
                        self.ld("sp", q3[mi * rw:(mi + 1) * rw, b * nmask + mi, 0:nq],
                                QT[b, mi * rw:(mi + 1) * rw, t0:t0 + nq], ["QT"], ["qt%d" % sl], partial=True)
                        k += 1

        LA = 2
        SB = 4
        EPD = 24 if mixer == "A" else 8
        pending = []
        state = {"step": 0, "accset": 0, "ep": 0}

        def do_chunk(ci):
            t0, nq, k0, k1 = chunks[ci]
            sl = ci % 2
            q3 = qt[sl].rearrange("p (b t) -> p b t", t=512)
            qn = "qt%d" % sl
            o3 = ostg[sl].rearrange("p (b t) -> p b t", t=512)
            on = "ostg%d" % sl
            steps = []
            for mi, mp in enumerate(maps):
                for k in range(k0, k1):
                    steps.append((mi, k))
            ns = len(steps)
            nk = k1 - k0
            def accbank(mi):
                if mixer == "A":
                    return 4 + 2 * ((mi // 2) % 2) + (mi % 2)
                return 4 + (mi % 4)

            def qk(si):
                mi, k = steps[si]
                qb, kb, r0, K, vh = maps[mi]
                g = state["step"] + si
                bk = g % SB
                if nmask == 1:
                    self.mm(self.bank(bk)[:, 0:nq], kt3[r0:r0 + K, kb, k * 128:(k + 1) * 128], q3[r0:r0 + K, qb, 0:nq],
                            True, True, [ktn, qn], ["psS%d" % bk])
                else:
                    qslot = qb * nmask + r0 // K
                    self.mm(self.bank(bk)[:, 0:nq], kt3[:, kb, k * 128:(k + 1) * 128], q3[:, qslot, 0:nq],
                            True, True, [ktn, qn], ["psS%d" % bk])

            def ex_pv(si):
                mi, k = steps[si]
                qb, kb, r0, K, vh = maps[mi]
                g = state["step"] + si
                bk = g % SB
                pi = g % NPT
                self.act(pt[pi][:, 0:nq], self.bank(bk)[:, 0:nq], AF.Exp, ["psS%d" % bk], ["pt%d" % pi], scale=scale)
                ab = accbank(mi)
                if k == k0:
                    last = -1
                    for pi_, pe_ in enumerate(pending):
                        if pe_[2] == ab:
                            last = pi_
                    for _ in range(last + 1):
                        P.replay(pending.pop(0)[1])
                self.mm(self.bank(ab)[:, 0:nq], vv4[:, k, vh, :], pt[pi][:, 0:nq], k == k0, k == k1 - 1,
                        [vvn, "pt%d" % pi], ["psA%d" % ab])
                gstep = state["step"] + si
                if k == k1 - 1:
                    P.begin_record()
                    epilogue(mi)
                    rec = P.end_record()
                    tail = []
                    if mixer == "A" and mi % 2 == 1:
                        tail = rec[-3:]
                        rec = rec[:-3]
                    P.begin_record()
                    if si == ns - 1:
                        self.ld("sp", self.OT[ob0:ob0 + 2, :, t0:t0 + nq].rearrange("b p t -> p b t"), o3[:, :, 0:nq], [on], ["OTw"])
                    tail = tail + P.end_record()
                    pending.append((gstep + EPD, rec, ab))
                    if tail:
                        pending.append((gstep + EPD + 20, tail, ab))
                while pending and pending[0][0] <= gstep:
                    P.replay(pending.pop(0)[1])

            def epilogue(mi):
                ab = accbank(mi)
                an = "psA%d" % ab
                acc = self.bank(ab)
                ri = state["ep"] % 2
                state["ep"] += 1
                rln = "rl%d" % ri
                P.op("dve", lambda e: e.reciprocal(out=rl[ri][0:64, 0:nq], in_=acc[64:128, 0:nq]), reads=[an], writes=[rln])
                if mixer == "A":
                    h = mi // 2
                    dst = d1 if mi % 2 == 0 else d2
                    dn = "d1" if mi % 2 == 0 else "d2"
                    self.tt("dve", dst[0:64, 0:nq], acc[0:64, 0:nq], rl[ri][0:64, 0:nq], ALU.mult, [an, rln], [dn])
                    if mi % 2 == 1:
                        self.stt(dff[0:64, 0:nq], d2[0:64, 0:nq], neg_lam, d1[0:64, 0:nq], ALU.mult, ALU.add,
                                 ["d1", "d2", "lams"], ["dff"])
                        self.tt("pool", sq[0:64, 0:nq], dff[0:64, 0:nq], dff[0:64, 0:nq], ALU.mult, ["dff"], ["sq"])
                        bk = 0
                        g = state["step"]
                        self.mm(self.bank(3)[0:64, 0:nq], self.ones_f[0:64, 0:64], sq[0:64, 0:nq], True, True,
                                ["sq", "consts"], ["psS3"])
                        self.act(lnv[0:64, 0:nq], self.bank(3)[0:64, 0:nq], AF.Ln, ["psS3", "consts"], ["lnv"],
                                 scale=1.0 / 64, bias=self.epsb[0:64, 0:1])
                        self.act(lnv[0:64, 0:nq], lnv[0:64, 0:nq], AF.Exp, ["lnv"], ["lnv"], scale=-0.5)
                        orow = 64 * (h % 2)
                        self.stt(o3[orow:orow + 64, h // 2, 0:nq], dff[0:64, 0:nq], gcol, lnv[0:64, 0:nq], ALU.mult, ALU.mult,
                                 ["dff", "gsub", "lnv"], [on], partial=True)
                else:
                    h = true_head[mi] if mixer == "B" else mi
                    orow = 64 * (h % 2)
                    self.tt("dve", o3[orow:orow + 64, h // 2, 0:nq], acc[0:64, 0:nq], rl[ri][0:64, 0:nq], ALU.mult,
                            [an, rln], [on], partial=True)

            for si in range(min(LA, ns)):
                qk(si)
            for si in range(ns):
                if si + LA < ns:
                    qk(si + LA)
                ex_pv(si)
            state["step"] += ns

        if mixer == "A":
            SB = 3
        load_q(0)
        for ci in range(len(chunks)):
            if ci + 1 < len(chunks):
                load_q(ci + 1)
            do_chunk(ci)
        while pending:
            P.replay(pending.pop(0)[1])

    def phase_out(self, l, tiles0, ntiles, stream_of):
        self.P.phase = 'out%d' % l
        P = self.P
        I = self.inp
        m = self.mark()
        wo = self.alloc(8 * D, BF16)
        wo3 = wo.rearrange("p (c n) -> p c n", n=D)
        Gp = self.alloc(D)
        xin = [self.alloc(2 * D) for _ in range(2)]
        ot = [self.alloc(8 * 256, BF16) for _ in range(2)]
        tmp = [self.alloc(D) for _ in range(2)]
        junk = self.alloc(D, BF16)
        st = [self.alloc(8) for _ in range(2)]
        stg = [self.alloc(1024) for _ in range(6)]
        self.stg_i = 0
        wv = I["w_out"][l].rearrange("(c p) n -> p c n", p=128)
        for c in range(8):
            self.load_cast(stg, wo3[:, c, :], wv[:, c, :], "wo")
        blocks = [(tiles0 + 2 * i) for i in range(ntiles // 2)]
        cur = [None]

        def load(i):
            g0 = blocks[i]
            sl = i % 2
            x3 = xin[sl].rearrange("p (t d) -> p t d", d=D)
            self.ld("sp", x3, self.XS[g0 * 128:(g0 + 2) * 128, :].rearrange("(t p) d -> p t d", p=128), ["XSr%d" % g0], ["oxin%d" % sl])
            o3 = ot[sl].rearrange("p (b t) -> p b t", t=256)
            self.ld("sp", o3, self.OT[:, :, g0 * 128:(g0 + 2) * 128].rearrange("b p t -> p b t"), ["OTall"], ["ot%d" % sl])

        def compute(i):
            g0 = blocks[i]
            sl = i % 2
            s = stream_of(g0)
            if cur[0] != s:
                cur[0] = s
                self.ld("sp", Gp, self.DV[l][s, 1, 2:3, :].partition_broadcast(128), ["DV%d" % l], ["Gp"])
            o3 = ot[sl].rearrange("p (b t) -> p b t", t=256)
            xn = "oxin%d" % sl
            stn = "ost%d" % sl
            for t in range(2):
                py = self.bank(2 * ((2 * i + t) % 4), n=2)
                pn = "psY%d" % ((2 * i + t) % 4)
                for half in range(2):
                    for b in range(8):
                        self.mm(py[:, half * 512:(half + 1) * 512], o3[:, b, t * 128:(t + 1) * 128],
                                wo3[:, b, half * 512:(half + 1) * 512], b == 0, b == 7, ["ot%d" % sl, "wo"], [pn])
                self.act(junk, py, AF.Square, [pn], [stn, "junk"], partial=True, accum_out=st[sl][:, t:t + 1])
            self.ts("dve", st[sl][:, 2:4], st[sl][:, 0:2], 1.0 / D, ALU.mult, [stn], [stn], s2=EPS, op1=ALU.add)
            self.tt("pool", st[sl][:, 2:4], st[sl][:, 2:4], self.neghalf[:, 0:2], ALU.pow, [stn, "consts"], [stn])
            for t in range(2):
                py = self.bank(2 * ((2 * i + t) % 4), n=2)
                pn = "psY%d" % ((2 * i + t) % 4)
                tn = "otmp%d" % t
                self.stt(tmp[t], py, st[sl][:, 2 + t:3 + t], Gp, ALU.mult, ALU.mult, [pn, stn, "Gp"], [tn])
                self.tt("pool", xin[sl][:, t * D:(t + 1) * D], tmp[t], xin[sl][:, t * D:(t + 1) * D], ALU.add,
                        [tn, xn], [xn], partial=True)
            x3 = xin[sl].rearrange("p (t d) -> p t d", d=D)
            self.ld("sp", self.XS[g0 * 128:(g0 + 2) * 128, :].rearrange("(t p) d -> p t d", p=128), x3, [xn], ["XSw%d" % g0])

        n = len(blocks)
        load(0)
        for i in range(n):
            if i + 1 < n:
                load(i + 1)
            compute(i)
        P.barrier()
        self.release(m)


def build(debug=(), stop_after=None):
    kb = KB(debug)
    nc = kb.nc
    inp = {}

    def din(name, shape, dt=F32):
        inp[name] = nc.dram_tensor(name, list(shape), dt, kind="ExternalInput").ap()

    din("x", [SEQ, D])
    din("ctx", [CTX, D])
    din("cvec", [128, 8, 2])
    din("w_ada", [DEPTH, D, 9 * D])
    din("b_ada", [DEPTH, 9 * D])
    din("g_pre", [DEPTH, 3, D])
    din("g_post", [DEPTH, 3, D])
    din("w_ffn1_in", [DEPTH, D, 2 * DFF])
    din("w_ffn1_out", [DEPTH, DFF, D])
    din("w_ffn2_in", [DEPTH, D, 2 * DFF])
    din("w_ffn2_out", [DEPTH, DFF, D])
    din("w_in_p", [DEPTH, D, INW])
    din("w_out", [DEPTH, D, D])
    din("w_uq", [DEPTH, 256, 384])
    din("w_ukv", [DEPTH, 128, 512])
    din("wspT", [DEPTH, 128, 4, 128])
    din("bspT", [DEPTH, 128, 4])
    din("gqk", [DEPTH, 384])
    din("ln_g", [DEPTH, 256])
    din("ln_b", [DEPTH, 256])
    din("g_q_a", [DEPTH, 256])
    din("g_kv_a", [DEPTH, 128])
    din("lam_vecs", [DEPTH, 4, 32])
    din("g_subln", [DEPTH, 64])
    din("rope", [SEQ, 192])
    kb.inp = inp
    out = nc.dram_tensor("out", [SEQ, D], F32, kind="ExternalOutput").ap()
    kb.XS = kb.dram("XS", [T, D], F32)
    kb.DV = [kb.dram("DV%d" % l, [2, 3, 3, D], F32) for l in range(DEPTH)]
    kb.QAT = kb.dram("QAT", [2, 128, T], BF16)
    kb.KAT = kb.dram("KAT", [2, 128, T], BF16)
    kb.QBT = kb.dram("QBT", [2, 128, T], BF16)
    kb.KBT = kb.dram("KBT", [1, 128, T], BF16)
    kb.QDT = kb.dram("QDT", [4, 128, T], BF16)
    kb.KDT = kb.dram("KDT", [4, 128, T], BF16)
    kb.OT = kb.dram("OT", [8, 128, T], BF16)
    kb.VV = kb.dram("VV", [T, 1280], BF16)

    def done():
        kb.P.emit()
        return kb

    kb.setup_consts()
    for l in range(DEPTH):
        kb.phase_mod(l)
    if stop_after == "mod":
        return done()

    def src0(b):
        return inp["ctx"] if b == 0 else inp["x"][(b - 1) * 256:b * 256, :]

    def xs(b):
        return kb.XS[b * 256:(b + 1) * 256, :]

    def outb(b):
        return out[(b - 1) * 256:b * 256, :]

    allblocks = [(0, 1)] + [(b, 0) for b in range(1, 17)]
    latblocks = [(b, 0) for b in range(1, 17)]
    for l in range(DEPTH):
        need_ctx = (l < DEPTH - 1)
        if "noffn" in kb.debug and l == 0:
            kb.ld("sp", kb.XS[0:CTX, :].rearrange("(p a) d -> p (a d)", p=128), inp["ctx"].rearrange("(p a) d -> p (a d)", p=128), [], ["XScopy"], partial=True)
            for i in range(8):
                kb.ld("sp", kb.XS[CTX + 512 * i:CTX + 512 * (i + 1), :].rearrange("(p a) d -> p (a d)", p=128),
                      inp["x"][512 * i:512 * (i + 1), :].rearrange("(p a) d -> p (a d)", p=128), [], ["XScopy"], partial=True)
            kb.P.barrier()
        else:
            kb.phase_ffn(l, 0, src0 if l == 0 else xs, xs, allblocks)
        if stop_after == "ffn%d" % l:
            return done()
        kb.phase_proj(l, need_ctx)
        if stop_after == "proj%d" % l:
            return done()
        kb.phase_att_all(l, need_ctx)
        if stop_after == "attD%d" % l:
            return done()
        if need_ctx:
            kb.phase_out(l, 0, NT, lambda g: 1 if g < 2 else 0)
        else:
            kb.phase_out(l, 2, NT - 2, lambda g: 0)
        if stop_after == "out%d" % l:
            return done()
        if need_ctx:
            kb.phase_ffn(l, 2, xs, xs, allblocks)
        else:
            kb.phase_ffn(l, 2, xs, outb, latblocks)
        if stop_after == "ffnb%d" % l:
            return done()
    return done()


def _rope_table():
    def tabs(rot_dim):
        rows = np.repeat(np.arange(SEQ // 64, dtype=np.float32), 64)
        cols = np.tile(np.arange(64, dtype=np.float32), SEQ // 64)
        axis_dim = rot_dim // 2
        inv_freq = (np.float32(10000.0) ** (-np.arange(0, axis_dim, 2, dtype=np.float32) / np.float32(axis_dim))).astype(np.float32)
        ar = rows[:, None] * inv_freq[None, :]
        ac = cols[:, None] * inv_freq[None, :]
        cr, sr, cc, sc = np.cos(ar), np.sin(ar), np.cos(ac), np.sin(ac)
        C = np.concatenate([cr, cr, cc, cc], axis=1)
        S = np.concatenate([-sr, sr, -sc, sc], axis=1)
        return C.astype(np.float32), S.astype(np.float32)
    C32, S32 = tabs(32)
    C64, S64 = tabs(64)
    return np.ascontiguousarray(np.concatenate([C32, S32, C64, S64], axis=1).astype(np.float32))


def _w_in_perm():
    o = np.arange(INW)
    qb = 768 + np.concatenate([np.arange(0, 64), np.arange(128, 192), np.arange(64, 128), np.arange(192, 256)])
    return np.concatenate([o[0:256], o[256:512], qb, o[1024:1152], o[1152:1280], o[512:768], o[1280:2208]])


def host_inputs(inputs):
    f = lambda a: np.ascontiguousarray(np.asarray(a, dtype=np.float32))
    shared = {}
    for k in ["w_ada", "b_ada", "g_pre", "g_post", "w_ffn1_in", "w_ffn1_out", "w_ffn2_in", "w_ffn2_out",
              "w_out", "w_uq", "w_ukv", "ln_g", "ln_b", "g_q_a", "g_kv_a", "lam_vecs", "g_subln"]:
        shared[k] = f(inputs[k])
    shared["w_in_p"] = f(np.asarray(inputs["w_in"])[:, :, _w_in_perm()])
    shared["wspT"] = f(np.asarray(inputs["w_spatial"]).transpose(0, 3, 1, 2))
    shared["bspT"] = f(np.asarray(inputs["b_spatial"]).transpose(0, 2, 1))
    gq = np.asarray(inputs["g_qnorm"])
    gk = np.asarray(inputs["g_knorm"])
    shared["gqk"] = f(np.concatenate([gq, gq, gq, gq, gk, gk], axis=1))
    shared["rope"] = _rope_table()
    x = np.asarray(inputs["x"])
    ctx = np.asarray(inputs["ctx"])
    c = np.asarray(inputs["c"])
    cc = np.asarray(inputs["c_ctx"])
    maps = []
    for b in range(x.shape[0]):
        mm = dict(shared)
        mm["x"] = f(x[b])
        mm["ctx"] = f(ctx[b])
        mm["cvec"] = f(np.stack([c[b].reshape(8, 128).T, cc.reshape(8, 128).T], axis=-1))
        maps.append(mm)
    return maps


_CACHE = {}


def kernel(**inputs):
    if "kb" not in _CACHE:
        _CACHE["kb"] = build()
    kb = _CACHE["kb"]
    maps = host_inputs(inputs)
    res = run_bass_kernel_spmd(kb.nc, maps, core_ids=list(range(len(maps))))
    return np.stack([np.asarray(r["out"], dtype=np.float32) for r in res.results], axis=0)
```

```python
import math
from contextlib import ExitStack
import numpy as np
import ml_dtypes
import concourse.bass as bass
import concourse.mybir as mybir
from concourse.bass_utils import run_bass_kernel_spmd

AF = mybir.ActivationFunctionType
ALU = mybir.AluOpType
AX = mybir.AxisListType
F32 = mybir.dt.float32
BF16 = mybir.dt.bfloat16

N_DMA_SEMS = 32

D = 1024
SEQ = 4096
CTX = 256
T = SEQ + CTX
NT = T // 128
DFF = 2816
NFF = DFF // 128
DEPTH = 2
EPS = 1e-6
INW = 2208


class Buf:
    __slots__ = ("name", "writers", "readers")

    def __init__(self, name):
        self.name = name
        self.writers = []
        self.readers = []


class Instr:
    __slots__ = ("eng", "fn", "is_dma", "deps", "sig", "idx", "needs_inc", "phase")


class Prog:
    ENGS = ("pe", "act", "dve", "pool", "sp")

    def __init__(self, nc):
        self.nc = nc
        self.instrs = []
        self.bufs = {}
        self.bstart = 0
        self.phase = 'init'
        self.profile = False
        self._rec = None

    def buf(self, name):
        b = self.bufs.get(name)
        if b is None:
            b = Buf(name)
            self.bufs[name] = b
        return b

    def _new(self, eng, fn, is_dma, deps):
        ins = Instr()
        ins.eng = eng
        ins.fn = fn
        ins.is_dma = is_dma
        ins.idx = len(self.instrs)
        ins.needs_inc = False
        ins.sig = None
        ins.phase = self.phase
        ins.deps = sorted(deps)
        for d in ins.deps:
            self.instrs[d].needs_inc = True
        self.instrs.append(ins)
        return ins

    def begin_record(self):
        self._rec = []

    def end_record(self):
        r = self._rec
        self._rec = None
        return r

    def replay(self, ops):
        for o in ops:
            self.op(*o)

    def op(self, eng, fn, reads=(), writes=(), partial=False, is_dma=False):
        if self._rec is not None:
            self._rec.append((eng, fn, tuple(reads), tuple(writes), partial, is_dma))
            return None
        reads = [self.buf(x) for x in reads if x is not None]
        writes = [self.buf(x) for x in writes if x is not None]
        instrs = self.instrs
        deps = set()
        for r in reads:
            deps.update(r.writers)
        for wb in writes:
            deps.update(wb.writers)
            deps.update(wb.readers)
        if eng == "pe" and not is_dma:
            deps = {d for d in deps if instrs[d].eng != "pe" or instrs[d].is_dma}
        ins = self._new(eng, fn, is_dma, deps)
        for r in reads:
            r.readers.append(ins.idx)
        for wb in writes:
            if partial:
                wb.writers.append(ins.idx)
            else:
                wb.writers = [ins.idx]
                wb.readers = []
        return ins

    def dma(self, q, fn, reads=(), writes=(), partial=False):
        return self.op(q, fn, reads, writes, partial, is_dma=True)

    def barrier(self):
        last = {}
        dmas = []
        for ins in self.instrs[self.bstart:]:
            if ins.fn is None:
                continue
            if ins.is_dma:
                dmas.append(ins.idx)
            else:
                last[ins.eng] = ins.idx
        for e in self.ENGS:
            deps = list(last.values()) + dmas
            self._new(e, None, False, deps)
        self.bstart = len(self.instrs)
        for b in self.bufs.values():
            b.writers = []
            b.readers = []

    def emit(self):
        nc = self.nc
        with ExitStack() as es:
            esem = {e: es.enter_context(nc.semaphore("s_" + e)) for e in self.ENGS}
            dsems = [es.enter_context(nc.semaphore("d%d" % i)) for i in range(N_DMA_SEMS)]
            dval = [0] * N_DMA_SEMS
            ecnt = {e: 0 for e in self.ENGS}
            k = 0
            pre = {}
            for ins in self.instrs:
                if ins.is_dma:
                    si = k % N_DMA_SEMS
                    k += 1
                    pre[ins.idx] = (dsems[si], dval[si])
                    dval[si] += 16
                    ins.sig = (dsems[si], dval[si])
                elif ins.needs_inc:
                    ecnt[ins.eng] += 1
                    ins.sig = (esem[ins.eng], ecnt[ins.eng])
            streams = {e: [i for i in self.instrs if i.eng == e] for e in self.ENGS}
            instrs = self.instrs

            def run(e, name):
                waited = {}

                def wait(sem, val):
                    if val <= 0:
                        return
                    key = id(sem)
                    if waited.get(key, 0) >= val:
                        return
                    e.wait_ge(sem, val)
                    waited[key] = val

                cur = [None, None]
                for ins in streams[name]:
                    if self.profile and ins.fn is not None and ins.phase != cur[0]:
                        if cur[0] is not None:
                            nc.leave_named_scope(cur[0], cur[1], False)
                        cur[0] = ins.phase
                        cur[1], _ = nc.enter_named_scope(ins.phase, False)
                    need = {}
                    for d in ins.deps:
                        s, v = instrs[d].sig
                        k2 = id(s)
                        if k2 not in need or need[k2][1] < v:
                            need[k2] = (s, v)
                    for s, v in need.values():
                        wait(s, v)
                    if ins.fn is None:
                        continue
                    if ins.is_dma:
                        s, v = pre[ins.idx]
                        wait(s, v)
                    r = ins.fn(e)
                    if ins.is_dma:
                        r.then_inc(ins.sig[0], 16)
                    elif ins.needs_inc:
                        r.then_inc(ins.sig[0], 1)
                if cur[0] is not None:
                    nc.leave_named_scope(cur[0], cur[1], False)

            with nc.Block() as block:
                @block.sync
                def _(e):
                    run(e, "sp")

                @block.tensor
                def _(e):
                    run(e, "pe")

                @block.scalar
                def _(e):
                    run(e, "act")

                @block.vector
                def _(e):
                    run(e, "dve")

                @block.gpsimd
                def _(e):
                    run(e, "pool")


ARENA_WORDS = 53100


class KB:
    def __init__(self, debug=()):
        self.debug = set(debug)
        nc = bass.Bass("TRN2", target_bir_lowering=False)
        self.nc = nc
        self.P = Prog(nc)
        self.P.profile = 'profile' in self.debug
        self.es = ExitStack()
        self.arena = self.es.enter_context(nc.sbuf_tensor("arena", [128, ARENA_WORDS], F32))
        self.psum = self.es.enter_context(nc.psum_tensor("psum", [128, 4096], F32))
        self.off = 0
        self.uid = 0

    def alloc(self, nfree, dt=F32):
        nb = nfree * (2 if dt == BF16 else 4)
        nw = (nb + 3) // 4
        nw = (nw + 7) // 8 * 8
        assert self.off + nw <= ARENA_WORDS, ("SBUF arena overflow", self.off, nw)
        a = self.arena[:, self.off:self.off + nw]
        self.off += nw
        if dt == BF16:
            a = a.bitcast(BF16)[:, 0:nfree]
        else:
            a = a[:, 0:nfree]
        return a

    def mark(self):
        return self.off

    def release(self, m):
        self.off = m

    def bank(self, i, dt=F32, n=1):
        a = self.psum[:, 512 * i:512 * (i + n)]
        if dt == BF16:
            a = a.bitcast(BF16)
        return a

    def dram(self, name, shape, dt, kind="Internal"):
        if name in self.debug:
            kind = "ExternalOutput"
        return self.nc.dram_tensor(name, list(shape), dt, kind=kind).ap()

    def name(self, p):
        self.uid += 1
        return "%s_%d" % (p, self.uid)

    def mm(self, out, lhsT, rhs, start, stop, reads, writes, **kw):
        self.P.op("pe", lambda e: e.matmul(out, lhsT=lhsT, rhs=rhs, start=start, stop=stop, **kw),
                  reads=reads, writes=writes, partial=True)

    def tr(self, out, in_, reads, writes):
        ident = self.ident
        self.P.op("pe", lambda e: e.transpose(out=out, in_=in_, identity=ident),
                  reads=list(reads) + ["ident"], writes=writes, partial=True)

    def act(self, out, in_, func, reads, writes, partial=False, eng="act", **kw):
        self.P.op(eng, lambda e: e.activation(out=out, in_=in_, func=func, **kw),
                  reads=reads, writes=writes, partial=partial)

    def tt(self, eng, out, in0, in1, op, reads, writes, partial=False):
        self.P.op(eng, lambda e: e.tensor_tensor(out=out, in0=in0, in1=in1, op=op),
                  reads=reads, writes=writes, partial=partial)

    def ts(self, eng, out, in0, s1, op0, reads, writes, s2=None, op1=None, partial=False):
        if op1 is None:
            self.P.op(eng, lambda e: e.tensor_scalar(out=out, in0=in0, scalar1=s1, scalar2=None, op0=op0),
                      reads=reads, writes=writes, partial=partial)
        else:
            self.P.op(eng, lambda e: e.tensor_scalar(out=out, in0=in0, scalar1=s1, scalar2=s2, op0=op0, op1=op1),
                      reads=reads, writes=writes, partial=partial)

    def stt(self, out, in0, scalar, in1, op0, op1, reads, writes, partial=False):
        self.P.op("dve", lambda e: e.scalar_tensor_tensor(out=out, in0=in0, scalar=scalar, in1=in1, op0=op0, op1=op1),
                  reads=reads, writes=writes, partial=partial)

    def cp(self, eng, out, in_, reads, writes, partial=False):
        if eng == "act":
            self.P.op("act", lambda e: e.copy(out=out, in_=in_), reads=reads, writes=writes, partial=partial)
        else:
            self.P.op(eng, lambda e: e.tensor_copy(out=out, in_=in_), reads=reads, writes=writes, partial=partial)

    def ld(self, q, out, in_, reads, writes, partial=False, **kw):
        self.P.dma(q, lambda e: e.dma_start(out=out, in_=in_, **kw), reads=reads, writes=writes, partial=partial)

    def rstd(self, out, ss, n, reads_name, scale):
        self.ts("dve", out, ss, scale, ALU.mult, reads=[reads_name], writes=[reads_name], s2=EPS, op1=ALU.add)
        nh = self.neghalf[:, 0:n]
        self.tt("pool", out, out, nh, ALU.pow, reads=[reads_name, "consts"], writes=[reads_name])

    def load_cast(self, stg, dst, src, wname, n3=None):
        i = self.stg_i
        self.stg_i += 1
        sl = i % len(stg)
        n = dst.shape[-1] if n3 is None else dst.shape[-1] * (dst.shape[-2] if len(dst.shape) > 2 else 1)
        sv = stg[sl][:, 0:n]
        sn = "stg%d_%d" % (id(stg) % 1000, sl)
        if n3 is not None:
            sv3 = sv.rearrange("p (a n) -> p a n", n=n3)
            self.ld("sp" if i % 2 == 0 else "act", sv3, src, [], [sn])
            d3 = dst if len(dst.shape) > 2 else dst.rearrange("p (a n) -> p a n", n=n3)
            self.cp(("dve", "pool", "dve")[i % 3], d3, sv3, [sn], [wname], partial=True)
        else:
            self.ld("sp" if i % 2 == 0 else "act", sv, src, [], [sn])
            self.cp(("dve", "pool", "dve")[i % 3], dst, sv, [sn], [wname], partial=True)

    def setup_consts(self):
        P = self.P
        self.ident = self.alloc(128, BF16)
        self.neghalf = self.alloc(16, F32)
        self.ones_f = self.alloc(64, F32)
        self.ones_b = self.alloc(128, BF16)
        self.epsb = self.alloc(1, F32)
        m = self.mark()
        idf = self.alloc(128, F32)
        ident, nh, of, ob, epsb = self.ident, self.neghalf, self.ones_f, self.ones_b, self.epsb
        P.op("pool", lambda e: e.memset(idf, 0.0), writes=["idf"])
        P.op("pool", lambda e: e.affine_select(out=idf, in_=idf, pattern=[[-1, 128]], compare_op=ALU.not_equal,
                                               fill=1.0, base=0, channel_multiplier=1), reads=["idf"], writes=["idf"])
        P.op("dve", lambda e: e.tensor_copy(out=ident, in_=idf), reads=["idf"], writes=["ident"])
        P.op("pool", lambda e: e.memset(nh, -0.5), writes=["consts"])
        P.op("pool", lambda e: e.memset(of, 1.0), writes=["consts"], partial=True)
        P.op("pool", lambda e: e.memset(ob, 1.0), writes=["consts"], partial=True)
        P.op("pool", lambda e: e.memset(epsb, EPS), writes=["consts"], partial=True)
        P.barrier()
        self.release(m)

    def phase_mod(self, l):
        self.P.phase = 'mod%d' % l
        P = self.P
        I = self.inp
        m = self.mark()
        cv = self.alloc(16, F32)
        sc = self.alloc(16, F32)
        cv3 = cv.rearrange("p (c s) -> p c s", s=2)
        sc3 = sc.rearrange("p (c s) -> p c s", s=2)
        modsb = self.alloc(9 * D, F32)
        bada = self.alloc(9 * D, F32)
        gpre = self.alloc(3 * D, F32)
        gpost = self.alloc(3 * D, F32)
        dv = self.alloc(9 * D, F32)
        NS = 3
        wsl = [self.alloc(8 * 512, F32) for _ in range(NS)]
        self.ld("sp", cv3, I["cvec"], [], ["cv"])
        self.ld("sp", bada[0:2, :], I["b_ada"][l:l + 1, :].partition_broadcast(2), [], ["bada"])
        self.ld("sp", gpre[0:2, :], I["g_pre"][l:l + 1].rearrange("o j d -> o (j d)").partition_broadcast(2), [], ["gpre"])
        self.ld("sp", gpost[0:2, :], I["g_post"][l:l + 1].rearrange("o j d -> o (j d)").partition_broadcast(2), [], ["gpost"])
        self.act(sc, cv, AF.Tanh, ["cv"], ["sc"], scale=0.5)
        self.ts("dve", sc, sc, 1.0, ALU.add, ["sc"], ["sc"], s2=0.5, op1=ALU.mult)
        self.tt("dve", sc, sc, cv, ALU.mult, ["sc", "cv"], ["sc"])
        wada = I["w_ada"][l].rearrange("(c p) n -> p c n", p=128)
        for n in range(18):
            s = n % NS
            w3 = wsl[s].rearrange("p (c n) -> p c n", n=512)
            self.ld("sp" if n % 2 == 0 else "act", w3, wada[:, :, n * 512:(n + 1) * 512], [], ["wsl%d" % s])
            pb = "psm%d" % (n % 2)
            po = self.bank(n % 2)[0:2, :]
            for c in range(8):
                self.mm(po, sc3[:, c, :], w3[:, c, :], c == 0, c == 7, ["sc", "wsl%d" % s], [pb])
            self.tt("dve", modsb[0:2, n * 512:(n + 1) * 512], po, bada[0:2, n * 512:(n + 1) * 512], ALU.add,
                    [pb, "bada"], ["modsb"], partial=True)
        for j in range(3):
            wj = 1.0 if j == 1 else 0.5
            sh = modsb[0:2, (3 * j) * D:(3 * j + 1) * D]
            scl = modsb[0:2, (3 * j + 1) * D:(3 * j + 2) * D]
            gt = modsb[0:2, (3 * j + 2) * D:(3 * j + 3) * D]
            self.stt(dv[0:2, (3 * j) * D:(3 * j + 1) * D], scl, 1.0, gpre[0:2, j * D:(j + 1) * D], ALU.add, ALU.mult,
                     ["modsb", "gpre"], ["dv"], partial=True)
            self.cp("dve", dv[0:2, (3 * j + 1) * D:(3 * j + 2) * D], sh, ["modsb"], ["dv"], partial=True)
            self.stt(dv[0:2, (3 * j + 2) * D:(3 * j + 3) * D], gt, wj, gpost[0:2, j * D:(j + 1) * D], ALU.mult, ALU.mult,
                     ["modsb", "gpost"], ["dv"], partial=True)
        self.ld("sp", self.DV[l].rearrange("s j k d -> s (j k d)"), dv[0:2, :], ["dv"], ["DV%d" % l])
        P.barrier()
        self.release(m)

    def phase_ffn(self, l, j, src, dst, blocks):
        self.P.phase = 'ffn%d_%d' % (l, j)
        P = self.P
        I = self.inp
        m = self.mark()
        w_in = I["w_ffn1_in" if j == 0 else "w_ffn2_in"][l]
        w_out = I["w_ffn1_out" if j == 0 else "w_ffn2_out"][l]
        w1 = self.alloc(8 * 2 * DFF, BF16)
        w2 = self.alloc(NFF * D, BF16)
        w13 = w1.rearrange("p (c n) -> p c n", n=2 * DFF)
        w23 = w2.rearrange("p (c n) -> p c n", n=D)
        mstg = self.mark()
        stg = [self.alloc(2816) for _ in range(6)]
        self.stg_i = 0
        w_in_v = w_in.rearrange("(c p) n -> p c n", p=128)
        for c in range(8):
            for h in range(2):
                self.load_cast(stg, w13[:, c, h * DFF:(h + 1) * DFF], w_in_v[:, c, h * DFF:(h + 1) * DFF], "w1")
        w_out_v = w_out.rearrange("(c p) n -> p c n", p=128)
        for c0 in range(0, NFF, 2):
            self.load_cast(stg, w23[:, c0:c0 + 2, :], w_out_v[:, c0:c0 + 2, :], "w2", n3=D)
        P.barrier()
        self.release(mstg)
        G = self.alloc(D)
        S = self.alloc(D)
        Gp = self.alloc(D)
        xin = [self.alloc(2 * D) for _ in range(2)]
        hb = self.alloc(2 * D, BF16)
        hT = self.alloc(8 * 256, BF16)
        hT3 = hT.rearrange("p (c t) -> p c t", t=256)
        actT = self.alloc(NFF * 256, BF16)
        actT3 = actT.rearrange("p (c t) -> p c t", t=256)
        tmp = [self.alloc(D) for _ in range(2)]
        junk = self.alloc(D, BF16)
        sg = [self.alloc(256) for _ in range(3)]
        st = [self.alloc(8) for _ in range(2)]
        cur_stream = [None, None]

        def load_mod(s):
            if cur_stream[0] == s:
                return
            cur_stream[0] = s
            dvl = self.DV[l]
            self.ld("sp", G, dvl[s, j, 0:1, :].partition_broadcast(128), ["DV%d" % l], ["G"])
            self.ld("sp", S, dvl[s, j, 1:2, :].partition_broadcast(128), ["DV%d" % l], ["S"])

        def load_gp(s):
            if cur_stream[1] == s:
                return
            cur_stream[1] = s
            dvl = self.DV[l]
            self.ld("sp", Gp, dvl[s, j, 2:3, :].partition_broadcast(128), ["DV%d" % l], ["Gp"])

        def load(i):
            b, s = blocks[i]
            sl = i % 2
            x3 = xin[sl].rearrange("p (t d) -> p t d", d=D)
            self.ld("sp", x3, src(b).rearrange("(t p) d -> p t d", p=128), [self.srcname(b)], ["xin%d" % sl])

        def prenorm(i):
            b, s = blocks[i]
            sl = i % 2
            load_mod(s)
            xn = "xin%d" % sl
            stn = "st%d" % sl
            for t in range(2):
                self.act(junk, xin[sl][:, t * D:(t + 1) * D], AF.Square, [xn], [stn, "junk"], partial=True,
                         accum_out=st[sl][:, t:t + 1])
            self.ts("dve", st[sl][:, 2:4], st[sl][:, 0:2], 1.0 / D, ALU.mult, [stn], [stn], s2=EPS, op1=ALU.add)
            self.tt("pool", st[sl][:, 2:4], st[sl][:, 2:4], self.neghalf[:, 0:2], ALU.pow, [stn, "consts"], [stn])
            for t in range(2):
                tn = "tmp%d" % t
                self.stt(tmp[t], xin[sl][:, t * D:(t + 1) * D], st[sl][:, 2 + t:3 + t], G, ALU.mult, ALU.mult,
                         [xn, stn, "G"], [tn])
                self.tt("dve", hb[:, t * D:(t + 1) * D], tmp[t], S, ALU.add, [tn, "S"], ["hb%d" % t])

        def transposes(i):
            for t in range(2):
                pT = self.bank(0, BF16)
                pT3 = pT.rearrange("p (c t) -> p c t", t=128)
                for c in range(8):
                    self.tr(pT3[:, c, :], hb[:, t * D + c * 128:t * D + (c + 1) * 128], ["hb%d" % t], ["psT"])
                self.cp("act", hT3[:, :, t * 128:(t + 1) * 128], pT3, ["psT"], ["hT"], partial=True)

        def mm1(i, side=()):
            side = list(side)
            per = max(1, (len(side) + 13) // 14)
            for f in range(NFF):
                if f >= 3 and side:
                    P.replay(side[:per])
                    side = side[per:]
                bk = 1 + (f % 3)
                pn = "psM%d" % bk
                pm = self.bank(bk)
                for half in range(2):
                    col = half * DFF + f * 128
                    for c in range(8):
                        self.mm(pm[:, half * 256:(half + 1) * 256], w13[:, c, col:col + 128], hT3[:, c, :],
                                c == 0, c == 7, ["w1", "hT"], [pn])
                sgi = f % 3
                self.act(sg[sgi], pm[:, 0:256], AF.Silu, [pn], ["sg%d" % sgi])
                self.tt("dve", actT3[:, f, :], pm[:, 256:512], sg[sgi], ALU.mult, [pn, "sg%d" % sgi], ["actT"], partial=True)
            P.replay(side)

        def mm2_post(i):
            b, s = blocks[i]
            sl = i % 2
            xn = "xin%d" % sl
            stn = "st%d" % sl
            load_gp(s)
            for t in range(2):
                py = self.bank(4 + 2 * t, n=2)
                pn = "psY%d" % t
                for half in range(2):
                    for f in range(NFF):
                        self.mm(py[:, half * 512:(half + 1) * 512], actT3[:, f, t * 128:(t + 1) * 128],
                                w23[:, f, half * 512:(half + 1) * 512], f == 0, f == NFF - 1, ["actT", "w2"], [pn])
                self.act(junk, py, AF.Square, [pn], [stn, "junk"], partial=True, accum_out=st[sl][:, 4 + t:5 + t])
            self.ts("dve", st[sl][:, 6:8], st[sl][:, 4:6], 1.0 / D, ALU.mult, [stn], [stn], s2=EPS, op1=ALU.add)
            self.tt("pool", st[sl][:, 6:8], st[sl][:, 6:8], self.neghalf[:, 0:2], ALU.pow, [stn, "consts"], [stn])
            for t in range(2):
                py = self.bank(4 + 2 * t, n=2)
                pn = "psY%d" % t
                tn = "tmp%d" % t
                self.stt(tmp[t], py, st[sl][:, 6 + t:7 + t], Gp, ALU.mult, ALU.mult, [pn, stn, "Gp"], [tn])
                self.tt("dve", xin[sl][:, t * D:(t + 1) * D], tmp[t], xin[sl][:, t * D:(t + 1) * D], ALU.add,
                        [tn, xn], [xn], partial=True)
            x3 = xin[sl].rearrange("p (t d) -> p t d", d=D)
            self.ld("sp", dst(b).rearrange("(t p) d -> p t d", p=128), x3, [xn], [self.dstname(b)])

        n = len(blocks)
        load(0)
        prenorm(0)
        transposes(0)
        for i in range(n):
            if i + 1 < n:
                load(i + 1)
            side = ()
            if i + 1 < n:
                P.begin_record()
                prenorm(i + 1)
                side = P.end_record()
            mm1(i, side)
            if i + 1 < n:
                transposes(i + 1)
            mm2_post(i)
        P.barrier()
        self.release(m)

    def srcname(self, b):
        return "XS%d" % b

    def dstname(self, b):
        return "XS%d" % b


    def bc(self, ap, G):
        shp = list(ap.shape)
        return ap[:, None].to_broadcast([shp[0], G] + shp[1:])

    def rope(self, sfx, eng_a, xv, outv, Ct, St, G, Dh, t1, t2, rn, wn, partial=True):
        w = Dh // 4
        t13 = t1[:, 0:G * Dh].rearrange("p (g d) -> p g d", d=Dh)
        t25 = t2[:, 0:G * Dh].rearrange("p (g a h w) -> p g a h w", a=2, h=2, w=w)
        t23 = t2[:, 0:G * Dh].rearrange("p (g d) -> p g d", d=Dh)
        xv5 = xv.rearrange("p g (a h w) -> p g a h w", a=2, h=2, w=w)
        S4 = St.rearrange("p (a h w) -> p a h w", a=2, h=2, w=w)
        self.tt(eng_a, t13, xv, self.bc(Ct, G), ALU.mult, rn + ["rope"], ["rt1_%d" % sfx])
        self.tt("dve", t25[:, :, :, 0, :], xv5[:, :, :, 1, :], self.bc(S4[:, :, 0, :], G), ALU.mult, rn + ["rope"], ["rt2_%d" % sfx], partial=True)
        self.tt("dve", t25[:, :, :, 1, :], xv5[:, :, :, 0, :], self.bc(S4[:, :, 1, :], G), ALU.mult, rn + ["rope"], ["rt2_%d" % sfx], partial=True)
        self.tt("dve", outv, t13, t23, ALU.add, ["rt1_%d" % sfx, "rt2_%d" % sfx], wn, partial=partial)

    def phase_proj(self, l, need_q_ctx):
        self.P.phase = 'proj%d' % l
        P = self.P
        I = self.inp
        m = self.mark()
        NW = INW
        w_in = self.alloc(8 * NW, BF16)
        w_in3 = w_in.rearrange("p (c n) -> p c n", n=NW)
        w_uq = self.alloc(2 * 384, BF16)
        w_uq3 = w_uq.rearrange("p (c n) -> p c n", n=384)
        w_ukv = self.alloc(512, BF16)
        wsp = self.alloc(4 * 128, BF16)
        wsp3 = wsp.rearrange("p (g n) -> p g n", n=128)
        G = self.alloc(D)
        S = self.alloc(D)
        gqk = self.alloc(384)
        lng = self.alloc(256)
        lnb = self.alloc(256)
        gqa = self.alloc(256)
        gkva = self.alloc(128)
        bsp = self.alloc(4)
        xt = [self.alloc(D) for _ in range(4)]
        rts = [self.alloc(192) for _ in range(6)]
        hb = self.alloc(D, BF16)
        hT = self.alloc(8 * 128, BF16)
        hT3 = hT.rearrange("p (c t) -> p c t", t=128)
        pj = [self.alloc(NW) for _ in range(4)]
        t1_s = [self.alloc(512) for _ in range(2)]
        t2_s = [self.alloc(512) for _ in range(2)]
        ta = self.alloc(512)
        tb = self.alloc(512)
        tB_s = [self.alloc(384) for _ in range(2)]
        tC_s = [self.alloc(512) for _ in range(2)]
        tD_s = [self.alloc(256) for _ in range(2)]
        guv_s = [self.alloc(512) for _ in range(2)]
        junk = self.alloc(D, BF16)
        qkA_s = [self.alloc(512, BF16) for _ in range(2)]
        qkB_s = [self.alloc(384, BF16) for _ in range(2)]
        vnb_s = [self.alloc(256, BF16) for _ in range(2)]
        cl_s = [self.alloc(256, BF16) for _ in range(2)]
        cn_s = [self.alloc(384, BF16) for _ in range(2)]
        cT_s = [self.alloc(3 * 128, BF16) for _ in range(2)]
        qDb_s = [self.alloc(4 * 96, BF16) for _ in range(2)]
        kDb_s = [self.alloc(4 * 96, BF16) for _ in range(2)]
        krr_s = [self.alloc(32) for _ in range(2)]
        qDf_s = [self.alloc(128) for _ in range(2)]
        st = [self.alloc(32) for _ in range(4)]
        mstg = self.mark()
        stg = [self.alloc(2208) for _ in range(6)]
        self.stg_i = 0
        wv = I["w_in_p"][l].rearrange("(c p) n -> p c n", p=128)
        for c in range(8):
            self.load_cast(stg, w_in3[:, c, :], wv[:, c, :], "w_in")
        for c in range(2):
            self.load_cast(stg, w_uq3[:, c, :], I["w_uq"][l][c * 128:(c + 1) * 128, :], "w_uq")
        self.load_cast(stg, w_ukv, I["w_ukv"][l], "w_ukv")
        self.load_cast(stg, wsp, I["wspT"][l], "wsp", n3=128)
        self.ld("sp", gqk, I["gqk"][l:l + 1, :].partition_broadcast(128), [], ["par"], partial=True)
        self.ld("sp", lng, I["ln_g"][l:l + 1, :].partition_broadcast(128), [], ["par"], partial=True)
        self.ld("sp", lnb, I["ln_b"][l:l + 1, :].partition_broadcast(128), [], ["par"], partial=True)
        self.ld("sp", gqa, I["g_q_a"][l:l + 1, :].partition_broadcast(128), [], ["par"], partial=True)
        self.ld("sp", gkva, I["g_kv_a"][l:l + 1, :].partition_broadcast(128), [], ["par"], partial=True)
        self.ld("sp", bsp, I["bspT"][l], [], ["par"], partial=True)
        P.barrier()
        self.release(mstg)
        stgT = [self.alloc(17 * 512, BF16) for _ in range(2)]
        vst = [self.alloc(4 * 1280, BF16) for _ in range(2)]
        for sl in range(2):
            v4 = vst[sl].rearrange("p (t h c) -> p t h c", h=10, c=128)
            P.op("pool", lambda e, v4=v4: e.memset(v4[:, :, :, 64:128], 1.0), writes=["vst%d" % sl], partial=True)

        blocks = [(0, 2, 1)] + [(2 + 4 * i, 4, 0) for i in range(8)]
        tiles = []
        for bi, (g0, nt, s) in enumerate(blocks):
            for tt in range(nt):
                tiles.append((bi, g0, nt, s, tt))
        cur = [None]

        def loadx(ti):
            bi, g0, nt, s, tt = tiles[ti]
            gt = g0 + tt
            k4 = ti % 4
            self.ld("sp", xt[k4], self.XS[gt * 128:(gt + 1) * 128, :], ["XSall"], ["pxt%d" % k4])
            if s == 0:
                lt = gt - 2
                self.ld("sp", rts[ti % 6], I["rope"][lt * 128:(lt + 1) * 128, :], [], ["rt%d" % (ti % 6)])

        def stageA(ti):
            bi, g0, nt, s, tt = tiles[ti]
            ps = ti % 4
            if cur[0] != s:
                cur[0] = s
                dvl = self.DV[l]
                self.ld("sp", G, dvl[s, 1, 0:1, :].partition_broadcast(128), ["DV%d" % l], ["G"])
                self.ld("sp", S, dvl[s, 1, 1:2, :].partition_broadcast(128), ["DV%d" % l], ["S"])
            xn = "pxt%d" % ps
            stn = "pst%d" % ps
            x = xt[ps]
            sv = st[ps]
            self.act(junk, x, AF.Square, [xn], [stn, "junk"], partial=True, accum_out=sv[:, 0:1])
            self.ts("dve", sv[:, 1:2], sv[:, 0:1], 1.0 / D, ALU.mult, [stn], [stn], s2=EPS, op1=ALU.add)
            self.tt("pool", sv[:, 1:2], sv[:, 1:2], self.neghalf[:, 0:1], ALU.pow, [stn, "consts"], [stn])
            self.stt(ta, x[:, 0:512], sv[:, 1:2], G[:, 0:512], ALU.mult, ALU.mult, [xn, stn, "G"], ["ta"])
            self.tt("pool", hb[:, 0:512], ta, S[:, 0:512], ALU.add, ["ta", "S"], ["hb"], partial=True)
            self.stt(tb, x[:, 512:1024], sv[:, 1:2], G[:, 512:1024], ALU.mult, ALU.mult, [xn, stn, "G"], ["tb"])
            self.tt("pool", hb[:, 512:1024], tb, S[:, 512:1024], ALU.add, ["tb", "S"], ["hb"], partial=True)
            pT = self.bank(0, BF16)
            pT3 = pT.rearrange("p (c t) -> p c t", t=128)
            for c in range(8):
                self.tr(pT3[:, c, :], hb[:, c * 128:(c + 1) * 128], ["hb"], ["psT"])
            self.cp("act", hT3, pT3, ["psT"], ["hT"])
            chunks = [(0, 512), (512, 512), (1024, 512), (1536, 512), (2048, 160)]
            for n, (c0, cw) in enumerate(chunks):
                bk = 1 + (n % 2)
                pn = "psP%d" % bk
                po = self.bank(bk)[:, 0:cw]
                for c in range(8):
                    self.mm(po, hT3[:, c, :], w_in3[:, c, c0:c0 + cw], c == 0, c == 7, ["hT", "w_in"], [pn])
                self.cp("act" if n % 2 == 0 else "dve", pj[ps][:, c0:c0 + cw], po, [pn], ["pj%d" % ps], partial=True)

        def stageB(ti):
            bi, g0, nt, s, tt = tiles[ti]
            sl = bi % 2
            ps = ti % 4
            p = ti % 2
            X = 3 + 2 * p
            Y = 4 + 2 * p
            psX = "ps%d" % X
            psY = "ps%d" % Y
            t1, t2, tB, tC, tD, guv, krr, qDf = t1_s[p], t2_s[p], tB_s[p], tC_s[p], tD_s[p], guv_s[p], krr_s[p], qDf_s[p]
            qkA, qkB, vnb, cl, cn, cT, qDb, kDb = qkA_s[p], qkB_s[p], vnb_s[p], cl_s[p], cn_s[p], cT_s[p], qDb_s[p], kDb_s[p]
            cT3 = cT.rearrange("p (c t) -> p c t", t=128)
            lat = (s == 0)
            latA = lat and 'norA' not in self.debug
            latB = lat and 'norB' not in self.debug
            latDq = lat and 'norDq' not in self.debug
            latDk = lat and 'norDk' not in self.debug
            pjn = "pj%d" % ps
            stn = "pst%d" % ps
            x = pj[ps]
            sv = st[ps]
            rt = rts[ti % 6]
            C32, S32, C64, S64 = rt[:, 0:32], rt[:, 32:64], rt[:, 64:128], rt[:, 128:192]
            rn = ["rt%d" % (ti % 6)]
            vn = "vst%d" % sl
            v4 = vst[sl].rearrange("p (t h c) -> p t h c", h=10, c=128)
            xa = x[:, 0:512].rearrange("p (g d) -> p g d", d=32)
            if latA:
                self.rope(p, "pool", xa, qkA.rearrange("p (g d) -> p g d", d=32), C32, S32, 16, 32, t1, t2, [pjn] + rn, ["qkA%d" % p], partial=False)
            else:
                self.cp("pool", qkA, x[:, 0:512], [pjn], ["qkA%d" % p])
            xb = x[:, 512:896]
            xb3 = xb.rearrange("p (g d) -> p g d", d=64)
            self.act(tB, xb, AF.Square, [pjn], ["tB%d" % p])
            P.op("dve", lambda e: e.tensor_reduce(out=sv[:, 8:14], in_=tB.rearrange("p (g d) -> p g d", d=64),
                                                  op=ALU.add, axis=AX.X), reads=["tB%d" % p], writes=[stn])
            self.ts("dve", sv[:, 8:14], sv[:, 8:14], 1.0 / 64, ALU.mult, [stn], [stn], s2=EPS, op1=ALU.add)
            self.tt("pool", sv[:, 8:14], sv[:, 8:14], self.neghalf[:, 0:6], ALU.pow, [stn, "consts"], [stn])
            tB3 = tB.rearrange("p (g d) -> p g d", d=64)
            self.tt("dve", tB3, xb3, sv[:, 8:14, None].to_broadcast([128, 6, 64]), ALU.mult, [pjn, stn], ["tB%d" % p])
            self.tt("pool", tB, tB, gqk, ALU.mult, ["tB%d" % p, "par"], ["tB%d" % p])
            if latB:
                self.rope(p, "pool", tB3, qkB.rearrange("p (g d) -> p g d", d=64), C64, S64, 6, 64, t1, t2, ["tB%d" % p] + rn, ["qkB%d" % p], partial=False)
            else:
                self.cp("pool", qkB, tB, ["tB%d" % p], ["qkB%d" % p])
            self.cp("act", v4[:, tt, 0:4, 0:64], x[:, 1024:1280].rearrange("p (h c) -> p h c", c=64), [pjn], [vn], partial=True)
            self.cp("act", v4[:, tt, 4:6, 0:64], x[:, 896:1024].rearrange("p (h c) -> p h c", c=64), [pjn], [vn], partial=True)
            xc = x[:, 1280:1792]
            self.act(tC, xc, AF.Square, [pjn], ["tC%d" % p])
            self.ts("dve", tC, tC, 0.044715, ALU.mult, ["tC%d" % p], ["tC%d" % p], s2=1.0, op1=ALU.add)
            self.tt("pool", tC, tC, xc, ALU.mult, ["tC%d" % p, pjn], ["tC%d" % p])
            self.act(tC, tC, AF.Tanh, ["tC%d" % p], ["tC%d" % p], scale=0.7978845608028654)
            self.ts("dve", tC, tC, 1.0, ALU.add, ["tC%d" % p], ["tC%d" % p], s2=0.5, op1=ALU.mult)
            self.tt("dve", guv, tC, xc, ALU.mult, ["tC%d" % p, pjn], ["guv%d" % p])
            gv = guv[:, 256:512]
            self.act(junk[:, 0:256], gv, AF.Identity, ["guv%d" % p], [stn, "junk"], partial=True, accum_out=sv[:, 16:17])
            self.act(junk[:, 256:512], gv, AF.Square, ["guv%d" % p], [stn, "junk"], partial=True, accum_out=sv[:, 17:18])
            self.ts("dve", sv[:, 18:19], sv[:, 16:17], 1.0 / 256, ALU.mult, [stn], [stn])
            self.tt("dve", sv[:, 19:20], sv[:, 18:19], sv[:, 18:19], ALU.mult, [stn], [stn])
            self.stt(sv[:, 20:21], sv[:, 17:18], 1.0 / 256, sv[:, 19:20], ALU.mult, ALU.subtract, [stn], [stn])
            self.ts("dve", sv[:, 20:21], sv[:, 20:21], EPS, ALU.add, [stn], [stn])
            self.tt("pool", sv[:, 20:21], sv[:, 20:21], self.neghalf[:, 0:1], ALU.pow, [stn, "consts"], [stn])
            self.ts("dve", tD, gv, sv[:, 18:19], ALU.subtract, ["guv%d" % p, stn], ["tD%d" % p], s2=sv[:, 20:21], op1=ALU.mult)
            self.tt("pool", tD, tD, lng, ALU.mult, ["tD%d" % p, "par"], ["tD%d" % p])
            self.tt("pool", vnb, tD, lnb, ALU.add, ["tD%d" % p, "par"], ["vnb%d" % p])
            pg = self.bank(X)[:, 256:512]
            for g in range(4):
                self.mm(pg[:, g * 64:(g + 1) * 64], wsp3[:, g, :], vnb[:, g * 64:(g + 1) * 64], True, True, ["wsp", "vnb%d" % p], [psX])
            for g in range(4):
                self.stt(cl[:, g * 64:(g + 1) * 64], pg[:, g * 64:(g + 1) * 64], bsp[:, g:g + 1], guv[:, g * 64:(g + 1) * 64],
                         ALU.add, ALU.mult, [psX, "par", "guv%d" % p], ["cl%d" % p], partial=True)
            self.act(junk[:, 512:768], x[:, 1792:2048], AF.Square, [pjn], [stn, "junk"], partial=True, accum_out=sv[:, 24:25])
            self.act(junk[:, 768:896], x[:, 2048:2176], AF.Square, [pjn], [stn, "junk"], partial=True, accum_out=sv[:, 25:26])
            self.ts("dve", sv[:, 26:27], sv[:, 24:25], 1.0 / 256, ALU.mult, [stn], [stn], s2=EPS, op1=ALU.add)
            self.ts("dve", sv[:, 27:28], sv[:, 25:26], 1.0 / 128, ALU.mult, [stn], [stn], s2=EPS, op1=ALU.add)
            self.tt("pool", sv[:, 26:28], sv[:, 26:28], self.neghalf[:, 0:2], ALU.pow, [stn, "consts"], [stn])
            self.stt(cn[:, 0:256], x[:, 1792:2048], sv[:, 26:27], gqa, ALU.mult, ALU.mult, [pjn, stn, "par"], ["cn%d" % p], partial=True)
            self.stt(cn[:, 256:384], x[:, 2048:2176], sv[:, 27:28], gkva, ALU.mult, ALU.mult, [pjn, stn, "par"], ["cn%d" % p], partial=True)
            pc = self.bank(X, BF16)[:, 0:384]
            pc3 = pc.rearrange("p (c t) -> p c t", t=128)
            for c in range(3):
                self.tr(pc3[:, c, :], cn[:, c * 128:(c + 1) * 128], ["cn%d" % p], [psX])
            self.cp("act", cT3, pc3, [psX], ["cT%d" % p])
            pq = self.bank(Y)[:, 0:384]
            for c in range(2):
                self.mm(pq, cT3[:, c, :], w_uq3[:, c, :], c == 0, c == 1, ["cT%d" % p, "w_uq"], [psY])
            pq3 = pq.rearrange("p (h d) -> p h d", d=96)
            qD3 = qDb.rearrange("p (h d) -> p h d", d=96)
            kD3 = kDb.rearrange("p (h d) -> p h d", d=96)
            qDf3 = qDf.rearrange("p (h d) -> p h d", d=32)
            self.cp("act", qD3[:, :, 0:64], pq3[:, :, 0:64], [psY], ["qDb%d" % p], partial=True)
            if latDq:
                self.cp("act", qDf3, pq3[:, :, 64:96], [psY], ["qDf%d" % p])
            else:
                self.cp("act", qD3[:, :, 64:96], pq3[:, :, 64:96], [psY], ["qDb%d" % p], partial=True)
            pkv = self.bank(Y)
            self.mm(pkv, cT3[:, 2, :], w_ukv, True, True, ["cT%d" % p, "w_ukv"], [psY])
            pkv3 = pkv.rearrange("p (h d) -> p h d", d=128)
            self.cp("act", kD3[:, :, 0:64], pkv3[:, :, 0:64], [psY], ["kDb%d" % p], partial=True)
            self.cp("act", v4[:, tt, 6:10, 0:64], pkv3[:, :, 64:128], [psY], [vn], partial=True)
            if latDq:
                self.rope(p, "pool", qDf3, qD3[:, :, 64:96], C32, S32, 4, 32, t1, t2, ["qDf%d" % p] + rn, ["qDb%d" % p])
            if latDk:
                self.rope(p, "pool", x[:, 2176:2208].rearrange("p (g d) -> p g d", d=32), krr.rearrange("p (g d) -> p g d", d=32),
                          C32, S32, 1, 32, t1, t2, [pjn] + rn, ["krr%d" % p], partial=False)
                self.cp("pool", kD3[:, :, 64:96], self.bc(krr, 4), ["krr%d" % p], ["kDb%d" % p], partial=True)
            else:
                self.cp("pool", kD3[:, :, 64:96], self.bc(x[:, 2176:2208], 4), [pjn], ["kDb%d" % p], partial=True)
            sT = stgT[sl].rearrange("p (b t) -> p b t", t=512)
            sn = "stgT%d" % sl
            c0 = tt * 128
            p7 = self.bank(Y, BF16).rearrange("p (c t) -> p c t", t=128)
            p5 = self.bank(X, BF16)[:, 384:512]
            for k in range(4):
                self.tr(p7[:, k, :], qkA[:, k * 128:(k + 1) * 128], ["qkA%d" % p], [psY])
            for k in range(3):
                self.tr(p7[:, 4 + k, :], qkB[:, k * 128:(k + 1) * 128], ["qkB%d" % p], [psY])
            self.tr(p7[:, 7, :], cl[:, 0:128], ["cl%d" % p], [psY])
            self.tr(p5, cl[:, 128:256], ["cl%d" % p], [psX])
            self.cp("dve", sT[:, 0:8, c0:c0 + 128], p7, [psY], [sn], partial=True)
            self.cp("act", sT[:, 8, c0:c0 + 128], p5, [psX], [sn], partial=True)
            for h in range(4):
                self.tr(p7[0:96, h, :], qDb[:, h * 96:(h + 1) * 96], ["qDb%d" % p], [psY])
            for h in range(4):
                self.tr(p7[0:96, 4 + h, :], kDb[:, h * 96:(h + 1) * 96], ["kDb%d" % p], [psY])
            self.cp("dve", sT[0:96, 9:17, c0:c0 + 128], p7[0:96], [psY], [sn], partial=True)

        def store_block(bi):
            g0, nt, s = blocks[bi]
            sl = bi % 2
            sT = stgT[sl].rearrange("p (b t) -> p b t", t=512)
            sn = "stgT%d" % sl
            t0 = g0 * 128
            n = nt * 128
            L = "L%d" % l

            def st_(dst, b0, nb, rows, name, q):
                self.ld(q, dst[:, 0:rows, t0:t0 + n].rearrange("b p t -> p b t"), sT[0:rows, b0:b0 + nb, 0:n], [sn], [name])
            st_(self.QAT, 0, 2, 128, "QAT", "sp")
            st_(self.KAT, 2, 2, 128, "KAT", "sp")
            st_(self.QBT, 4, 2, 128, "QBT", "sp")
            st_(self.KBT, 6, 1, 128, "KBT", "sp")
            st_(self.OT[4:6], 7, 2, 128, "OTC", "sp")
            st_(self.QDT, 9, 4, 96, "QDT", "sp")
            st_(self.KDT, 13, 4, 96, "KDT", "sp")
            v3 = vst[sl][:, 0:nt * 1280].rearrange("p (t c) -> p t c", c=1280)
            self.ld("sp", self.VV[t0:t0 + n, :].rearrange("(t p) c -> p t c", p=128), v3, ["vst%d" % sl], ["VV"])

        nti = len(tiles)
        for dflag in self.debug:
            if dflag.startswith("ptiles="):
                nti = int(dflag[7:])

        def weave(lists):
            lists = [list(x) for x in lists]
            out = []
            while any(lists):
                for x in lists:
                    if x:
                        out.append(x.pop(0))
            return out

        def sprinkle(main, extra):
            if not extra:
                return list(main)
            out = []
            n, k = len(main), len(extra)
            j = 0
            for i, o in enumerate(main):
                out.append(o)
                while j < k and (j + 1) * n <= (i + 1) * k:
                    out.append(extra[j])
                    j += 1
            out.extend(extra[j:])
            return out

        for ti in range(min(4, nti)):
            loadx(ti)
        stageA(0)
        if nti > 1:
            stageA(1)
        for t0 in range(0, nti, 2):
            recs = []
            for ti in (t0, t0 + 1):
                if ti < nti:
                    P.begin_record()
                    stageB(ti)
                    recs.append(P.end_record())
            P.begin_record()
            for ti in (t0 + 2, t0 + 3):
                if ti < nti:
                    stageA(ti)
                if ti + 2 < nti:
                    loadx(ti + 2)
            recA = P.end_record()
            P.replay(sprinkle(weave(recs), recA))
            bi, g0, nt, s, tt = tiles[min(t0 + 1, nti - 1)]
            if tt == nt - 1 and "pnostore" not in self.debug:
                store_block(bi)
        P.barrier()
        self.release(m)

    def att_cfg(self, mixer):
        if mixer == "A":
            return dict(QT=self.QAT, KT=self.KAT, nqb=2, nkb=2, nh=4, vh0=0, ob0=0, rows=128, scale=32 ** -0.5)
        if mixer == "B":
            return dict(QT=self.QBT, KT=self.KBT, nqb=2, nkb=1, nh=2, vh0=4, ob0=2, rows=128, scale=64 ** -0.5)
        return dict(QT=self.QDT, KT=self.KDT, nqb=4, nkb=4, nh=4, vh0=6, ob0=6, rows=96, scale=96 ** -0.5)

    def att_prepare(self, mixer):
        c = self.att_cfg(mixer)
        nkb, nh, rows, vh0, KT = c["nkb"], c["nh"], c["rows"], c["vh0"], c["KT"]
        kt = self.alloc(nkb * T, BF16)
        kt3 = kt.rearrange("p (b t) -> p b t", t=T)
        vv = self.alloc(NT * nh * 128, BF16)
        vv4 = vv.rearrange("p (t h c) -> p t h c", h=nh, c=128)
        for b in range(nkb):
            self.ld("sp" if b % 2 == 0 else "act", kt3[0:rows, b, :], KT[b, 0:rows, :], ["KT"], ["kt" + mixer], partial=True)
        for q4 in range(0, NT, 9):
            n = min(9, NT - q4)
            self.ld("act" if (q4 // 9) % 2 == 0 else "sp", vv4[:, q4:q4 + n, :, :],
                    self.VV[q4 * 128:(q4 + n) * 128, vh0 * 128:(vh0 + nh) * 128].rearrange("(t p) (h c) -> p t h c", p=128, c=128),
                    ["VV"], ["vv" + mixer], partial=True)
        return kt3, vv4

    def att_shared(self):
        sh = {}
        sh["qt"] = [self.alloc(4096, BF16) for _ in range(2)]
        sh["pt"] = [self.alloc(512, BF16) for _ in range(4)]
        sh["ostg"] = [self.alloc(2 * 512, BF16) for _ in range(2)]
        sh["rl"] = [self.alloc(512) for _ in range(2)]
        for nm in ("d1", "d2", "dff", "sq", "lnv"):
            sh[nm] = self.alloc(512)
        sh["lamt"] = self.alloc(128)
        sh["lams"] = self.alloc(8)
        sh["gsub"] = self.alloc(2)
        return sh

    def phase_att_all(self, l, do_ctx):
        P = self.P
        m = self.mark()
        self.P.phase = 'attA%d' % l
        preps = {}
        preps["A"] = self.att_prepare("A")
        P.begin_record()
        for mixer in ("B", "D"):
            preps[mixer] = self.att_prepare(mixer)
        deferred = P.end_record()
        sh = self.att_shared()
        for mixer in ("A", "B", "D"):
            self.phase_att(l, mixer, do_ctx, preps[mixer], sh, deferred if mixer == "A" else None)
        P.barrier()
        self.release(m)

    def phase_att(self, l, mixer, do_ctx, prep, sh, deferred=None):
        self.P.phase = 'att%s%d' % (mixer, l)
        P = self.P
        I = self.inp
        lam_init = 0.8 - 0.6 * math.exp(-0.3 * l)
        c = self.att_cfg(mixer)
        QT, KT, nqb, nkb, nh, vh0, ob0, rows, scale = (c["QT"], c["KT"], c["nqb"], c["nkb"], c["nh"], c["vh0"], c["ob0"],
                                                       c["rows"], c["scale"])
        if mixer == "A":
            maps = [(mm_ // 4, mm_ // 4, 32 * (mm_ % 4), 32, mm_ // 2) for mm_ in range(8)]
        elif mixer == "B":
            maps = [(j, 0, 64 * i, 64, i) for j in range(2) for i in range(2)]
            true_head = [0, 2, 1, 3]
        else:
            maps = [(h, h, 0, 96, h) for h in range(4)]
        kt3, vv4 = prep
        ktn = "kt%s" % mixer
        vvn = "vv%s" % mixer
        nmask = {"A": 4, "B": 2}.get(mixer, 1)
        qt = [q[:, 0:nqb * nmask * 512] for q in sh["qt"]]
        if nmask > 1:
            for sl in range(2):
                P.op("pool", lambda e, sl=sl: e.memset(qt[sl], 0.0), writes=["qt%d" % sl])
        NPT = 4
        pt, ostg, rl = sh["pt"], sh["ostg"], sh["rl"]
        d1, d2, dff, sq, lnv, lamt, lams, gsub = (sh["d1"], sh["d2"], sh["dff"], sh["sq"], sh["lnv"], sh["lamt"], sh["lams"],
                                                  sh["gsub"])
        if mixer == "A":
            self.ld("sp", lamt, I["lam_vecs"][l:l + 1].rearrange("o a d -> o (a d)").partition_broadcast(128), [], ["lamt"])
            lt3 = lamt.rearrange("p (a d) -> p a d", d=32)
            self.tt("dve", d1[:, 0:32], lt3[:, 0, :], lt3[:, 1, :], ALU.mult, ["lamt"], ["d1"])
            self.tt("dve", d1[:, 32:64], lt3[:, 2, :], lt3[:, 3, :], ALU.mult, ["lamt"], ["d1"], partial=True)
            P.op("dve", lambda e: e.tensor_reduce(out=lams[:, 0:2], in_=d1[:, 0:64].rearrange("p (a d) -> p a d", d=32),
                                                  op=ALU.add, axis=AX.X), reads=["d1"], writes=["lams"])
            self.act(lams[:, 2:4], lams[:, 0:2], AF.Exp, ["lams"], ["lams"])
            self.tt("dve", lams[:, 4:5], lams[:, 3:4], lams[:, 2:3], ALU.subtract, ["lams"], ["lams"])
            self.ts("dve", lams[:, 5:6], lams[:, 4:5], -lam_init, ALU.add, ["lams"], ["lams"])
            self.ld("sp", gsub[0:64, 0:1], I["g_subln"][l].rearrange("(d o) -> d o", o=1), [], ["gsub"])
            self.ts("dve", gsub[0:64, 1:2], gsub[0:64, 0:1], 1.0 - lam_init, ALU.mult, ["gsub"], ["gsub"])
        neg_lam = lams[0:64, 5:6]
        gcol = gsub[0:64, 1:2]

        chunks = []
        if do_ctx:
            chunks.append((0, 256, 0, 2))
        for i in range(8):
            chunks.append((256 + 512 * i, 512, 0, NT))

        def load_q(ci):
            t0, nq, k0, k1 = chunks[ci]
            sl = ci % 2
            q3 = qt[sl].rearrange("p (b t) -> p b t", t=512)
            if nmask == 1:
                self.ld("sp", q3[0:rows, :, 0:nq], QT[:, 0:rows, t0:t0 + nq].rearrange("b p t -> p b t"), ["QT"], ["qt%d" % sl])
            else:
                rw = 128 // nmask
                k = 0
                for b in range(nqb):
                    for mi in range(nmask):
                        self.ld("sp", q3[mi * rw:(mi + 1) * rw, b * nmask + mi, 0:nq],
                                QT[b, mi * rw:(mi + 1) * rw, t0:t0 + nq], ["QT"], ["qt%d" % sl], partial=True)
                        k += 1

        LA = 2
        SB = 4
        EPD = 24 if mixer == "A" else 8
        pending = []
        state = {"step": 0, "accset": 0, "ep": 0}

        def do_chunk(ci):
            t0, nq, k0, k1 = chunks[ci]
            sl = ci % 2
            q3 = qt[sl].rearrange("p (b t) -> p b t", t=512)
            qn = "qt%d" % sl
            o3 = ostg[sl].rearrange("p (b t) -> p b t", t=512)
            on = "ostg%d" % sl
            steps = []
            for mi, mp in enumerate(maps):
                for k in range(k0, k1):
                    steps.append((mi, k))
            ns = len(steps)
            nk = k1 - k0
            def accbank(mi):
                if mixer == "A":
                    return 4 + 2 * ((mi // 2) % 2) + (mi % 2)
                return 4 + (mi % 4)

            def qk(si):
                mi, k = steps[si]
                qb, kb, r0, K, vh = maps[mi]
                g = state["step"] + si
                bk = g % SB
                if nmask == 1:
                    self.mm(self.bank(bk)[:, 0:nq], kt3[r0:r0 + K, kb, k * 128:(k + 1) * 128], q3[r0:r0 + K, qb, 0:nq],
                            True, True, [ktn, qn], ["psS%d" % bk])
                else:
                    qslot = qb * nmask + r0 // K
                    self.mm(self.bank(bk)[:, 0:nq], kt3[:, kb, k * 128:(k + 1) * 128], q3[:, qslot, 0:nq],
                            True, True, [ktn, qn], ["psS%d" % bk])

            def ex_pv(si):
                mi, k = steps[si]
                qb, kb, r0, K, vh = maps[mi]
                g = state["step"] + si
                bk = g % SB
                pi = g % NPT
                self.act(pt[pi][:, 0:nq], self.bank(bk)[:, 0:nq], AF.Exp, ["psS%d" % bk], ["pt%d" % pi], scale=scale)
                ab = accbank(mi)
                if k == k0:
                    last = -1
                    for pi_, pe_ in enumerate(pending):
                        if pe_[2] == ab:
                            last = pi_
                    for _ in range(last + 1):
                        P.replay(pending.pop(0)[1])
                self.mm(self.bank(ab)[:, 0:nq], vv4[:, k, vh, :], pt[pi][:, 0:nq], k == k0, k == k1 - 1,
                        [vvn, "pt%d" % pi], ["psA%d" % ab])
                gstep = state["step"] + si
                if k == k1 - 1:
                    P.begin_record()
                    epilogue(mi)
                    rec = P.end_record()
                    tail = []
                    if mixer == "A" and mi % 2 == 1:
                        tail = rec[-3:]
                        rec = rec[:-3]
                    P.begin_record()
                    if si == ns - 1:
                        self.ld("sp", self.OT[ob0:ob0 + 2, :, t0:t0 + nq].rearrange("b p t -> p b t"), o3[:, :, 0:nq], [on], ["OTw"])
                    tail = tail + P.end_record()
                    pending.append((gstep + EPD, rec, ab))
                    if tail:
                        pending.append((gstep + EPD + 20, tail, ab))
                while pending and pending[0][0] <= gstep:
                    P.replay(pending.pop(0)[1])

            def epilogue(mi):
                ab = accbank(mi)
                an = "psA%d" % ab
                acc = self.bank(ab)
                ri = state["ep"] % 2
                state["ep"] += 1
                rln = "rl%d" % ri
                P.op("dve", lambda e: e.reciprocal(out=rl[ri][0:64, 0:nq], in_=acc[64:128, 0:nq]), reads=[an], writes=[rln])
                if mixer == "A":
                    h = mi // 2
                    dst = d1 if mi % 2 == 0 else d2
                    dn = "d1" if mi % 2 == 0 else "d2"
                    self.tt("dve", dst[0:64, 0:nq], acc[0:64, 0:nq], rl[ri][0:64, 0:nq], ALU.mult, [an, rln], [dn])
                    if mi % 2 == 1:
                        self.stt(dff[0:64, 0:nq], d2[0:64, 0:nq], neg_lam, d1[0:64, 0:nq], ALU.mult, ALU.add,
                                 ["d1", "d2", "lams"], ["dff"])
                        self.tt("pool", sq[0:64, 0:nq], dff[0:64, 0:nq], dff[0:64, 0:nq], ALU.mult, ["dff"], ["sq"])
                        bk = 0
                        g = state["step"]
                        self.mm(self.bank(3)[0:64, 0:nq], self.ones_f[0:64, 0:64], sq[0:64, 0:nq], True, True,
                                ["sq", "consts"], ["psS3"])
                        self.act(lnv[0:64, 0:nq], self.bank(3)[0:64, 0:nq], AF.Ln, ["psS3", "consts"], ["lnv"],
                                 scale=1.0 / 64, bias=self.epsb[0:64, 0:1])
                        self.act(lnv[0:64, 0:nq], lnv[0:64, 0:nq], AF.Exp, ["lnv"], ["lnv"], scale=-0.5)
                        orow = 64 * (h % 2)
                        self.stt(o3[orow:orow + 64, h // 2, 0:nq], dff[0:64, 0:nq], gcol, lnv[0:64, 0:nq], ALU.mult, ALU.mult,
                                 ["dff", "gsub", "lnv"], [on], partial=True)
                else:
                    h = true_head[mi] if mixer == "B" else mi
                    orow = 64 * (h % 2)
                    self.tt("dve", o3[orow:orow + 64, h // 2, 0:nq], acc[0:64, 0:nq], rl[ri][0:64, 0:nq], ALU.mult,
                            [an, rln], [on], partial=True)

            for si in range(min(LA, ns)):
                qk(si)
            for si in range(ns):
                if si + LA < ns:
                    qk(si + LA)
                ex_pv(si)
            state["step"] += ns

        if mixer == "A":
            SB = 3
        load_q(0)
        for ci in range(len(chunks)):
            if ci + 1 < len(chunks):
                load_q(ci + 1)
            do_chunk(ci)
            if ci == 0 and deferred:
                P.replay(deferred)
        while pending:
            P.replay(pending.pop(0)[1])

    def phase_out(self, l, tiles0, ntiles, stream_of):
        self.P.phase = 'out%d' % l
        P = self.P
        I = self.inp
        m = self.mark()
        wo = self.alloc(8 * D, BF16)
        wo3 = wo.rearrange("p (c n) -> p c n", n=D)
        Gp = self.alloc(D)
        xin = [self.alloc(2 * D) for _ in range(2)]
        ot = [self.alloc(8 * 256, BF16) for _ in range(2)]
        tmp = [self.alloc(D) for _ in range(2)]
        junk = self.alloc(D, BF16)
        st = [self.alloc(8) for _ in range(2)]
        stg = [self.alloc(1024) for _ in range(6)]
        self.stg_i = 0
        wv = I["w_out"][l].rearrange("(c p) n -> p c n", p=128)
        for c in range(8):
            self.load_cast(stg, wo3[:, c, :], wv[:, c, :], "wo")
        blocks = [(tiles0 + 2 * i) for i in range(ntiles // 2)]
        cur = [None]

        def load(i):
            g0 = blocks[i]
            sl = i % 2
            x3 = xin[sl].rearrange("p (t d) -> p t d", d=D)
            self.ld("sp", x3, self.XS[g0 * 128:(g0 + 2) * 128, :].rearrange("(t p) d -> p t d", p=128), ["XSr%d" % g0], ["oxin%d" % sl])
            o3 = ot[sl].rearrange("p (b t) -> p b t", t=256)
            self.ld("sp", o3, self.OT[:, :, g0 * 128:(g0 + 2) * 128].rearrange("b p t -> p b t"), ["OTall"], ["ot%d" % sl])

        def compute(i):
            g0 = blocks[i]
            sl = i % 2
            s = stream_of(g0)
            if cur[0] != s:
                cur[0] = s
                self.ld("sp", Gp, self.DV[l][s, 1, 2:3, :].partition_broadcast(128), ["DV%d" % l], ["Gp"])
            o3 = ot[sl].rearrange("p (b t) -> p b t", t=256)
            xn = "oxin%d" % sl
            stn = "ost%d" % sl
            for t in range(2):
                py = self.bank(2 * ((2 * i + t) % 4), n=2)
                pn = "psY%d" % ((2 * i + t) % 4)
                for half in range(2):
                    for b in range(8):
                        self.mm(py[:, half * 512:(half + 1) * 512], o3[:, b, t * 128:(t + 1) * 128],
                                wo3[:, b, half * 512:(half + 1) * 512], b == 0, b == 7, ["ot%d" % sl, "wo"], [pn])
                self.act(junk, py, AF.Square, [pn], [stn, "junk"], partial=True, accum_out=st[sl][:, t:t + 1])
            self.ts("dve", st[sl][:, 2:4], st[sl][:, 0:2], 1.0 / D, ALU.mult, [stn], [stn], s2=EPS, op1=ALU.add)
            self.tt("pool", st[sl][:, 2:4], st[sl][:, 2:4], self.neghalf[:, 0:2], ALU.pow, [stn, "consts"], [stn])
            for t in range(2):
                py = self.bank(2 * ((2 * i + t) % 4), n=2)
                pn = "psY%d" % ((2 * i + t) % 4)
                tn = "otmp%d" % t
                self.stt(tmp[t], py, st[sl][:, 2 + t:3 + t], Gp, ALU.mult, ALU.mult, [pn, stn, "Gp"], [tn])
                self.tt("pool", xin[sl][:, t * D:(t + 1) * D], tmp[t], xin[sl][:, t * D:(t + 1) * D], ALU.add,
                        [tn, xn], [xn], partial=True)
            x3 = xin[sl].rearrange("p (t d) -> p t d", d=D)
            self.ld("sp", self.XS[g0 * 128:(g0 + 2) * 128, :].rearrange("(t p) d -> p t d", p=128), x3, [xn], ["XSw%d" % g0])

        n = len(blocks)
        load(0)
        for i in range(n):
            if i + 1 < n:
                load(i + 1)
            compute(i)
        P.barrier()
        self.release(m)


def build(debug=(), stop_after=None):
    kb = KB(debug)
    nc = kb.nc
    inp = {}

    def din(name, shape, dt=F32):
        inp[name] = nc.dram_tensor(name, list(shape), dt, kind="ExternalInput").ap()

    din("x", [SEQ, D])
    din("ctx", [CTX, D])
    din("cvec", [128, 8, 2])
    din("w_ada", [DEPTH, D, 9 * D])
    din("b_ada", [DEPTH, 9 * D])
    din("g_pre", [DEPTH, 3, D])
    din("g_post", [DEPTH, 3, D])
    din("w_ffn1_in", [DEPTH, D, 2 * DFF])
    din("w_ffn1_out", [DEPTH, DFF, D])
    din("w_ffn2_in", [DEPTH, D, 2 * DFF])
    din("w_ffn2_out", [DEPTH, DFF, D])
    din("w_in_p", [DEPTH, D, INW])
    din("w_out", [DEPTH, D, D])
    din("w_uq", [DEPTH, 256, 384])
    din("w_ukv", [DEPTH, 128, 512])
    din("wspT", [DEPTH, 128, 4, 128])
    din("bspT", [DEPTH, 128, 4])
    din("gqk", [DEPTH, 384])
    din("ln_g", [DEPTH, 256])
    din("ln_b", [DEPTH, 256])
    din("g_q_a", [DEPTH, 256])
    din("g_kv_a", [DEPTH, 128])
    din("lam_vecs", [DEPTH, 4, 32])
    din("g_subln", [DEPTH, 64])
    din("rope", [SEQ, 192])
    kb.inp = inp
    out = nc.dram_tensor("out", [SEQ, D], F32, kind="ExternalOutput").ap()
    kb.XS = kb.dram("XS", [T, D], F32)
    kb.DV = [kb.dram("DV%d" % l, [2, 3, 3, D], F32) for l in range(DEPTH)]
    kb.QAT = kb.dram("QAT", [2, 128, T], BF16)
    kb.KAT = kb.dram("KAT", [2, 128, T], BF16)
    kb.QBT = kb.dram("QBT", [2, 128, T], BF16)
    kb.KBT = kb.dram("KBT", [1, 128, T], BF16)
    kb.QDT = kb.dram("QDT", [4, 128, T], BF16)
    kb.KDT = kb.dram("KDT", [4, 128, T], BF16)
    kb.OT = kb.dram("OT", [8, 128, T], BF16)
    kb.VV = kb.dram("VV", [T, 1280], BF16)

    def done():
        kb.P.emit()
        return kb

    kb.setup_consts()
    for l in range(DEPTH):
        kb.phase_mod(l)
    if stop_after == "mod":
        return done()

    def src0(b):
        return inp["ctx"] if b == 0 else inp["x"][(b - 1) * 256:b * 256, :]

    def xs(b):
        return kb.XS[b * 256:(b + 1) * 256, :]

    def outb(b):
        return out[(b - 1) * 256:b * 256, :]

    allblocks = [(0, 1)] + [(b, 0) for b in range(1, 17)]
    latblocks = [(b, 0) for b in range(1, 17)]
    for l in range(DEPTH):
        need_ctx = (l < DEPTH - 1)
        if "noffn" in kb.debug and l == 0:
            kb.ld("sp", kb.XS[0:CTX, :].rearrange("(p a) d -> p (a d)", p=128), inp["ctx"].rearrange("(p a) d -> p (a d)", p=128), [], ["XScopy"], partial=True)
            for i in range(8):
                kb.ld("sp", kb.XS[CTX + 512 * i:CTX + 512 * (i + 1), :].rearrange("(p a) d -> p (a d)", p=128),
                      inp["x"][512 * i:512 * (i + 1), :].rearrange("(p a) d -> p (a d)", p=128), [], ["XScopy"], partial=True)
            kb.P.barrier()
        else:
            kb.phase_ffn(l, 0, src0 if l == 0 else xs, xs, allblocks)
        if stop_after == "ffn%d" % l:
            return done()
        kb.phase_proj(l, need_ctx)
        if stop_after == "proj%d" % l:
            return done()
        kb.phase_att_all(l, need_ctx)
        if stop_after == "attD%d" % l:
            return done()
        if need_ctx:
            kb.phase_out(l, 0, NT, lambda g: 1 if g < 2 else 0)
        else:
            kb.phase_out(l, 2, NT - 2, lambda g: 0)
        if stop_after == "out%d" % l:
            return done()
        if need_ctx:
            kb.phase_ffn(l, 2, xs, xs, allblocks)
        else:
            kb.phase_ffn(l, 2, xs, outb, latblocks)
        if stop_after == "ffnb%d" % l:
            return done()
    return done()


def _rope_table():
    def tabs(rot_dim):
        rows = np.repeat(np.arange(SEQ // 64, dtype=np.float32), 64)
        cols = np.tile(np.arange(64, dtype=np.float32), SEQ // 64)
        axis_dim = rot_dim // 2
        inv_freq = (np.float32(10000.0) ** (-np.arange(0, axis_dim, 2, dtype=np.float32) / np.float32(axis_dim))).astype(np.float32)
        ar = rows[:, None] * inv_freq[None, :]
        ac = cols[:, None] * inv_freq[None, :]
        cr, sr, cc, sc = np.cos(ar), np.sin(ar), np.cos(ac), np.sin(ac)
        C = np.concatenate([cr, cr, cc, cc], axis=1)
        S = np.concatenate([-sr, sr, -sc, sc], axis=1)
        return C.astype(np.float32), S.astype(np.float32)
    C32, S32 = tabs(32)
    C64, S64 = tabs(64)
    return np.ascontiguousarray(np.concatenate([C32, S32, C64, S64], axis=1).astype(np.float32))


def _w_in_perm():
    o = np.arange(INW)
    qb = 768 + np.concatenate([np.arange(0, 64), np.arange(128, 192), np.arange(64, 128), np.arange(192, 256)])
    return np.concatenate([o[0:256], o[256:512], qb, o[1024:1152], o[1152:1280], o[512:768], o[1280:2208]])


def host_inputs(inputs):
    f = lambda a: np.ascontiguousarray(np.asarray(a, dtype=np.float32))
    shared = {}
    for k in ["w_ada", "b_ada", "g_pre", "g_post", "w_ffn1_in", "w_ffn1_out", "w_ffn2_in", "w_ffn2_out",
              "w_out", "w_uq", "w_ukv", "ln_g", "ln_b", "g_q_a", "g_kv_a", "lam_vecs", "g_subln"]:
        shared[k] = f(inputs[k])
    shared["w_in_p"] = f(np.asarray(inputs["w_in"])[:, :, _w_in_perm()])
    shared["wspT"] = f(np.asarray(inputs["w_spatial"]).transpose(0, 3, 1, 2))
    shared["bspT"] = f(np.asarray(inputs["b_spatial"]).transpose(0, 2, 1))
    gq = np.asarray(inputs["g_qnorm"])
    gk = np.asarray(inputs["g_knorm"])
    shared["gqk"] = f(np.concatenate([gq, gq, gq, gq, gk, gk], axis=1))
    shared["rope"] = _rope_table()
    x = np.asarray(inputs["x"])
    ctx = np.asarray(inputs["ctx"])
    c = np.asarray(inputs["c"])
    cc = np.asarray(inputs["c_ctx"])
    maps = []
    for b in range(x.shape[0]):
        mm = dict(shared)
        mm["x"] = f(x[b])
        mm["ctx"] = f(ctx[b])
        mm["cvec"] = f(np.stack([c[b].reshape(8, 128).T, cc.reshape(8, 128).T], axis=-1))
        maps.append(mm)
    return maps


_CACHE = {}


def kernel(**inputs):
    if "kb" not in _CACHE:
        _CACHE["kb"] = build()
    kb = _CACHE["kb"]
    maps = host_inputs(inputs)
    res = run_bass_kernel_spmd(kb.nc, maps, core_ids=list(range(len(maps))))
    return np.stack([np.asarray(r["out"], dtype=np.float32) for r in res.results], axis=0)
```

```python
import math
from contextlib import ExitStack
import numpy as np
import ml_dtypes
import concourse.bass as bass
import concourse.mybir as mybir
from concourse.bass_utils import run_bass_kernel_spmd

AF = mybir.ActivationFunctionType
ALU = mybir.AluOpType
AX = mybir.AxisListType
F32 = mybir.dt.float32
BF16 = mybir.dt.bfloat16

N_DMA_SEMS = 48

D = 1024
SEQ = 4096
CTX = 256
T = SEQ + CTX
NT = T // 128
DFF = 2816
NFF = DFF // 128
DEPTH = 2
EPS = 1e-6
INW = 2208


class Buf:
    __slots__ = ("name", "writers", "readers")

    def __init__(self, name):
        self.name = name
        self.writers = []
        self.readers = []


class Instr:
    __slots__ = ("eng", "fn", "is_dma", "deps", "sig", "idx", "needs_inc", "phase")


class Prog:
    ENGS = ("pe", "act", "dve", "pool", "sp")

    def __init__(self, nc):
        self.nc = nc
        self.instrs = []
        self.bufs = {}
        self.bstart = 0
        self.phase = 'init'
        self.profile = False
        self._rec = None

    def buf(self, name):
        b = self.bufs.get(name)
        if b is None:
            b = Buf(name)
            self.bufs[name] = b
        return b

    def _new(self, eng, fn, is_dma, deps):
        ins = Instr()
        ins.eng = eng
        ins.fn = fn
        ins.is_dma = is_dma
        ins.idx = len(self.instrs)
        ins.needs_inc = False
        ins.sig = None
        ins.phase = self.phase
        ins.deps = sorted(deps)
        for d in ins.deps:
            self.instrs[d].needs_inc = True
        self.instrs.append(ins)
        return ins

    def begin_record(self):
        self._rec = []

    def end_record(self):
        r = self._rec
        self._rec = None
        return r

    def replay(self, ops):
        for o in ops:
            self.op(*o)

    def op(self, eng, fn, reads=(), writes=(), partial=False, is_dma=False):
        if self._rec is not None:
            self._rec.append((eng, fn, tuple(reads), tuple(writes), partial, is_dma))
            return None
        reads = [self.buf(x) for x in reads if x is not None]
        writes = [self.buf(x) for x in writes if x is not None]
        instrs = self.instrs
        deps = set()
        for r in reads:
            deps.update(r.writers)
        for wb in writes:
            deps.update(wb.writers)
            deps.update(wb.readers)
        if eng == "pe" and not is_dma:
            deps = {d for d in deps if instrs[d].eng != "pe" or instrs[d].is_dma}
        ins = self._new(eng, fn, is_dma, deps)
        for r in reads:
            r.readers.append(ins.idx)
        for wb in writes:
            if partial:
                wb.writers.append(ins.idx)
            else:
                wb.writers = [ins.idx]
                wb.readers = []
        return ins

    def dma(self, q, fn, reads=(), writes=(), partial=False):
        return self.op(q, fn, reads, writes, partial, is_dma=True)

    def barrier(self):
        last = {}
        dmas = []
        for ins in self.instrs[self.bstart:]:
            if ins.fn is None:
                continue
            if ins.is_dma:
                dmas.append(ins.idx)
            else:
                last[ins.eng] = ins.idx
        for e in self.ENGS:
            deps = list(last.values()) + dmas
            self._new(e, None, False, deps)
        self.bstart = len(self.instrs)
        for b in self.bufs.values():
            b.writers = []
            b.readers = []

    def emit(self):
        nc = self.nc
        with ExitStack() as es:
            esem = {e: es.enter_context(nc.semaphore("s_" + e)) for e in self.ENGS}
            dsems = [es.enter_context(nc.semaphore("d%d" % i)) for i in range(N_DMA_SEMS)]
            dval = [0] * N_DMA_SEMS
            ecnt = {e: 0 for e in self.ENGS}
            k = 0
            pre = {}
            for ins in self.instrs:
                if ins.is_dma:
                    si = k % N_DMA_SEMS
                    k += 1
                    pre[ins.idx] = (dsems[si], dval[si])
                    dval[si] += 16
                    ins.sig = (dsems[si], dval[si])
                elif ins.needs_inc:
                    ecnt[ins.eng] += 1
                    ins.sig = (esem[ins.eng], ecnt[ins.eng])
            streams = {e: [i for i in self.instrs if i.eng == e] for e in self.ENGS}
            instrs = self.instrs

            def run(e, name):
                waited = {}

                def wait(sem, val):
                    if val <= 0:
                        return
                    key = id(sem)
                    if waited.get(key, 0) >= val:
                        return
                    e.wait_ge(sem, val)
                    waited[key] = val

                cur = [None, None]
                for ins in streams[name]:
                    if self.profile and ins.fn is not None and ins.phase != cur[0]:
                        if cur[0] is not None:
                            nc.leave_named_scope(cur[0], cur[1], False)
                        cur[0] = ins.phase
                        cur[1], _ = nc.enter_named_scope(ins.phase, False)
                    need = {}
                    for d in ins.deps:
                        s, v = instrs[d].sig
                        k2 = id(s)
                        if k2 not in need or need[k2][1] < v:
                            need[k2] = (s, v)
                    for s, v in need.values():
                        wait(s, v)
                    if ins.fn is None:
                        continue
                    if ins.is_dma:
                        s, v = pre[ins.idx]
                        wait(s, v)
                    r = ins.fn(e)
                    if ins.is_dma:
                        r.then_inc(ins.sig[0], 16)
                    elif ins.needs_inc:
                        r.then_inc(ins.sig[0], 1)
                if cur[0] is not None:
                    nc.leave_named_scope(cur[0], cur[1], False)

            with nc.Block() as block:
                @block.sync
                def _(e):
                    run(e, "sp")

                @block.tensor
                def _(e):
                    run(e, "pe")

                @block.scalar
                def _(e):
                    run(e, "act")

                @block.vector
                def _(e):
                    run(e, "dve")

                @block.gpsimd
                def _(e):
                    run(e, "pool")


ARENA_WORDS = 53100


class KB:
    def __init__(self, debug=()):
        self.debug = set(debug)
        nc = bass.Bass("TRN2", target_bir_lowering=False)
        self.nc = nc
        self.P = Prog(nc)
        self.P.profile = 'profile' in self.debug
        self.es = ExitStack()
        self.arena = self.es.enter_context(nc.sbuf_tensor("arena", [128, ARENA_WORDS], F32))
        self.psum = self.es.enter_context(nc.psum_tensor("psum", [128, 4096], F32))
        self.off = 0
        self.uid = 0

    def alloc(self, nfree, dt=F32):
        nb = nfree * (2 if dt == BF16 else 4)
        nw = (nb + 3) // 4
        nw = (nw + 7) // 8 * 8
        assert self.off + nw <= ARENA_WORDS, ("SBUF arena overflow", self.off, nw)
        a = self.arena[:, self.off:self.off + nw]
        self.off += nw
        if dt == BF16:
            a = a.bitcast(BF16)[:, 0:nfree]
        else:
            a = a[:, 0:nfree]
        return a

    def mark(self):
        return self.off

    def release(self, m):
        self.off = m

    def bank(self, i, dt=F32, n=1):
        a = self.psum[:, 512 * i:512 * (i + n)]
        if dt == BF16:
            a = a.bitcast(BF16)
        return a

    def dram(self, name, shape, dt, kind="Internal"):
        if name in self.debug:
            kind = "ExternalOutput"
        return self.nc.dram_tensor(name, list(shape), dt, kind=kind).ap()

    def name(self, p):
        self.uid += 1
        return "%s_%d" % (p, self.uid)

    def mm(self, out, lhsT, rhs, start, stop, reads, writes, **kw):
        self.P.op("pe", lambda e: e.matmul(out, lhsT=lhsT, rhs=rhs, start=start, stop=stop, **kw),
                  reads=reads, writes=writes, partial=True)

    def tr(self, out, in_, reads, writes):
        ident = self.ident
        self.P.op("pe", lambda e: e.transpose(out=out, in_=in_, identity=ident),
                  reads=list(reads) + ["ident"], writes=writes, partial=True)

    def act(self, out, in_, func, reads, writes, partial=False, eng="act", **kw):
        self.P.op(eng, lambda e: e.activation(out=out, in_=in_, func=func, **kw),
                  reads=reads, writes=writes, partial=partial)

    def tt(self, eng, out, in0, in1, op, reads, writes, partial=False):
        self.P.op(eng, lambda e: e.tensor_tensor(out=out, in0=in0, in1=in1, op=op),
                  reads=reads, writes=writes, partial=partial)

    def ts(self, eng, out, in0, s1, op0, reads, writes, s2=None, op1=None, partial=False):
        if op1 is None:
            self.P.op(eng, lambda e: e.tensor_scalar(out=out, in0=in0, scalar1=s1, scalar2=None, op0=op0),
                      reads=reads, writes=writes, partial=partial)
        else:
            self.P.op(eng, lambda e: e.tensor_scalar(out=out, in0=in0, scalar1=s1, scalar2=s2, op0=op0, op1=op1),
                      reads=reads, writes=writes, partial=partial)

    def stt(self, out, in0, scalar, in1, op0, op1, reads, writes, partial=False):
        self.P.op("dve", lambda e: e.scalar_tensor_tensor(out=out, in0=in0, scalar=scalar, in1=in1, op0=op0, op1=op1),
                  reads=reads, writes=writes, partial=partial)

    def cp(self, eng, out, in_, reads, writes, partial=False):
        if eng == "act":
            self.P.op("act", lambda e: e.copy(out=out, in_=in_), reads=reads, writes=writes, partial=partial)
        else:
            self.P.op(eng, lambda e: e.tensor_copy(out=out, in_=in_), reads=reads, writes=writes, partial=partial)

    def ld(self, q, out, in_, reads, writes, partial=False, **kw):
        self.P.dma(q, lambda e: e.dma_start(out=out, in_=in_, **kw), reads=reads, writes=writes, partial=partial)

    def rstd(self, out, ss, n, reads_name, scale):
        self.ts("dve", out, ss, scale, ALU.mult, reads=[reads_name], writes=[reads_name], s2=EPS, op1=ALU.add)
        nh = self.neghalf[:, 0:n]
        self.tt("pool", out, out, nh, ALU.pow, reads=[reads_name, "consts"], writes=[reads_name])

    def load_cast(self, stg, dst, src, wname, n3=None):
        i = self.stg_i
        self.stg_i += 1
        sl = i % len(stg)
        n = dst.shape[-1] if n3 is None else dst.shape[-1] * (dst.shape[-2] if len(dst.shape) > 2 else 1)
        sv = stg[sl][:, 0:n]
        sn = "stg%d_%d" % (id(stg) % 1000, sl)
        if n3 is not None:
            sv3 = sv.rearrange("p (a n) -> p a n", n=n3)
            self.ld("sp" if i % 2 == 0 else "act", sv3, src, [], [sn])
            d3 = dst if len(dst.shape) > 2 else dst.rearrange("p (a n) -> p a n", n=n3)
            self.cp(("dve", "pool", "dve")[i % 3], d3, sv3, [sn], [wname], partial=True)
        else:
            self.ld("sp" if i % 2 == 0 else "act", sv, src, [], [sn])
            self.cp(("dve", "pool", "dve")[i % 3], dst, sv, [sn], [wname], partial=True)

    def setup_consts(self):
        P = self.P
        self.ident = self.alloc(128, BF16)
        self.neghalf = self.alloc(16, F32)
        self.ones_f = self.alloc(64, F32)
        self.ones_b = self.alloc(128, BF16)
        self.epsb = self.alloc(1, F32)
        m = self.mark()
        idf = self.alloc(128, F32)
        ident, nh, of, ob, epsb = self.ident, self.neghalf, self.ones_f, self.ones_b, self.epsb
        P.op("pool", lambda e: e.memset(idf, 0.0), writes=["idf"])
        P.op("pool", lambda e: e.affine_select(out=idf, in_=idf, pattern=[[-1, 128]], compare_op=ALU.not_equal,
                                               fill=1.0, base=0, channel_multiplier=1), reads=["idf"], writes=["idf"])
        P.op("dve", lambda e: e.tensor_copy(out=ident, in_=idf), reads=["idf"], writes=["ident"])
        P.op("pool", lambda e: e.memset(nh, -0.5), writes=["consts"])
        P.op("pool", lambda e: e.memset(of, 1.0), writes=["consts"], partial=True)
        P.op("pool", lambda e: e.memset(ob, 1.0), writes=["consts"], partial=True)
        P.op("pool", lambda e: e.memset(epsb, EPS), writes=["consts"], partial=True)
        P.barrier()
        self.release(m)

    def phase_mod(self, l):
        self.P.phase = 'mod%d' % l
        P = self.P
        I = self.inp
        m = self.mark()
        cv = self.alloc(16, F32)
        sc = self.alloc(16, F32)
        cv3 = cv.rearrange("p (c s) -> p c s", s=2)
        sc3 = sc.rearrange("p (c s) -> p c s", s=2)
        modsb = self.alloc(9 * D, F32)
        bada = self.alloc(9 * D, F32)
        gpre = self.alloc(3 * D, F32)
        gpost = self.alloc(3 * D, F32)
        dv = self.alloc(9 * D, F32)
        NS = 4
        wsl = [self.alloc(8 * 512, F32) for _ in range(NS)]
        self.ld("sp", cv3, I["cvec"], [], ["cv"])
        self.ld("sp", bada[0:2, :], I["b_ada"][l:l + 1, :].partition_broadcast(2), [], ["bada"])
        self.ld("sp", gpre[0:2, :], I["g_pre"][l:l + 1].rearrange("o j d -> o (j d)").partition_broadcast(2), [], ["gpre"])
        self.ld("sp", gpost[0:2, :], I["g_post"][l:l + 1].rearrange("o j d -> o (j d)").partition_broadcast(2), [], ["gpost"])
        self.act(sc, cv, AF.Tanh, ["cv"], ["sc"], scale=0.5)
        self.ts("dve", sc, sc, 1.0, ALU.add, ["sc"], ["sc"], s2=0.5, op1=ALU.mult)
        self.tt("dve", sc, sc, cv, ALU.mult, ["sc", "cv"], ["sc"])
        wada = I["w_ada"][l].rearrange("(c p) n -> p c n", p=128)
        for n in range(18):
            s = n % NS
            w3 = wsl[s].rearrange("p (c n) -> p c n", n=512)
            self.ld("sp" if n % 2 == 0 else "act", w3, wada[:, :, n * 512:(n + 1) * 512], [], ["wsl%d" % s])
            pb = "psm%d" % (n % 2)
            po = self.bank(n % 2)[0:2, :]
            for c in range(8):
                self.mm(po, sc3[:, c, :], w3[:, c, :], c == 0, c == 7, ["sc", "wsl%d" % s], [pb])
            self.tt("dve", modsb[0:2, n * 512:(n + 1) * 512], po, bada[0:2, n * 512:(n + 1) * 512], ALU.add,
                    [pb, "bada"], ["modsb"], partial=True)
        for j in range(3):
            wj = 1.0 if j == 1 else 0.5
            sh = modsb[0:2, (3 * j) * D:(3 * j + 1) * D]
            scl = modsb[0:2, (3 * j + 1) * D:(3 * j + 2) * D]
            gt = modsb[0:2, (3 * j + 2) * D:(3 * j + 3) * D]
            self.stt(dv[0:2, (3 * j) * D:(3 * j + 1) * D], scl, 1.0, gpre[0:2, j * D:(j + 1) * D], ALU.add, ALU.mult,
                     ["modsb", "gpre"], ["dv"], partial=True)
            self.cp("dve", dv[0:2, (3 * j + 1) * D:(3 * j + 2) * D], sh, ["modsb"], ["dv"], partial=True)
            self.stt(dv[0:2, (3 * j + 2) * D:(3 * j + 3) * D], gt, wj, gpost[0:2, j * D:(j + 1) * D], ALU.mult, ALU.mult,
                     ["modsb", "gpost"], ["dv"], partial=True)
        self.ld("sp", self.DV[l].rearrange("s j k d -> s (j k d)"), dv[0:2, :], ["dv"], ["DV%d" % l])
        P.barrier()
        self.release(m)

    def phase_ffn(self, l, j, src, dst, blocks):
        self.P.phase = 'ffn%d_%d' % (l, j)
        P = self.P
        I = self.inp
        m = self.mark()
        w_in = I["w_ffn1_in" if j == 0 else "w_ffn2_in"][l]
        w_out = I["w_ffn1_out" if j == 0 else "w_ffn2_out"][l]
        w1 = self.alloc(8 * 2 * DFF, BF16)
        w2 = self.alloc(NFF * D, BF16)
        w13 = w1.rearrange("p (c n) -> p c n", n=2 * DFF)
        w23 = w2.rearrange("p (c n) -> p c n", n=D)
        mstg = self.mark()
        stg = [self.alloc(2816) for _ in range(6)]
        self.stg_i = 0
        w_in_v = w_in.rearrange("(c p) n -> p c n", p=128)
        for c in range(8):
            for h in range(2):
                self.load_cast(stg, w13[:, c, h * DFF:(h + 1) * DFF], w_in_v[:, c, h * DFF:(h + 1) * DFF], "w1")
        w_out_v = w_out.rearrange("(c p) n -> p c n", p=128)
        for c0 in range(0, NFF, 2):
            self.load_cast(stg, w23[:, c0:c0 + 2, :], w_out_v[:, c0:c0 + 2, :], "w2", n3=D)
        P.barrier()
        self.release(mstg)
        G = self.alloc(D)
        S = self.alloc(D)
        Gp = self.alloc(D)
        xin = [self.alloc(2 * D) for _ in range(2)]
        hb = self.alloc(2 * D, BF16)
        hT = self.alloc(8 * 256, BF16)
        hT3 = hT.rearrange("p (c t) -> p c t", t=256)
        actT = self.alloc(NFF * 256, BF16)
        actT3 = actT.rearrange("p (c t) -> p c t", t=256)
        tmp = [self.alloc(D) for _ in range(2)]
        junk = self.alloc(D, BF16)
        sg = [self.alloc(256) for _ in range(3)]
        st = [self.alloc(8) for _ in range(2)]
        cur_stream = [None, None]

        def load_mod(s):
            if cur_stream[0] == s:
                return
            cur_stream[0] = s
            dvl = self.DV[l]
            self.ld("sp", G, dvl[s, j, 0:1, :].partition_broadcast(128), ["DV%d" % l], ["G"])
            self.ld("sp", S, dvl[s, j, 1:2, :].partition_broadcast(128), ["DV%d" % l], ["S"])

        def load_gp(s):
            if cur_stream[1] == s:
                return
            cur_stream[1] = s
            dvl = self.DV[l]
            self.ld("sp", Gp, dvl[s, j, 2:3, :].partition_broadcast(128), ["DV%d" % l], ["Gp"])

        def load(i):
            b, s = blocks[i]
            sl = i % 2
            x3 = xin[sl].rearrange("p (t d) -> p t d", d=D)
            self.ld("sp", x3, src(b).rearrange("(t p) d -> p t d", p=128), [self.srcname(b)], ["xin%d" % sl])

        def prenorm(i):
            b, s = blocks[i]
            sl = i % 2
            load_mod(s)
            xn = "xin%d" % sl
            stn = "st%d" % sl
            for t in range(2):
                self.act(junk, xin[sl][:, t * D:(t + 1) * D], AF.Square, [xn], [stn, "junk"], partial=True,
                         accum_out=st[sl][:, t:t + 1])
            self.ts("dve", st[sl][:, 2:4], st[sl][:, 0:2], 1.0 / D, ALU.mult, [stn], [stn], s2=EPS, op1=ALU.add)
            self.tt("pool", st[sl][:, 2:4], st[sl][:, 2:4], self.neghalf[:, 0:2], ALU.pow, [stn, "consts"], [stn])
            for t in range(2):
                tn = "tmp%d" % t
                self.stt(tmp[t], xin[sl][:, t * D:(t + 1) * D], st[sl][:, 2 + t:3 + t], G, ALU.mult, ALU.mult,
                         [xn, stn, "G"], [tn])
                self.tt("dve", hb[:, t * D:(t + 1) * D], tmp[t], S, ALU.add, [tn, "S"], ["hb%d" % t])

        def transposes(i):
            for t in range(2):
                pT = self.bank(0, BF16)
                pT3 = pT.rearrange("p (c t) -> p c t", t=128)
                for c in range(8):
                    self.tr(pT3[:, c, :], hb[:, t * D + c * 128:t * D + (c + 1) * 128], ["hb%d" % t], ["psT"])
                self.cp("act", hT3[:, :, t * 128:(t + 1) * 128], pT3, ["psT"], ["hT"], partial=True)

        def mm1(i, side=()):
            side = list(side)
            per = max(1, (len(side) + 13) // 14)
            for f in range(NFF):
                if f >= 3 and side:
                    P.replay(side[:per])
                    side = side[per:]
                bk = 1 + (f % 3)
                pn = "psM%d" % bk
                pm = self.bank(bk)
                for half in range(2):
                    col = half * DFF + f * 128
                    for c in range(8):
                        self.mm(pm[:, half * 256:(half + 1) * 256], w13[:, c, col:col + 128], hT3[:, c, :],
                                c == 0, c == 7, ["w1", "hT"], [pn])
                sgi = f % 3
                self.act(sg[sgi], pm[:, 0:256], AF.Silu, [pn], ["sg%d" % sgi])
                self.tt("dve", actT3[:, f, :], pm[:, 256:512], sg[sgi], ALU.mult, [pn, "sg%d" % sgi], ["actT"], partial=True)
            P.replay(side)

        def mm2_post(i):
            b, s = blocks[i]
            sl = i % 2
            xn = "xin%d" % sl
            stn = "st%d" % sl
            load_gp(s)
            for t in range(2):
                py = self.bank(4 + 2 * t, n=2)
                pn = "psY%d" % t
                for half in range(2):
                    for f in range(NFF):
                        self.mm(py[:, half * 512:(half + 1) * 512], actT3[:, f, t * 128:(t + 1) * 128],
                                w23[:, f, half * 512:(half + 1) * 512], f == 0, f == NFF - 1, ["actT", "w2"], [pn])
                self.act(junk, py, AF.Square, [pn], [stn, "junk"], partial=True, accum_out=st[sl][:, 4 + t:5 + t])
            self.ts("dve", st[sl][:, 6:8], st[sl][:, 4:6], 1.0 / D, ALU.mult, [stn], [stn], s2=EPS, op1=ALU.add)
            self.tt("pool", st[sl][:, 6:8], st[sl][:, 6:8], self.neghalf[:, 0:2], ALU.pow, [stn, "consts"], [stn])
            for t in range(2):
                py = self.bank(4 + 2 * t, n=2)
                pn = "psY%d" % t
                tn = "tmp%d" % t
                self.stt(tmp[t], py, st[sl][:, 6 + t:7 + t], Gp, ALU.mult, ALU.mult, [pn, stn, "Gp"], [tn])
                self.tt("dve", xin[sl][:, t * D:(t + 1) * D], tmp[t], xin[sl][:, t * D:(t + 1) * D], ALU.add,
                        [tn, xn], [xn], partial=True)
            x3 = xin[sl].rearrange("p (t d) -> p t d", d=D)
            self.ld("sp", dst(b).rearrange("(t p) d -> p t d", p=128), x3, [xn], [self.dstname(b)])

        n = len(blocks)
        load(0)
        prenorm(0)
        transposes(0)
        for i in range(n):
            if i + 1 < n:
                load(i + 1)
            side = ()
            if i + 1 < n:
                P.begin_record()
                prenorm(i + 1)
                side = P.end_record()
            mm1(i, side)
            if i + 1 < n:
                transposes(i + 1)
            mm2_post(i)
        P.barrier()
        self.release(m)

    def srcname(self, b):
        return "XS%d" % b

    def dstname(self, b):
        return "XS%d" % b


    def bc(self, ap, G):
        shp = list(ap.shape)
        return ap[:, None].to_broadcast([shp[0], G] + shp[1:])

    def rope(self, sfx, eng_a, xv, outv, Ct, St, G, Dh, t1, t2, rn, wn, partial=True):
        w = Dh // 4
        t13 = t1[:, 0:G * Dh].rearrange("p (g d) -> p g d", d=Dh)
        t25 = t2[:, 0:G * Dh].rearrange("p (g a h w) -> p g a h w", a=2, h=2, w=w)
        t23 = t2[:, 0:G * Dh].rearrange("p (g d) -> p g d", d=Dh)
        xv5 = xv.rearrange("p g (a h w) -> p g a h w", a=2, h=2, w=w)
        S4 = St.rearrange("p (a h w) -> p a h w", a=2, h=2, w=w)
        self.tt(eng_a, t13, xv, self.bc(Ct, G), ALU.mult, rn + ["rope"], ["rt1_%d" % sfx])
        self.tt("dve", t25[:, :, :, 0, :], xv5[:, :, :, 1, :], self.bc(S4[:, :, 0, :], G), ALU.mult, rn + ["rope"], ["rt2_%d" % sfx], partial=True)
        self.tt("dve", t25[:, :, :, 1, :], xv5[:, :, :, 0, :], self.bc(S4[:, :, 1, :], G), ALU.mult, rn + ["rope"], ["rt2_%d" % sfx], partial=True)
        self.tt("dve", outv, t13, t23, ALU.add, ["rt1_%d" % sfx, "rt2_%d" % sfx], wn, partial=partial)

    def phase_proj(self, l, need_q_ctx):
        self.P.phase = 'proj%d' % l
        P = self.P
        I = self.inp
        m = self.mark()
        NW = INW
        w_in = self.alloc(8 * NW, BF16)
        w_in3 = w_in.rearrange("p (c n) -> p c n", n=NW)
        w_uq = self.alloc(2 * 384, BF16)
        w_uq3 = w_uq.rearrange("p (c n) -> p c n", n=384)
        w_ukv = self.alloc(512, BF16)
        wsp = self.alloc(4 * 128, BF16)
        wsp3 = wsp.rearrange("p (g n) -> p g n", n=128)
        G = self.alloc(D)
        S = self.alloc(D)
        gqk = self.alloc(384)
        lng = self.alloc(256)
        lnb = self.alloc(256)
        gqa = self.alloc(256)
        gkva = self.alloc(128)
        bsp = self.alloc(4)
        xt = [self.alloc(D) for _ in range(4)]
        rts = [self.alloc(192) for _ in range(6)]
        hb = self.alloc(D, BF16)
        hT = self.alloc(8 * 128, BF16)
        hT3 = hT.rearrange("p (c t) -> p c t", t=128)
        pj = [self.alloc(NW) for _ in range(4)]
        t1_s = [self.alloc(512) for _ in range(2)]
        t2_s = [self.alloc(512) for _ in range(2)]
        ta = self.alloc(512)
        tb = self.alloc(512)
        tB_s = [self.alloc(384) for _ in range(2)]
        tC_s = [self.alloc(512) for _ in range(2)]
        tD_s = [self.alloc(256) for _ in range(2)]
        guv_s = [self.alloc(512) for _ in range(2)]
        junk = self.alloc(D, BF16)
        qkA_s = [self.alloc(512, BF16) for _ in range(2)]
        qkB_s = [self.alloc(384, BF16) for _ in range(2)]
        vnb_s = [self.alloc(256, BF16) for _ in range(2)]
        cl_s = [self.alloc(256, BF16) for _ in range(2)]
        cn_s = [self.alloc(384, BF16) for _ in range(2)]
        cT_s = [self.alloc(3 * 128, BF16) for _ in range(2)]
        qDb_s = [self.alloc(4 * 96, BF16) for _ in range(2)]
        kDb_s = [self.alloc(4 * 96, BF16) for _ in range(2)]
        krr_s = [self.alloc(32) for _ in range(2)]
        qDf_s = [self.alloc(128) for _ in range(2)]
        st = [self.alloc(32) for _ in range(4)]
        mstg = self.mark()
        stg = [self.alloc(2208) for _ in range(6)]
        self.stg_i = 0
        wv = I["w_in_p"][l].rearrange("(c p) n -> p c n", p=128)
        for c in range(8):
            self.load_cast(stg, w_in3[:, c, :], wv[:, c, :], "w_in")
        for c in range(2):
            self.load_cast(stg, w_uq3[:, c, :], I["w_uq"][l][c * 128:(c + 1) * 128, :], "w_uq")
        self.load_cast(stg, w_ukv, I["w_ukv"][l], "w_ukv")
        self.load_cast(stg, wsp, I["wspT"][l], "wsp", n3=128)
        self.ld("sp", gqk, I["gqk"][l:l + 1, :].partition_broadcast(128), [], ["par"], partial=True)
        self.ld("sp", lng, I["ln_g"][l:l + 1, :].partition_broadcast(128), [], ["par"], partial=True)
        self.ld("sp", lnb, I["ln_b"][l:l + 1, :].partition_broadcast(128), [], ["par"], partial=True)
        self.ld("sp", gqa, I["g_q_a"][l:l + 1, :].partition_broadcast(128), [], ["par"], partial=True)
        self.ld("sp", gkva, I["g_kv_a"][l:l + 1, :].partition_broadcast(128), [], ["par"], partial=True)
        self.ld("sp", bsp, I["bspT"][l], [], ["par"], partial=True)
        P.barrier()
        self.release(mstg)
        stgT = [self.alloc(17 * 512, BF16) for _ in range(2)]
        vst = [self.alloc(4 * 1280, BF16) for _ in range(2)]
        for sl in range(2):
            v4 = vst[sl].rearrange("p (t h c) -> p t h c", h=10, c=128)
            P.op("pool", lambda e, v4=v4: e.memset(v4[:, :, :, 64:128], 1.0), writes=["vst%d" % sl], partial=True)

        blocks = [(0, 2, 1)] + [(2 + 4 * i, 4, 0) for i in range(8)]
        tiles = []
        for bi, (g0, nt, s) in enumerate(blocks):
            for tt in range(nt):
                tiles.append((bi, g0, nt, s, tt))
        cur = [None]

        def loadx(ti):
            bi, g0, nt, s, tt = tiles[ti]
            gt = g0 + tt
            k4 = ti % 4
            self.ld("sp", xt[k4], self.XS[gt * 128:(gt + 1) * 128, :], ["XSall"], ["pxt%d" % k4])
            if s == 0:
                lt = gt - 2
                self.ld("sp", rts[ti % 6], I["rope"][lt * 128:(lt + 1) * 128, :], [], ["rt%d" % (ti % 6)])

        def stageA(ti):
            bi, g0, nt, s, tt = tiles[ti]
            ps = ti % 4
            if cur[0] != s:
                cur[0] = s
                dvl = self.DV[l]
                self.ld("sp", G, dvl[s, 1, 0:1, :].partition_broadcast(128), ["DV%d" % l], ["G"])
                self.ld("sp", S, dvl[s, 1, 1:2, :].partition_broadcast(128), ["DV%d" % l], ["S"])
            xn = "pxt%d" % ps
            stn = "pst%d" % ps
            x = xt[ps]
            sv = st[ps]
            self.act(junk, x, AF.Square, [xn], [stn, "junk"], partial=True, accum_out=sv[:, 0:1])
            self.ts("dve", sv[:, 1:2], sv[:, 0:1], 1.0 / D, ALU.mult, [stn], [stn], s2=EPS, op1=ALU.add)
            self.tt("pool", sv[:, 1:2], sv[:, 1:2], self.neghalf[:, 0:1], ALU.pow, [stn, "consts"], [stn])
            self.stt(ta, x[:, 0:512], sv[:, 1:2], G[:, 0:512], ALU.mult, ALU.mult, [xn, stn, "G"], ["ta"])
            self.tt("pool", hb[:, 0:512], ta, S[:, 0:512], ALU.add, ["ta", "S"], ["hb"], partial=True)
            self.stt(tb, x[:, 512:1024], sv[:, 1:2], G[:, 512:1024], ALU.mult, ALU.mult, [xn, stn, "G"], ["tb"])
            self.tt("pool", hb[:, 512:1024], tb, S[:, 512:1024], ALU.add, ["tb", "S"], ["hb"], partial=True)
            pT = self.bank(0, BF16)
            pT3 = pT.rearrange("p (c t) -> p c t", t=128)
            for c in range(8):
                self.tr(pT3[:, c, :], hb[:, c * 128:(c + 1) * 128], ["hb"], ["psT"])
            self.cp("act", hT3, pT3, ["psT"], ["hT"])
            chunks = [(0, 512), (512, 512), (1024, 512), (1536, 512), (2048, 160)]
            for n, (c0, cw) in enumerate(chunks):
                bk = 1 + (n % 2)
                pn = "psP%d" % bk
                po = self.bank(bk)[:, 0:cw]
                for c in range(8):
                    self.mm(po, hT3[:, c, :], w_in3[:, c, c0:c0 + cw], c == 0, c == 7, ["hT", "w_in"], [pn])
                self.cp("act" if n % 2 == 0 else "dve", pj[ps][:, c0:c0 + cw], po, [pn], ["pj%d" % ps], partial=True)

        def stageB(ti):
            bi, g0, nt, s, tt = tiles[ti]
            sl = bi % 2
            ps = ti % 4
            p = ti % 2
            X = 3 + 2 * p
            Y = 4 + 2 * p
            psX = "ps%d" % X
            psY = "ps%d" % Y
            t1, t2, tB, tC, tD, guv, krr, qDf = t1_s[p], t2_s[p], tB_s[p], tC_s[p], tD_s[p], guv_s[p], krr_s[p], qDf_s[p]
            qkA, qkB, vnb, cl, cn, cT, qDb, kDb = qkA_s[p], qkB_s[p], vnb_s[p], cl_s[p], cn_s[p], cT_s[p], qDb_s[p], kDb_s[p]
            cT3 = cT.rearrange("p (c t) -> p c t", t=128)
            lat = (s == 0)
            latA = lat and 'norA' not in self.debug
            latB = lat and 'norB' not in self.debug
            latDq = lat and 'norDq' not in self.debug
            latDk = lat and 'norDk' not in self.debug
            pjn = "pj%d" % ps
            stn = "pst%d" % ps
            x = pj[ps]
            sv = st[ps]
            rt = rts[ti % 6]
            C32, S32, C64, S64 = rt[:, 0:32], rt[:, 32:64], rt[:, 64:128], rt[:, 128:192]
            rn = ["rt%d" % (ti % 6)]
            vn = "vst%d" % sl
            v4 = vst[sl].rearrange("p (t h c) -> p t h c", h=10, c=128)
            xa = x[:, 0:512].rearrange("p (g d) -> p g d", d=32)
            if latA:
                self.rope(p, "pool", xa, qkA.rearrange("p (g d) -> p g d", d=32), C32, S32, 16, 32, t1, t2, [pjn] + rn, ["qkA%d" % p], partial=False)
            else:
                self.cp("pool", qkA, x[:, 0:512], [pjn], ["qkA%d" % p])
            xb = x[:, 512:896]
            xb3 = xb.rearrange("p (g d) -> p g d", d=64)
            self.act(tB, xb, AF.Square, [pjn], ["tB%d" % p])
            P.op("dve", lambda e: e.tensor_reduce(out=sv[:, 8:14], in_=tB.rearrange("p (g d) -> p g d", d=64),
                                                  op=ALU.add, axis=AX.X), reads=["tB%d" % p], writes=[stn])
            self.ts("dve", sv[:, 8:14], sv[:, 8:14], 1.0 / 64, ALU.mult, [stn], [stn], s2=EPS, op1=ALU.add)
            self.tt("pool", sv[:, 8:14], sv[:, 8:14], self.neghalf[:, 0:6], ALU.pow, [stn, "consts"], [stn])
            tB3 = tB.rearrange("p (g d) -> p g d", d=64)
            self.tt("dve", tB3, xb3, sv[:, 8:14, None].to_broadcast([128, 6, 64]), ALU.mult, [pjn, stn], ["tB%d" % p])
            self.tt("pool", tB, tB, gqk, ALU.mult, ["tB%d" % p, "par"], ["tB%d" % p])
            if latB:
                self.rope(p, "pool", tB3, qkB.rearrange("p (g d) -> p g d", d=64), C64, S64, 6, 64, t1, t2, ["tB%d" % p] + rn, ["qkB%d" % p], partial=False)
            else:
                self.cp("pool", qkB, tB, ["tB%d" % p], ["qkB%d" % p])
            self.cp("act", v4[:, tt, 0:4, 0:64], x[:, 1024:1280].rearrange("p (h c) -> p h c", c=64), [pjn], [vn], partial=True)
            self.cp("act", v4[:, tt, 4:6, 0:64], x[:, 896:1024].rearrange("p (h c) -> p h c", c=64), [pjn], [vn], partial=True)
            xc = x[:, 1280:1792]
            self.act(tC, xc, AF.Square, [pjn], ["tC%d" % p])
            self.ts("dve", tC, tC, 0.044715, ALU.mult, ["tC%d" % p], ["tC%d" % p], s2=1.0, op1=ALU.add)
            self.tt("pool", tC, tC, xc, ALU.mult, ["tC%d" % p, pjn], ["tC%d" % p])
            self.act(tC, tC, AF.Tanh, ["tC%d" % p], ["tC%d" % p], scale=0.7978845608028654)
            self.ts("dve", tC, tC, 1.0, ALU.add, ["tC%d" % p], ["tC%d" % p], s2=0.5, op1=ALU.mult)
            self.tt("dve", guv, tC, xc, ALU.mult, ["tC%d" % p, pjn], ["guv%d" % p])
            gv = guv[:, 256:512]
            self.act(junk[:, 0:256], gv, AF.Identity, ["guv%d" % p], [stn, "junk"], partial=True, accum_out=sv[:, 16:17])
            self.act(junk[:, 256:512], gv, AF.Square, ["guv%d" % p], [stn, "junk"], partial=True, accum_out=sv[:, 17:18])
            self.ts("dve", sv[:, 18:19], sv[:, 16:17], 1.0 / 256, ALU.mult, [stn], [stn])
            self.tt("dve", sv[:, 19:20], sv[:, 18:19], sv[:, 18:19], ALU.mult, [stn], [stn])
            self.stt(sv[:, 20:21], sv[:, 17:18], 1.0 / 256, sv[:, 19:20], ALU.mult, ALU.subtract, [stn], [stn])
            self.ts("dve", sv[:, 20:21], sv[:, 20:21], EPS, ALU.add, [stn], [stn])
            self.tt("pool", sv[:, 20:21], sv[:, 20:21], self.neghalf[:, 0:1], ALU.pow, [stn, "consts"], [stn])
            self.ts("dve", tD, gv, sv[:, 18:19], ALU.subtract, ["guv%d" % p, stn], ["tD%d" % p], s2=sv[:, 20:21], op1=ALU.mult)
            self.tt("pool", tD, tD, lng, ALU.mult, ["tD%d" % p, "par"], ["tD%d" % p])
            self.tt("pool", vnb, tD, lnb, ALU.add, ["tD%d" % p, "par"], ["vnb%d" % p])
            pg = self.bank(X)[:, 256:512]
            for g in range(4):
                self.mm(pg[:, g * 64:(g + 1) * 64], wsp3[:, g, :], vnb[:, g * 64:(g + 1) * 64], True, True, ["wsp", "vnb%d" % p], [psX])
            for g in range(4):
                self.stt(cl[:, g * 64:(g + 1) * 64], pg[:, g * 64:(g + 1) * 64], bsp[:, g:g + 1], guv[:, g * 64:(g + 1) * 64],
                         ALU.add, ALU.mult, [psX, "par", "guv%d" % p], ["cl%d" % p], partial=True)
            self.act(junk[:, 512:768], x[:, 1792:2048], AF.Square, [pjn], [stn, "junk"], partial=True, accum_out=sv[:, 24:25])
            self.act(junk[:, 768:896], x[:, 2048:2176], AF.Square, [pjn], [stn, "junk"], partial=True, accum_out=sv[:, 25:26])
            self.ts("dve", sv[:, 26:27], sv[:, 24:25], 1.0 / 256, ALU.mult, [stn], [stn], s2=EPS, op1=ALU.add)
            self.ts("dve", sv[:, 27:28], sv[:, 25:26], 1.0 / 128, ALU.mult, [stn], [stn], s2=EPS, op1=ALU.add)
            self.tt("pool", sv[:, 26:28], sv[:, 26:28], self.neghalf[:, 0:2], ALU.pow, [stn, "consts"], [stn])
            self.stt(cn[:, 0:256], x[:, 1792:2048], sv[:, 26:27], gqa, ALU.mult, ALU.mult, [pjn, stn, "par"], ["cn%d" % p], partial=True)
            self.stt(cn[:, 256:384], x[:, 2048:2176], sv[:, 27:28], gkva, ALU.mult, ALU.mult, [pjn, stn, "par"], ["cn%d" % p], partial=True)
            pc = self.bank(X, BF16)[:, 0:384]
            pc3 = pc.rearrange("p (c t) -> p c t", t=128)
            for c in range(3):
                self.tr(pc3[:, c, :], cn[:, c * 128:(c + 1) * 128], ["cn%d" % p], [psX])
            self.cp("act", cT3, pc3, [psX], ["cT%d" % p])
            pq = self.bank(Y)[:, 0:384]
            for c in range(2):
                self.mm(pq, cT3[:, c, :], w_uq3[:, c, :], c == 0, c == 1, ["cT%d" % p, "w_uq"], [psY])
            pq3 = pq.rearrange("p (h d) -> p h d", d=96)
            qD3 = qDb.rearrange("p (h d) -> p h d", d=96)
            kD3 = kDb.rearrange("p (h d) -> p h d", d=96)
            qDf3 = qDf.rearrange("p (h d) -> p h d", d=32)
            self.cp("act", qD3[:, :, 0:64], pq3[:, :, 0:64], [psY], ["qDb%d" % p], partial=True)
            if latDq:
                self.cp("act", qDf3, pq3[:, :, 64:96], [psY], ["qDf%d" % p])
            else:
                self.cp("act", qD3[:, :, 64:96], pq3[:, :, 64:96], [psY], ["qDb%d" % p], partial=True)
            pkv = self.bank(Y)
            self.mm(pkv, cT3[:, 2, :], w_ukv, True, True, ["cT%d" % p, "w_ukv"], [psY])
            pkv3 = pkv.rearrange("p (h d) -> p h d", d=128)
            self.cp("act", kD3[:, :, 0:64], pkv3[:, :, 0:64], [psY], ["kDb%d" % p], partial=True)
            self.cp("act", v4[:, tt, 6:10, 0:64], pkv3[:, :, 64:128], [psY], [vn], partial=True)
            if latDq:
                self.rope(p, "pool", qDf3, qD3[:, :, 64:96], C32, S32, 4, 32, t1, t2, ["qDf%d" % p] + rn, ["qDb%d" % p])
            if latDk:
                self.rope(p, "pool", x[:, 2176:2208].rearrange("p (g d) -> p g d", d=32), krr.rearrange("p (g d) -> p g d", d=32),
                          C32, S32, 1, 32, t1, t2, [pjn] + rn, ["krr%d" % p], partial=False)
                self.cp("pool", kD3[:, :, 64:96], self.bc(krr, 4), ["krr%d" % p], ["kDb%d" % p], partial=True)
            else:
                self.cp("pool", kD3[:, :, 64:96], self.bc(x[:, 2176:2208], 4), [pjn], ["kDb%d" % p], partial=True)
            sT = stgT[sl].rearrange("p (b t) -> p b t", t=512)
            sn = "stgT%d" % sl
            c0 = tt * 128
            p7 = self.bank(Y, BF16).rearrange("p (c t) -> p c t", t=128)
            p5 = self.bank(X, BF16)[:, 384:512]
            for k in range(4):
                self.tr(p7[:, k, :], qkA[:, k * 128:(k + 1) * 128], ["qkA%d" % p], [psY])
            for k in range(3):
                self.tr(p7[:, 4 + k, :], qkB[:, k * 128:(k + 1) * 128], ["qkB%d" % p], [psY])
            self.tr(p7[:, 7, :], cl[:, 0:128], ["cl%d" % p], [psY])
            self.tr(p5, cl[:, 128:256], ["cl%d" % p], [psX])
            self.cp("dve", sT[:, 0:8, c0:c0 + 128], p7, [psY], [sn], partial=True)
            self.cp("act", sT[:, 8, c0:c0 + 128], p5, [psX], [sn], partial=True)
            for h in range(4):
                self.tr(p7[0:96, h, :], qDb[:, h * 96:(h + 1) * 96], ["qDb%d" % p], [psY])
            for h in range(4):
                self.tr(p7[0:96, 4 + h, :], kDb[:, h * 96:(h + 1) * 96], ["kDb%d" % p], [psY])
            self.cp("dve", sT[0:96, 9:17, c0:c0 + 128], p7[0:96], [psY], [sn], partial=True)

        def store_block(bi):
            g0, nt, s = blocks[bi]
            sl = bi % 2
            sT = stgT[sl].rearrange("p (b t) -> p b t", t=512)
            sn = "stgT%d" % sl
            t0 = g0 * 128
            n = nt * 128
            L = "L%d" % l

            def st_(dst, b0, nb, rows, name, q):
                self.ld(q, dst[:, 0:rows, t0:t0 + n].rearrange("b p t -> p b t"), sT[0:rows, b0:b0 + nb, 0:n], [sn], [name])
            st_(self.QAT, 0, 2, 128, "QAT", "sp")
            st_(self.KAT, 2, 2, 128, "KAT", "sp")
            st_(self.QBT, 4, 2, 128, "QBT", "sp")
            st_(self.KBT, 6, 1, 128, "KBT", "sp")
            st_(self.OT[4:6], 7, 2, 128, "OTC", "sp")
            st_(self.QDT, 9, 4, 96, "QDT", "sp")
            st_(self.KDT, 13, 4, 96, "KDT", "sp")
            v3 = vst[sl][:, 0:nt * 1280].rearrange("p (t c) -> p t c", c=1280)
            self.ld("sp", self.VV[t0:t0 + n, :].rearrange("(t p) c -> p t c", p=128), v3, ["vst%d" % sl], ["VV"])

        nti = len(tiles)
        for dflag in self.debug:
            if dflag.startswith("ptiles="):
                nti = int(dflag[7:])

        def weave(lists):
            lists = [list(x) for x in lists]
            out = []
            while any(lists):
                for x in lists:
                    if x:
                        out.append(x.pop(0))
            return out

        def sprinkle(main, extra):
            if not extra:
                return list(main)
            out = []
            n, k = len(main), len(extra)
            j = 0
            for i, o in enumerate(main):
                out.append(o)
                while j < k and (j + 1) * n <= (i + 1) * k:
                    out.append(extra[j])
                    j += 1
            out.extend(extra[j:])
            return out

        for ti in range(min(4, nti)):
            loadx(ti)
        stageA(0)
        if nti > 1:
            stageA(1)
        for t0 in range(0, nti, 2):
            recs = []
            for ti in (t0, t0 + 1):
                if ti < nti:
                    P.begin_record()
                    stageB(ti)
                    recs.append(P.end_record())
            P.begin_record()
            for ti in (t0 + 2, t0 + 3):
                if ti < nti:
                    stageA(ti)
                if ti + 2 < nti:
                    loadx(ti + 2)
            recA = P.end_record()
            P.replay(sprinkle(weave(recs), recA))
            bi, g0, nt, s, tt = tiles[min(t0 + 1, nti - 1)]
            if tt == nt - 1 and "pnostore" not in self.debug:
                store_block(bi)
        P.barrier()
        self.release(m)

    def att_cfg(self, mixer):
        if mixer == "A":
            return dict(QT=self.QAT, KT=self.KAT, nqb=2, nkb=2, nh=4, vh0=0, ob0=0, rows=128, scale=32 ** -0.5)
        if mixer == "B":
            return dict(QT=self.QBT, KT=self.KBT, nqb=2, nkb=1, nh=2, vh0=4, ob0=2, rows=128, scale=64 ** -0.5)
        return dict(QT=self.QDT, KT=self.KDT, nqb=4, nkb=4, nh=4, vh0=6, ob0=6, rows=96, scale=96 ** -0.5)

    def att_prepare(self, mixer):
        c = self.att_cfg(mixer)
        nkb, nh, rows, vh0, KT = c["nkb"], c["nh"], c["rows"], c["vh0"], c["KT"]
        kt = self.alloc(nkb * T, BF16)
        kt3 = kt.rearrange("p (b t) -> p b t", t=T)
        vv = self.alloc(NT * nh * 128, BF16)
        vv4 = vv.rearrange("p (t h c) -> p t h c", h=nh, c=128)
        for b in range(nkb):
            self.ld("sp" if b % 2 == 0 else "act", kt3[0:rows, b, :], KT[b, 0:rows, :], ["KT"], ["kt" + mixer], partial=True)
        for q4 in range(0, NT, 9):
            n = min(9, NT - q4)
            self.ld("act" if (q4 // 9) % 2 == 0 else "sp", vv4[:, q4:q4 + n, :, :],
                    self.VV[q4 * 128:(q4 + n) * 128, vh0 * 128:(vh0 + nh) * 128].rearrange("(t p) (h c) -> p t h c", p=128, c=128),
                    ["VV"], ["vv" + mixer], partial=True)
        return kt3, vv4

    def att_shared(self):
        sh = {}
        sh["qt"] = [self.alloc(4096, BF16) for _ in range(2)]
        sh["pt"] = [self.alloc(512, BF16) for _ in range(4)]
        sh["ostg"] = [self.alloc(2 * 512, BF16) for _ in range(2)]
        sh["rl"] = [self.alloc(512) for _ in range(2)]
        for nm in ("d1", "d2", "dff", "sq", "lnv"):
            sh[nm] = self.alloc(512)
        sh["lamt"] = self.alloc(128)
        sh["lams"] = self.alloc(8)
        sh["gsub"] = self.alloc(2)
        return sh

    def phase_att_all(self, l, do_ctx):
        P = self.P
        m = self.mark()
        self.P.phase = 'attA%d' % l
        preps = {}
        for mixer in ("A", "B", "D"):
            preps[mixer] = self.att_prepare(mixer)
        sh = self.att_shared()
        for mixer in ("A", "B", "D"):
            self.phase_att(l, mixer, do_ctx, preps[mixer], sh)
        P.barrier()
        self.release(m)

    def phase_att(self, l, mixer, do_ctx, prep, sh):
        self.P.phase = 'att%s%d' % (mixer, l)
        P = self.P
        I = self.inp
        lam_init = 0.8 - 0.6 * math.exp(-0.3 * l)
        c = self.att_cfg(mixer)
        QT, KT, nqb, nkb, nh, vh0, ob0, rows, scale = (c["QT"], c["KT"], c["nqb"], c["nkb"], c["nh"], c["vh0"], c["ob0"],
                                                       c["rows"], c["scale"])
        if mixer == "A":
            maps = [(mm_ // 4, mm_ // 4, 32 * (mm_ % 4), 32, mm_ // 2) for mm_ in range(8)]
        elif mixer == "B":
            maps = [(j, 0, 64 * i, 64, i) for j in range(2) for i in range(2)]
            true_head = [0, 2, 1, 3]
        else:
            maps = [(h, h, 0, 96, h) for h in range(4)]
        kt3, vv4 = prep
        ktn = "kt%s" % mixer
        vvn = "vv%s" % mixer
        nmask = {"A": 4, "B": 2}.get(mixer, 1)
        qt = [q[:, 0:nqb * nmask * 512] for q in sh["qt"]]
        if nmask > 1:
            for sl in range(2):
                P.op("pool", lambda e, sl=sl: e.memset(qt[sl], 0.0), writes=["qt%d" % sl])
        NPT = 4
        pt, ostg, rl = sh["pt"], sh["ostg"], sh["rl"]
        d1, d2, dff, sq, lnv, lamt, lams, gsub = (sh["d1"], sh["d2"], sh["dff"], sh["sq"], sh["lnv"], sh["lamt"], sh["lams"],
                                                  sh["gsub"])
        if mixer == "A":
            self.ld("sp", lamt, I["lam_vecs"][l:l + 1].rearrange("o a d -> o (a d)").partition_broadcast(128), [], ["lamt"])
            lt3 = lamt.rearrange("p (a d) -> p a d", d=32)
            self.tt("dve", d1[:, 0:32], lt3[:, 0, :], lt3[:, 1, :], ALU.mult, ["lamt"], ["d1"])
            self.tt("dve", d1[:, 32:64], lt3[:, 2, :], lt3[:, 3, :], ALU.mult, ["lamt"], ["d1"], partial=True)
            P.op("dve", lambda e: e.tensor_reduce(out=lams[:, 0:2], in_=d1[:, 0:64].rearrange("p (a d) -> p a d", d=32),
                                                  op=ALU.add, axis=AX.X), reads=["d1"], writes=["lams"])
            self.act(lams[:, 2:4], lams[:, 0:2], AF.Exp, ["lams"], ["lams"])
            self.tt("dve", lams[:, 4:5], lams[:, 3:4], lams[:, 2:3], ALU.subtract, ["lams"], ["lams"])
            self.ts("dve", lams[:, 5:6], lams[:, 4:5], -lam_init, ALU.add, ["lams"], ["lams"])
            self.ld("sp", gsub[0:64, 0:1], I["g_subln"][l].rearrange("(d o) -> d o", o=1), [], ["gsub"])
            self.ts("dve", gsub[0:64, 1:2], gsub[0:64, 0:1], 1.0 - lam_init, ALU.mult, ["gsub"], ["gsub"])
        neg_lam = lams[0:64, 5:6]
        gcol = gsub[0:64, 1:2]

        chunks = []
        if do_ctx:
            chunks.append((0, 256, 0, 2))
        for i in range(8):
            chunks.append((256 + 512 * i, 512, 0, NT))

        def load_q(ci):
            t0, nq, k0, k1 = chunks[ci]
            sl = ci % 2
            q3 = qt[sl].rearrange("p (b t) -> p b t", t=512)
            if nmask == 1:
                self.ld("sp", q3[0:rows, :, 0:nq], QT[:, 0:rows, t0:t0 + nq].rearrange("b p t -> p b t"), ["QT"], ["qt%d" % sl])
            else:
                rw = 128 // nmask
                k = 0
                for b in range(nqb):
                    for mi in range(nmask):
                        self.ld("sp", q3[mi * rw:(mi + 1) * rw, b * nmask + mi, 0:nq],
                                QT[b, mi * rw:(mi + 1) * rw, t0:t0 + nq], ["QT"], ["qt%d" % sl], partial=True)
                        k += 1

        LA = 2 if mixer == "A" else 3
        SB = 4
        EPD = 24 if mixer == "A" else 8
        pending = []
        state = {"step": 0, "accset": 0, "ep": 0}

        def do_chunk(ci):
            t0, nq, k0, k1 = chunks[ci]
            sl = ci % 2
            q3 = qt[sl].rearrange("p (b t) -> p b t", t=512)
            qn = "qt%d" % sl
            o3 = ostg[sl].rearrange("p (b t) -> p b t", t=512)
            on = "ostg%d" % sl
            steps = []
            for mi, mp in enumerate(maps):
                for k in range(k0, k1):
                    steps.append((mi, k))
            ns = len(steps)
            nk = k1 - k0
            def accbank(mi):
                if mixer == "A":
                    return 4 + 2 * ((mi // 2) % 2) + (mi % 2)
                return 4 + (mi % 4)

            def qk(si):
                mi, k = steps[si]
                qb, kb, r0, K, vh = maps[mi]
                g = state["step"] + si
                bk = g % SB
                if nmask == 1:
                    self.mm(self.bank(bk)[:, 0:nq], kt3[r0:r0 + K, kb, k * 128:(k + 1) * 128], q3[r0:r0 + K, qb, 0:nq],
                            True, True, [ktn, qn], ["psS%d" % bk])
                else:
                    qslot = qb * nmask + r0 // K
                    self.mm(self.bank(bk)[:, 0:nq], kt3[:, kb, k * 128:(k + 1) * 128], q3[:, qslot, 0:nq],
                            True, True, [ktn, qn], ["psS%d" % bk])

            def ex_pv(si):
                mi, k = steps[si]
                qb, kb, r0, K, vh = maps[mi]
                g = state["step"] + si
                bk = g % SB
                pi = g % NPT
                self.act(pt[pi][:, 0:nq], self.bank(bk)[:, 0:nq], AF.Exp, ["psS%d" % bk], ["pt%d" % pi], scale=scale)
                ab = accbank(mi)
                if k == k0:
                    last = -1
                    for pi_, pe_ in enumerate(pending):
                        if pe_[2] == ab:
                            last = pi_
                    for _ in range(last + 1):
                        P.replay(pending.pop(0)[1])
                self.mm(self.bank(ab)[:, 0:nq], vv4[:, k, vh, :], pt[pi][:, 0:nq], k == k0, k == k1 - 1,
                        [vvn, "pt%d" % pi], ["psA%d" % ab])
                gstep = state["step"] + si
                if k == k1 - 1:
                    P.begin_record()
                    epilogue(mi)
                    rec = P.end_record()
                    tail = []
                    if mixer == "A" and mi % 2 == 1:
                        tail = rec[-3:]
                        rec = rec[:-3]
                    P.begin_record()
                    if si == ns - 1:
                        self.ld("sp", self.OT[ob0:ob0 + 2, :, t0:t0 + nq].rearrange("b p t -> p b t"), o3[:, :, 0:nq], [on], ["OTw"])
                    tail = tail + P.end_record()
                    pending.append((gstep + EPD, rec, ab))
                    if tail:
                        pending.append((gstep + EPD + 20, tail, ab))
                while pending and pending[0][0] <= gstep:
                    P.replay(pending.pop(0)[1])

            def epilogue(mi):
                ab = accbank(mi)
                an = "psA%d" % ab
                acc = self.bank(ab)
                ri = state["ep"] % 2
                state["ep"] += 1
                rln = "rl%d" % ri
                P.op("dve", lambda e: e.reciprocal(out=rl[ri][0:64, 0:nq], in_=acc[64:128, 0:nq]), reads=[an], writes=[rln])
                if mixer == "A":
                    h = mi // 2
                    dst = d1 if mi % 2 == 0 else d2
                    dn = "d1" if mi % 2 == 0 else "d2"
                    self.tt("dve", dst[0:64, 0:nq], acc[0:64, 0:nq], rl[ri][0:64, 0:nq], ALU.mult, [an, rln], [dn])
                    if mi % 2 == 1:
                        self.stt(dff[0:64, 0:nq], d2[0:64, 0:nq], neg_lam, d1[0:64, 0:nq], ALU.mult, ALU.add,
                                 ["d1", "d2", "lams"], ["dff"])
                        self.tt("pool", sq[0:64, 0:nq], dff[0:64, 0:nq], dff[0:64, 0:nq], ALU.mult, ["dff"], ["sq"])
                        bk = 0
                        g = state["step"]
                        self.mm(self.bank(3)[0:64, 0:nq], self.ones_f[0:64, 0:64], sq[0:64, 0:nq], True, True,
                                ["sq", "consts"], ["psS3"])
                        self.act(lnv[0:64, 0:nq], self.bank(3)[0:64, 0:nq], AF.Ln, ["psS3", "consts"], ["lnv"],
                                 scale=1.0 / 64, bias=self.epsb[0:64, 0:1])
                        self.act(lnv[0:64, 0:nq], lnv[0:64, 0:nq], AF.Exp, ["lnv"], ["lnv"], scale=-0.5)
                        orow = 64 * (h % 2)
                        self.stt(o3[orow:orow + 64, h // 2, 0:nq], dff[0:64, 0:nq], gcol, lnv[0:64, 0:nq], ALU.mult, ALU.mult,
                                 ["dff", "gsub", "lnv"], [on], partial=True)
                else:
                    h = true_head[mi] if mixer == "B" else mi
                    orow = 64 * (h % 2)
                    self.tt("dve", o3[orow:orow + 64, h // 2, 0:nq], acc[0:64, 0:nq], rl[ri][0:64, 0:nq], ALU.mult,
                            [an, rln], [on], partial=True)

            for si in range(min(LA, ns)):
                qk(si)
            for si in range(ns):
                if si + LA < ns:
                    qk(si + LA)
                ex_pv(si)
            state["step"] += ns

        if mixer == "A":
            SB = 3
        load_q(0)
        for ci in range(len(chunks)):
            if ci + 1 < len(chunks):
                load_q(ci + 1)
            do_chunk(ci)
        while pending:
            P.replay(pending.pop(0)[1])

    def phase_out(self, l, tiles0, ntiles, stream_of):
        self.P.phase = 'out%d' % l
        P = self.P
        I = self.inp
        m = self.mark()
        wo = self.alloc(8 * D, BF16)
        wo3 = wo.rearrange("p (c n) -> p c n", n=D)
        Gp = self.alloc(D)
        xin = [self.alloc(2 * D) for _ in range(2)]
        ot = [self.alloc(8 * 256, BF16) for _ in range(2)]
        tmp = [self.alloc(D) for _ in range(2)]
        junk = self.alloc(D, BF16)
        st = [self.alloc(8) for _ in range(2)]
        stg = [self.alloc(1024) for _ in range(6)]
        self.stg_i = 0
        wv = I["w_out"][l].rearrange("(c p) n -> p c n", p=128)
        for c in range(8):
            self.load_cast(stg, wo3[:, c, :], wv[:, c, :], "wo")
        blocks = [(tiles0 + 2 * i) for i in range(ntiles // 2)]
        cur = [None]

        def load(i):
            g0 = blocks[i]
            sl = i % 2
            x3 = xin[sl].rearrange("p (t d) -> p t d", d=D)
            self.ld("sp", x3, self.XS[g0 * 128:(g0 + 2) * 128, :].rearrange("(t p) d -> p t d", p=128), ["XSr%d" % g0], ["oxin%d" % sl])
            o3 = ot[sl].rearrange("p (b t) -> p b t", t=256)
            self.ld("sp", o3, self.OT[:, :, g0 * 128:(g0 + 2) * 128].rearrange("b p t -> p b t"), ["OTall"], ["ot%d" % sl])

        def compute(i):
            g0 = blocks[i]
            sl = i % 2
            s = stream_of(g0)
            if cur[0] != s:
                cur[0] = s
                self.ld("sp", Gp, self.DV[l][s, 1, 2:3, :].partition_broadcast(128), ["DV%d" % l], ["Gp"])
            o3 = ot[sl].rearrange("p (b t) -> p b t", t=256)
            xn = "oxin%d" % sl
            stn = "ost%d" % sl
            for t in range(2):
                py = self.bank(2 * ((2 * i + t) % 4), n=2)
                pn = "psY%d" % ((2 * i + t) % 4)
                for half in range(2):
                    for b in range(8):
                        self.mm(py[:, half * 512:(half + 1) * 512], o3[:, b, t * 128:(t + 1) * 128],
                                wo3[:, b, half * 512:(half + 1) * 512], b == 0, b == 7, ["ot%d" % sl, "wo"], [pn])
                self.act(junk, py, AF.Square, [pn], [stn, "junk"], partial=True, accum_out=st[sl][:, t:t + 1])
            self.ts("dve", st[sl][:, 2:4], st[sl][:, 0:2], 1.0 / D, ALU.mult, [stn], [stn], s2=EPS, op1=ALU.add)
            self.tt("pool", st[sl][:, 2:4], st[sl][:, 2:4], self.neghalf[:, 0:2], ALU.pow, [stn, "consts"], [stn])
            for t in range(2):
                py = self.bank(2 * ((2 * i + t) % 4), n=2)
                pn = "psY%d" % ((2 * i + t) % 4)
                tn = "otmp%d" % t
                self.stt(tmp[t], py, st[sl][:, 2 + t:3 + t], Gp, ALU.mult, ALU.mult, [pn, stn, "Gp"], [tn])
                self.tt("pool", xin[sl][:, t * D:(t + 1) * D], tmp[t], xin[sl][:, t * D:(t + 1) * D], ALU.add,
                        [tn, xn], [xn], partial=True)
            x3 = xin[sl].rearrange("p (t d) -> p t d", d=D)
            self.ld("sp", self.XS[g0 * 128:(g0 + 2) * 128, :].rearrange("(t p) d -> p t d", p=128), x3, [xn], ["XSw%d" % g0])

        n = len(blocks)
        load(0)
        for i in range(n):
            if i + 1 < n:
                load(i + 1)
            compute(i)
        P.barrier()
        self.release(m)


def build(debug=(), stop_after=None):
    kb = KB(debug)
    nc = kb.nc
    inp = {}

    def din(name, shape, dt=F32):
        inp[name] = nc.dram_tensor(name, list(shape), dt, kind="ExternalInput").ap()

    din("x", [SEQ, D])
    din("ctx", [CTX, D])
    din("cvec", [128, 8, 2])
    din("w_ada", [DEPTH, D, 9 * D])
    din("b_ada", [DEPTH, 9 * D])
    din("g_pre", [DEPTH, 3, D])
    din("g_post", [DEPTH, 3, D])
    din("w_ffn1_in", [DEPTH, D, 2 * DFF])
    din("w_ffn1_out", [DEPTH, DFF, D])
    din("w_ffn2_in", [DEPTH, D, 2 * DFF])
    din("w_ffn2_out", [DEPTH, DFF, D])
    din("w_in_p", [DEPTH, D, INW])
    din("w_out", [DEPTH, D, D])
    din("w_uq", [DEPTH, 256, 384])
    din("w_ukv", [DEPTH, 128, 512])
    din("wspT", [DEPTH, 128, 4, 128])
    din("bspT", [DEPTH, 128, 4])
    din("gqk", [DEPTH, 384])
    din("ln_g", [DEPTH, 256])
    din("ln_b", [DEPTH, 256])
    din("g_q_a", [DEPTH, 256])
    din("g_kv_a", [DEPTH, 128])
    din("lam_vecs", [DEPTH, 4, 32])
    din("g_subln", [DEPTH, 64])
    din("rope", [SEQ, 192])
    kb.inp = inp
    out = nc.dram_tensor("out", [SEQ, D], F32, kind="ExternalOutput").ap()
    kb.XS = kb.dram("XS", [T, D], F32)
    kb.DV = [kb.dram("DV%d" % l, [2, 3, 3, D], F32) for l in range(DEPTH)]
    kb.QAT = kb.dram("QAT", [2, 128, T], BF16)
    kb.KAT = kb.dram("KAT", [2, 128, T], BF16)
    kb.QBT = kb.dram("QBT", [2, 128, T], BF16)
    kb.KBT = kb.dram("KBT", [1, 128, T], BF16)
    kb.QDT = kb.dram("QDT", [4, 128, T], BF16)
    kb.KDT = kb.dram("KDT", [4, 128, T], BF16)
    kb.OT = kb.dram("OT", [8, 128, T], BF16)
    kb.VV = kb.dram("VV", [T, 1280], BF16)

    def done():
        kb.P.emit()
        return kb

    kb.setup_consts()
    for l in range(DEPTH):
        kb.phase_mod(l)
    if stop_after == "mod":
        return done()

    def src0(b):
        return inp["ctx"] if b == 0 else inp["x"][(b - 1) * 256:b * 256, :]

    def xs(b):
        return kb.XS[b * 256:(b + 1) * 256, :]

    def outb(b):
        return out[(b - 1) * 256:b * 256, :]

    allblocks = [(0, 1)] + [(b, 0) for b in range(1, 17)]
    latblocks = [(b, 0) for b in range(1, 17)]
    for l in range(DEPTH):
        need_ctx = (l < DEPTH - 1)
        if "noffn" in kb.debug and l == 0:
            kb.ld("sp", kb.XS[0:CTX, :].rearrange("(p a) d -> p (a d)", p=128), inp["ctx"].rearrange("(p a) d -> p (a d)", p=128), [], ["XScopy"], partial=True)
            for i in range(8):
                kb.ld("sp", kb.XS[CTX + 512 * i:CTX + 512 * (i + 1), :].rearrange("(p a) d -> p (a d)", p=128),
                      inp["x"][512 * i:512 * (i + 1), :].rearrange("(p a) d -> p (a d)", p=128), [], ["XScopy"], partial=True)
            kb.P.barrier()
        else:
            kb.phase_ffn(l, 0, src0 if l == 0 else xs, xs, allblocks)
        if stop_after == "ffn%d" % l:
            return done()
        kb.phase_proj(l, need_ctx)
        if stop_after == "proj%d" % l:
            return done()
        kb.phase_att_all(l, need_ctx)
        if stop_after == "attD%d" % l:
            return done()
        if need_ctx:
            kb.phase_out(l, 0, NT, lambda g: 1 if g < 2 else 0)
        else:
            kb.phase_out(l, 2, NT - 2, lambda g: 0)
        if stop_after == "out%d" % l:
            return done()
        if need_ctx:
            kb.phase_ffn(l, 2, xs, xs, allblocks)
        else:
            kb.phase_ffn(l, 2, xs, outb, latblocks)
        if stop_after == "ffnb%d" % l:
            return done()
    return done()


def _rope_table():
    def tabs(rot_dim):
        rows = np.repeat(np.arange(SEQ // 64, dtype=np.float32), 64)
        cols = np.tile(np.arange(64, dtype=np.float32), SEQ // 64)
        axis_dim = rot_dim // 2
        inv_freq = (np.float32(10000.0) ** (-np.arange(0, axis_dim, 2, dtype=np.float32) / np.float32(axis_dim))).astype(np.float32)
        ar = rows[:, None] * inv_freq[None, :]
        ac = cols[:, None] * inv_freq[None, :]
        cr, sr, cc, sc = np.cos(ar), np.sin(ar), np.cos(ac), np.sin(ac)
        C = np.concatenate([cr, cr, cc, cc], axis=1)
        S = np.concatenate([-sr, sr, -sc, sc], axis=1)
        return C.astype(np.float32), S.astype(np.float32)
    C32, S32 = tabs(32)
    C64, S64 = tabs(64)
    return np.ascontiguousarray(np.concatenate([C32, S32, C64, S64], axis=1).astype(np.float32))


def _w_in_perm():
    o = np.arange(INW)
    qb = 768 + np.concatenate([np.arange(0, 64), np.arange(128, 192), np.arange(64, 128), np.arange(192, 256)])
    return np.concatenate([o[0:256], o[256:512], qb, o[1024:1152], o[1152:1280], o[512:768], o[1280:2208]])


def host_inputs(inputs):
    f = lambda a: np.ascontiguousarray(np.asarray(a, dtype=np.float32))
    shared = {}
    for k in ["w_ada", "b_ada", "g_pre", "g_post", "w_ffn1_in", "w_ffn1_out", "w_ffn2_in", "w_ffn2_out",
              "w_out", "w_uq", "w_ukv", "ln_g", "ln_b", "g_q_a", "g_kv_a", "lam_vecs", "g_subln"]:
        shared[k] = f(inputs[k])
    shared["w_in_p"] = f(np.asarray(inputs["w_in"])[:, :, _w_in_perm()])
    shared["wspT"] = f(np.asarray(inputs["w_spatial"]).transpose(0, 3, 1, 2))
    shared["bspT"] = f(np.asarray(inputs["b_spatial"]).transpose(0, 2, 1))
    gq = np.asarray(inputs["g_qnorm"])
    gk = np.asarray(inputs["g_knorm"])
    shared["gqk"] = f(np.concatenate([gq, gq, gq, gq, gk, gk], axis=1))
    shared["rope"] = _rope_table()
    x = np.asarray(inputs["x"])
    ctx = np.asarray(inputs["ctx"])
    c = np.asarray(inputs["c"])
    cc = np.asarray(inputs["c_ctx"])
    maps = []
    for b in range(x.shape[0]):
        mm = dict(shared)
        mm["x"] = f(x[b])
        mm["ctx"] = f(ctx[b])
        mm["cvec"] = f(np.stack([c[b].reshape(8, 128).T, cc.reshape(8, 128).T], axis=-1))
        maps.append(mm)
    return maps


_CACHE = {}


def kernel(**inputs):
    if "kb" not in _CACHE:
        _CACHE["kb"] = build()
    kb = _CACHE["kb"]
    maps = host_inputs(inputs)
    res = run_bass_kernel_spmd(kb.nc, maps, core_ids=list(range(len(maps))))
    return np.stack([np.asarray(r["out"], dtype=np.float32) for r in res.results], axis=0)
```
